# Optimizing a Trainium2 kernel written in Bass

```python
import jax, jax.numpy as jnp
from jax import lax
import numpy as np

D_MODEL = 1024
BATCH = 4
SEQ = 8192
DEPTH = 1

MIX_WIDTH = D_MODEL
POOL_WIDTH = D_MODEL // 4
POOL_WINDOWS = (2, 4, 8, 16)
POOL_GROUP = POOL_WIDTH // len(POOL_WINDOWS)
HEAD_DIM = 64
ATTN_WIDTH = MIX_WIDTH - POOL_WIDTH
N_HEADS = ATTN_WIDTH // HEAD_DIM
DILATED_CONFIGS = ((128, 1), (512, 4), (2048, 16))
BLOCK = 128
ROPE_THETA = 10000.0
IN_WIDTH = POOL_WIDTH + 3 * ATTN_WIDTH
_FF_RAW = -(-8 * D_MODEL // 3)
D_FF = ((_FF_RAW + 255) // 256) * 256
EPS = 1e-6

kernel_name = "hybrid_pool_dilated_attn_block"


def rms_norm(x, g):
    xf = x.astype(jnp.float32)
    y = xf * lax.rsqrt(jnp.mean(xf * xf, axis=-1, keepdims=True) + EPS)
    return (y * g.astype(jnp.float32)).astype(x.dtype)


def rope(x, pos):
    half = x.shape[-1] // 2
    freqs = ROPE_THETA ** (-jnp.arange(half, dtype=jnp.float32) * (2.0 / x.shape[-1]))
    ang = pos.astype(jnp.float32)[:, None] * freqs[None, :]
    cos = jnp.cos(ang)[None, :, None, :]
    sin = jnp.sin(ang)[None, :, None, :]
    xf = x.astype(jnp.float32)
    x1, x2 = xf[..., :half], xf[..., half:]
    out = jnp.concatenate([x1 * cos - x2 * sin, x2 * cos + x1 * sin], axis=-1)
    return out.astype(x.dtype)


def multi_scale_pool(u, w_pool, pool_scale):
    B, S, _ = u.shape
    ug = u.astype(jnp.float32).reshape(B, S, len(POOL_WINDOWS), POOL_GROUP)
    csum = lax.cumsum(ug, axis=1)
    t = jnp.arange(S)
    outs = []
    for gi, win in enumerate(POOL_WINDOWS):
        cg = csum[:, :, gi]
        shifted = jnp.pad(cg, ((0, 0), (win, 0), (0, 0)))[:, :S]
        cnt = jnp.minimum(t + 1, win).astype(jnp.float32)[None, :, None]
        outs.append((cg - shifted) / cnt - ug[:, :, gi])
    d = jnp.stack(outs, axis=2)
    y = jnp.einsum('bsgc,gcd->bsgd', d, w_pool.astype(jnp.float32))
    y = y.reshape(B, S, POOL_WIDTH) * pool_scale.astype(jnp.float32)
    return y.astype(u.dtype)


def dilated_branch(q, k, v, window, dilation):
    B, S, H, Dh = q.shape
    L = S // dilation
    nb = -(-L // BLOCK)
    Lp = nb * BLOCK
    w_sub = window // dilation

    def to_sub(a):
        a = a.reshape(B, L, dilation, H, Dh).transpose(0, 2, 1, 3, 4)
        return jnp.pad(a, ((0, 0), (0, 0), (0, Lp - L), (0, 0), (0, 0)))

    qs = to_sub(q).reshape(B, dilation, nb, BLOCK, H, Dh)
    kp = jnp.pad(to_sub(k), ((0, 0), (0, 0), (BLOCK, 0), (0, 0), (0, 0)))
    vp = jnp.pad(to_sub(v), ((0, 0), (0, 0), (BLOCK, 0), (0, 0), (0, 0)))

    def band(a):
        prev = a[:, :, :Lp].reshape(B, dilation, nb, BLOCK, H, Dh)
        cur = a[:, :, BLOCK:].reshape(B, dilation, nb, BLOCK, H, Dh)
        return jnp.concatenate([prev, cur], axis=3)

    kb, vb = band(kp), band(vp)
    scale = 1.0 / np.sqrt(Dh).astype(np.float32)
    s = jnp.einsum('brnqhd,brnkhd->brnhqk', qs.astype(jnp.float32), kb.astype(jnp.float32)) * scale

    qi = jnp.arange(BLOCK)[:, None]
    kj = jnp.arange(2 * BLOCK)[None, :]
    dist = qi + BLOCK - kj
    blk = jnp.arange(nb)[:, None, None]
    valid = (dist >= 0) & (dist <= w_sub) & (blk * BLOCK + kj - BLOCK >= 0)
    s = jnp.where(valid[None, None, :, None], s, -jnp.inf)

    m = jnp.max(s, axis=-1, keepdims=True)
    e = jnp.exp(s - m)
    den = jnp.sum(e, axis=-1, keepdims=True)
    lse = (m + jnp.log(den))[..., 0]
    o = jnp.einsum('brnhqk,brnkhd->brnqhd', e / den, vb.astype(jnp.float32))

    o = o.reshape(B, dilation, Lp, H, Dh)[:, :, :L].transpose(0, 2, 1, 3, 4).reshape(B, S, H, Dh)
    lse = lse.transpose(0, 1, 2, 4, 3).reshape(B, dilation, Lp, H)[:, :, :L]
    lse = lse.transpose(0, 2, 1, 3).reshape(B, S, H)
    return o, lse


def dilated_attention(q, k, v):
    outs, lses = [], []
    for window, dilation in DILATED_CONFIGS:
        o, lse = dilated_branch(q, k, v, window, dilation)
        outs.append(o)
        lses.append(lse)
    w = jax.nn.softmax(jnp.stack(lses, axis=0), axis=0)
    o = jnp.sum(w[..., None] * jnp.stack(outs, axis=0), axis=0)
    return o.astype(q.dtype)


def setup_inputs(seed: int = 0) -> dict:
    key = jax.random.key(seed)
    ks = jax.random.split(key, 13)
    f32 = jnp.float32
    nrm = lambda k, shape, s: jax.random.normal(k, shape, f32) * s
    return {
        "x": jax.random.normal(ks[0], (BATCH, SEQ, D_MODEL), f32),
        "ln_pre_mix": 1.0 + nrm(ks[1], (DEPTH, D_MODEL), 0.05),
        "w_in": nrm(ks[2], (DEPTH, D_MODEL, IN_WIDTH), D_MODEL ** -0.5),
        "w_pool": nrm(ks[3], (DEPTH, len(POOL_WINDOWS), POOL_GROUP, POOL_GROUP), POOL_GROUP ** -0.5),
        "pool_scale": 1.0 + nrm(ks[4], (DEPTH, POOL_WIDTH), 0.1),
        "w_out": nrm(ks[5], (DEPTH, MIX_WIDTH, D_MODEL), MIX_WIDTH ** -0.5),
        "ln_post_mix": 1.0 + nrm(ks[6], (DEPTH, D_MODEL), 0.05),
        "ln_pre_ffn": 1.0 + nrm(ks[7], (DEPTH, D_MODEL), 0.05),
        "w_gate": nrm(ks[8], (DEPTH, D_MODEL, D_FF), D_MODEL ** -0.5),
        "w_up": nrm(ks[9], (DEPTH, D_MODEL, D_FF), D_MODEL ** -0.5),
        "w_down": nrm(ks[10], (DEPTH, D_FF, D_MODEL), D_FF ** -0.5),
        "ln_post_ffn": 1.0 + nrm(ks[11], (DEPTH, D_MODEL), 0.05),
    }


def reference(x, ln_pre_mix, w_in, w_pool, pool_scale, w_out, ln_post_mix,
              ln_pre_ffn, w_gate, w_up, w_down, ln_post_ffn):
    B, S, _ = x.shape
    pos = jnp.arange(S)
    for l in range(DEPTH):
        h = rms_norm(x, ln_pre_mix[l])
        proj = h @ w_in[l]
        u_pool = proj[..., :POOL_WIDTH]
        q = proj[..., POOL_WIDTH:POOL_WIDTH + ATTN_WIDTH].reshape(B, S, N_HEADS, HEAD_DIM)
        k = proj[..., POOL_WIDTH + ATTN_WIDTH:POOL_WIDTH + 2 * ATTN_WIDTH].reshape(B, S, N_HEADS, HEAD_DIM)
        v = proj[..., POOL_WIDTH + 2 * ATTN_WIDTH:].reshape(B, S, N_HEADS, HEAD_DIM)
        q, k = rope(q, pos), rope(k, pos)
        pool_out = multi_scale_pool(u_pool, w_pool[l], pool_scale[l])
        attn_out = dilated_attention(q, k, v).reshape(B, S, ATTN_WIDTH)
        mix = jnp.concatenate([pool_out, attn_out], axis=-1) @ w_out[l]
        x = x + rms_norm(mix, ln_post_mix[l])
        h = rms_norm(x, ln_pre_ffn[l])
        f = (jax.nn.silu(h @ w_gate[l]) * (h @ w_up[l])) @ w_down[l]
        x = x + rms_norm(f, ln_post_ffn[l])
    return x
```

```python
import os
from contextlib import ExitStack
import numpy as np
import concourse.bass as bass
import concourse.mybir as mybir
from concourse.bass_utils import run_bass_kernel_spmd

F32 = mybir.dt.float32
BF16 = mybir.dt.bfloat16
AF = mybir.ActivationFunctionType
ALU = mybir.AluOpType

D = 1024
SEQ = 8192
TOK = 4096
HALO = 2048
BLK = 2048
DFF = 2816
NFT = DFF // 128
EPS = 1e-6
NCORES = 8


class Sem:
    def __init__(self, h, name):
        self.h = h
        self.name = name
        self.count = 0


class Obj:
    __slots__ = ("name", "w", "r")

    def __init__(self, name):
        self.name = name
        self.w = {}
        self.r = {}


class Ctx:
    def __init__(self, nc, es):
        self.nc = nc
        self.es = es
        self.eng = {}
        for n, h in (("pe", nc.tensor), ("act", nc.scalar), ("dve", nc.vector), ("pool", nc.gpsimd), ("sp", nc.sync)):
            s = Sem(es.enter_context(nc.semaphore("sem_" + n)), "sem_" + n)
            self.eng[n] = (h, s, {})
        self.dsems = []
        self._ds = {}
        self.nwaits = 0

    def dsem(self, name):
        s = Sem(self.es.enter_context(self.nc.semaphore("dsem_" + name)), "dsem_" + name)
        self.dsems.append(s)
        return s

    def _wait(self, en, sem, val):
        h, _, waited = self.eng[en]
        if waited.get(sem.name, 0) >= val:
            return
        assert val <= sem.count, f"wait on unsignaled {sem.name} {val}>{sem.count} from {en}"
        h.wait_ge(sem.h, val)
        waited[sem.name] = val
        self.nwaits += 1

    def _deps(self, en, reads, writes, is_dma, ds=None):
        _, own, _ = self.eng[en]
        deps = {}

        def need(s, v, raw, waw=False):
            if s is own and not is_dma:
                if en == "pe":
                    return
            if waw and s is ds:
                return
            if deps.get(s.name, (None, 0))[1] < v:
                deps[s.name] = (s, v)

        for o in reads:
            for s, v in o.w.values():
                need(s, v, True)
        for o in writes:
            for s, v in o.w.values():
                need(s, v, False, True)
            for s, v in o.r.values():
                need(s, v, False)
        for s, v in deps.values():
            self._wait(en, s, v)

    def op(self, en, fn, reads=(), writes=(), signal=True):
        h, own, _ = self.eng[en]
        writes = list(writes) + [o for o in reads if o.name.startswith("bank") and o not in writes]
        self._deps(en, reads, writes, False)
        inst = fn(h)
        if signal:
            inst.then_inc(own.h, 1)
            own.count += 1
            val = own.count
        else:
            val = own.count + 1
        for o in reads:
            o.r[own.name] = (own, val)
        for o in writes:
            o.w[own.name] = (own, val)
        return inst

    def ds_of(self, obj):
        if obj.name not in self._ds:
            self._ds[obj.name] = self.dsem(obj.name)
        return self._ds[obj.name]

    def dma(self, en, out, in_, reads, writes, key):
        h, own, _ = self.eng[en]
        ds = key if isinstance(key, Sem) else self.ds_of(key)
        self._deps(en, reads, writes, True, ds)
        inst = h.dma_start(out=out, in_=in_)
        inst.then_inc(ds.h, 16)
        ds.count += 16
        for o in reads:
            o.r[ds.name] = (ds, ds.count)
        for o in writes:
            o.w[ds.name] = (ds, ds.count)
        return inst

    def barrier(self):
        sems = [e[1] for e in self.eng.values()] + self.dsems
        for en in self.eng:
            for s in sems:
                if s is self.eng[en][1]:
                    continue
                if s.count > 0:
                    self._wait(en, s, s.count)


def build_program(dbg=None, dbg_blk=0):
    nc = bass.Bass("TRN2", target_bir_lowering=False)
    dt_in = lambda n, s: nc.dram_tensor(n, list(s), F32, kind="ExternalInput").ap()
    x_d = dt_in("x", (HALO + TOK, D))
    cs_d = dt_in("cs", (128, HALO + TOK))
    sn_d = dt_in("sn", (128, HALO + TOK))
    win_d = dt_in("w_in", (D, 2560))
    wout_d = dt_in("w_out", (D, D))
    wg_d = dt_in("w_gate", (D, DFF))
    wu_d = dt_in("w_up", (D, DFF))
    wd_d = dt_in("w_down", (DFF, D))
    wpbd_d = dt_in("wpool_bd", (2, 128, 128))
    gb_d = dt_in("gb", (4, 128, D))
    psc_d = dt_in("pscale", (128, 2))
    perm_d = dt_in("perm", (128, 128))
    mask_d = dt_in("masks", (3, 128, 512))
    invw_d = dt_in("invwin", (128, 2))
    invc_d = dt_in("invcnt", (128, 2, 16))
    out_d = nc.dram_tensor("out", [TOK, D], F32, kind="ExternalOutput").ap()
    dbg_d = {}
    if dbg:
        for n, (s, dty) in dbg.items():
            dbg_d[n] = nc.dram_tensor("dbg_" + n, list(s), dty, kind="ExternalOutput").ap()

    with ExitStack() as es:
        C = Ctx(nc, es)
        op, dma = C.op, C.dma
        sb = lambda n, s, d: es.enter_context(nc.sbuf_tensor(n, list(s), d))
        ps = es.enter_context(nc.psum_tensor("ps", [128, 4096], F32))
        bank = [ps[:, 512 * i:512 * (i + 1)] for i in range(8)]
        bankbf = [ps[:, 512 * i:512 * (i + 1)].bitcast(BF16) for i in range(8)]
        bobj = [Obj(f"bank{i}") for i in range(8)]
        bank2 = [ps[:, 1024 * i:1024 * (i + 1)] for i in range(4)]

        ds_const = C.dsem("const")
        o_const = Obj("const")
        ident = sb("ident", (128, 128), BF16)
        mhalf = sb("mhalf", (128, 1), F32)
        ssb = sb("ssb", (128, 160), F32)
        tsb = sb("tsb", (128, 160), F32)
        rsb = sb("rsb", (128, 160), F32)
        o_ss = Obj("ss")
        p1 = ExitStack()
        sbc = lambda n, s, d: p1.enter_context(nc.sbuf_tensor(n, list(s), d))
        ident_f = sbc("ident_f", (128, 128), F32)
        perm = sbc("perm_sb", (128, 128), BF16)
        masks = sbc("masks_sb", (128, 3, 512), BF16)
        wpbd = sbc("wpbd_sb", (128, 2, 128), BF16)
        psc = sbc("psc_sb", (128, 2), F32)
        invw = sbc("invw_sb", (128, 2), F32)
        invc = sbc("invc_sb", (128, 2, 16), F32)
        dma("pool", perm[:], perm_d, [], [o_const], ds_const)
        dma("pool", masks[:], mask_d.rearrange("v p c -> p v c"), [], [o_const], ds_const)
        dma("pool", wpbd[:], wpbd_d.rearrange("t p c -> p t c"), [], [o_const], ds_const)
        ds_const2 = C.dsem("const2")
        dma("sp", psc[:], psc_d, [], [o_const], ds_const2)
        dma("sp", invw[:], invw_d, [], [o_const], ds_const2)
        dma("sp", invc[:], invc_d, [], [o_const], ds_const2)
        o_id = Obj("ident")
        op("pool", lambda e: e.memset(ident_f[:], 1.0), [], [o_id])
        op("pool", lambda e: e.affine_select(out=ident_f[:], in_=ident_f[:], pattern=[[-1, 128]],
                                             compare_op=ALU.is_equal, fill=0.0, base=0, channel_multiplier=1),
           [o_id], [o_id])
        op("dve", lambda e: e.tensor_copy(out=ident[:], in_=ident_f[:]), [o_id], [o_id])
        op("dve", lambda e: e.memset(mhalf[:], -0.5), [], [o_const])
        op("dve", lambda e: e.memset(ssb[:], 0.0), [], [o_ss])

        sscol = [0]

        def rstd_from(psrc_ap_fn, junk_ap, src_objs, junk_obj, n_feat=D):
            c = sscol[0]
            sscol[0] += 1
            assert c < 160
            oc = Obj(f"rs{sscol[0]}")
            op("act", lambda e: e.activation(out=junk_ap, in_=psrc_ap_fn(), func=AF.Square,
                                             accum_out=ssb[:, c:c + 1]),
               list(src_objs) + [o_ss], [junk_obj, oc])
            op("dve", lambda e: e.tensor_scalar(out=tsb[:, c:c + 1], in0=ssb[:, c:c + 1], scalar1=1.0 / n_feat,
                                                scalar2=EPS, op0=ALU.mult, op1=ALU.add), [oc], [oc])
            op("pool", lambda e: e.tensor_tensor(out=rsb[:, c:c + 1], in0=tsb[:, c:c + 1], in1=mhalf[:],
                                                 op=ALU.pow), [oc, o_const], [oc])
            return rsb[:, c:c + 1], oc

        def rstd_multi(items, n_feat=D):
            n = len(items)
            c0 = sscol[0]
            sscol[0] += n
            assert sscol[0] <= 160
            oc = Obj(f"rsm{c0}")
            for j, (src, junk_ap, src_objs, junk_obj) in enumerate(items):
                op("act", lambda e, src=src, junk_ap=junk_ap, j=j: e.activation(
                    out=junk_ap, in_=src, func=AF.Square, accum_out=ssb[:, c0 + j:c0 + j + 1]),
                   list(src_objs) + [o_ss], [junk_obj, oc])
            op("dve", lambda e: e.tensor_scalar(out=tsb[:, c0:c0 + n], in0=ssb[:, c0:c0 + n], scalar1=1.0 / n_feat,
                                                scalar2=EPS, op0=ALU.mult, op1=ALU.add), [oc], [oc])
            op("pool", lambda e: e.tensor_tensor(out=rsb[:, c0:c0 + n], in0=tsb[:, c0:c0 + n],
                                                 in1=mhalf[:].to_broadcast([128, n]), op=ALU.pow),
               [oc, o_const], [oc])
            return [rsb[:, c0 + j:c0 + j + 1] for j in range(n)], oc

        with p1:
            sb1 = lambda n, s, d: p1.enter_context(nc.sbuf_tensor(n, list(s), d))
            xT = sb1("xT", (128, 2, 8, BLK), BF16)
            OT = sb1("OT", (128, 8, BLK), BF16)
            cst = sb1("cst", (128, 2 * BLK), F32)
            snt = sb1("snt", (128, 2 * BLK), F32)
            kT = sb1("kT", (128, 2, BLK), BF16)
            vT = sb1("vT", (128, 2, BLK), BF16)
            qT = sb1("qT", (128, BLK), BF16)
            wg_in = sb1("wg_in", (128, 8, 384), BF16)
            scr = sb1("scr", (128, 10240), F32)
            rbf = sb1("rbf", (128, 2, 512), BF16)
            PT = sb1("PT", (128, 2, 2, 512), BF16)
            acc = scr[:, 0:4096].rearrange("p (h t) -> p h t", h=2)
            Vaug = scr[:, 4096:8192].bitcast(BF16).rearrange("p (s c) -> p s c", s=32)
            Rn = scr[:, 9216:10240]
            t1b = [scr[:, 8192:8704], scr[:, 9216:9728]]
            t2b = [scr[:, 8704:9216], scr[:, 9728:10240]]
            NXS = 4
            xs = [scr[:, 1024 * i:1024 * (i + 1)] for i in range(NXS)]
            NXN = 4
            xn = [scr[:, 4096 + 512 * i:4096 + 512 * (i + 1)].bitcast(BF16) for i in range(NXN)]
            gt = qT[:].bitcast(F32)
            gt2 = scr[:, 6144:7168]
            xs2 = [scr[:, 7168 + 1024 * i:7168 + 1024 * (i + 1)] for i in range(2)]
            ys = [scr[:, 9216:10240]] * 2
            ub = scr[:, 6144:7200].rearrange("p (t c) -> p t c", t=2)
            lvb = [scr[:, 7200 + 1056 * i:7200 + 1056 * (i + 1)].rearrange("p (t c) -> p t c", t=2) for i in range(2)]
            dTb = scr[:, 9312:9824].bitcast(BF16).rearrange("p (t c) -> p t c", t=2)
            ptmp = scr[:, 9824:9888].rearrange("p (t c) -> p t c", t=2)
            wpl_in = PT[:].rearrange("p a b c -> p (a b c)").rearrange("p (k c) -> p k c", k=8)

            o_scr = Obj("scr")
            o_xT = [[Obj(f"xT{s}_{c}") for c in range(4)] for s in range(2)]
            o_OT = [Obj(f"OT{k}") for k in range(8)]
            o_tab = Obj("tab")
            o_k, o_v, o_q = Obj("kT"), Obj("vT"), Obj("qT")
            o_wg, o_wpl = Obj("wg_in"), Obj("wpl_in")
            o_acc, o_V, o_R = Obj("acc"), Obj("Vaug"), Obj("Rn")
            o_t1 = [Obj("t1k"), Obj("t1q")]
            o_t2 = [Obj("t2k"), Obj("t2q")]
            o_rbf = [Obj("rbf0"), Obj("rbf1")]
            o_PT = [[Obj(f"PT{s}{h}") for h in range(2)] for s in range(2)]
            ds_tab = C.dsem("tab")
            o_out = [Obj(f"out{t}") for t in range(TOK // 128)]

            def stage_fence(reads_scr=False):
                pass

            o_g = Obj("gt")
            o_xs_l = [Obj(f"xs{i}") for i in range(NXS)]
            o_xn_l = [Obj(f"xn{i}") for i in range(NXN)]

            def stage_x(b, prefetched=False, only_prefetch=False, after_group=None):
                jobs = [(0, 0), (BLK, 1)] if b == 0 else [(2 * BLK, 0)]
                tiles = [(e0 + tt * 128, slot, tt) for e0, slot in jobs for tt in range(BLK // 128)]
                GS = 2

                def loads(g0):
                    for j, (r0, slot, tt) in enumerate(tiles[g0:g0 + GS]):
                        i = (g0 + j) % NXS
                        dma("sp", xs[i], x_d[r0:r0 + 128, :], [], [o_xs_l[i]], o_xs_l[i])

                if not prefetched:
                    dma("sp", gt, gb_d[0], [], [o_g, o_q], o_g)
                    loads(0)
                    loads(GS)
                if only_prefetch:
                    return
                for g0 in range(0, len(tiles), GS):
                    if g0 > 0 and g0 + GS < len(tiles):
                        loads(g0 + GS)
                    grp = tiles[g0:g0 + GS]
                    items = [(xs[(g0 + j) % NXS], bank2[1], [o_xs_l[(g0 + j) % NXS], bobj[2]], bobj[3])
                             for j in range(len(grp))]
                    rss, o_rs = rstd_multi(items)
                    for j, (r0, slot, tt) in enumerate(grp):
                        it = g0 + j
                        xb, o_xs = xs[it % NXS], o_xs_l[it % NXS]
                        nb, o_xn = xn[it % NXN], o_xn_l[it % NXN]
                        bk = 4 + (it % 4)
                        op("dve", lambda e, xb=xb, nb=nb, rs=rss[j]: e.scalar_tensor_tensor(
                            out=nb, in0=xb, scalar=rs, in1=gt, op0=ALU.mult, op1=ALU.mult),
                           [o_xs, o_rs, o_g], [o_xn])
                        for kt in range(8):
                            op("pe", lambda e, nb=nb, kt=kt, bk=bk: e.transpose(
                                bankbf[bk][:, kt * 128:(kt + 1) * 128], nb[:, kt * 128:(kt + 1) * 128], ident[:]),
                               [o_xn, o_id], [bobj[bk]], signal=(kt == 7))
                        ch = tt // 4
                        if it % 2 == 0:
                            op("act", lambda e, bk=bk, slot=slot, tt=tt: e.activation(
                                out=xT[:, slot, :, tt * 128:(tt + 1) * 128],
                                in_=bankbf[bk].rearrange("p (k t) -> p k t", k=8), func=AF.Copy),
                               [bobj[bk]], [o_xT[slot][ch]])
                        else:
                            op("dve", lambda e, bk=bk, slot=slot, tt=tt: e.tensor_copy(
                                out=xT[:, slot, :, tt * 128:(tt + 1) * 128],
                                in_=bankbf[bk].rearrange("p (k t) -> p k t", k=8)),
                               [bobj[bk]], [o_xT[slot][ch]])
                    if after_group is not None:
                        after_group(g0 + GS, tiles)

            def load_wg(g):
                for j, c0 in enumerate((256 + 128 * g, 1024 + 128 * g, 1792 + 128 * g)):
                    dma("pool", wg_in[:, :, 128 * j:128 * (j + 1)],
                        win_d.rearrange("(k p) c -> p k c", p=128)[:, :, c0:c0 + 128], [], [o_wg], o_wg)

            def stage_p(b, g):
                pred_slot = b % 2
                cur_slot = 1 - pred_slot
                chunks = [(half, ch) for half in range(2) for ch in range(4)]

                def pmm(wcol0, slot, ch, bk):
                    for kt in range(8):
                        op("pe", lambda e, kt=kt: e.matmul(
                            bank[bk], lhsT=wg_in[:, kt, wcol0:wcol0 + 128],
                            rhs=xT[:, slot, kt, ch * 512:(ch + 1) * 512], start=(kt == 0), stop=(kt == 7)),
                           [o_wg, o_xT[slot][ch]], [bobj[bk]], signal=(kt == 7))

                def proj_mm(ci):
                    half, ch = chunks[ci]
                    slot = pred_slot if half == 0 else cur_slot
                    base = 3 * (ci % 2)
                    pmm(128, slot, ch, base)
                    pmm(256, slot, ch, base + 1)
                    if half == 1:
                        pmm(0, slot, ch, base + 2)

                def copies(ci):
                    half, ch = chunks[ci]
                    base = 3 * (ci % 2)
                    op("act", lambda e: e.activation(out=rbf[:, 0, :], in_=bank[base], func=AF.Copy),
                       [bobj[base]], [o_rbf[0]])
                    if half == 1:
                        op("act", lambda e: e.activation(out=rbf[:, 1, :], in_=bank[base + 2], func=AF.Copy),
                           [bobj[base + 2]], [o_rbf[1]])
                    op("act", lambda e: e.activation(out=vT[:, half, ch * 512:(ch + 1) * 512], in_=bank[base + 1],
                                                     func=AF.Copy), [bobj[base + 1]], [o_v])

                def rope_rest(ci):
                    half, ch = chunks[ci]
                    base = 3 * (ci % 2)
                    tab0 = half * BLK + ch * 512
                    jobs = [(0, base, 6, kT[:, half, ch * 512:(ch + 1) * 512], o_k)]
                    if half == 1:
                        jobs.append((1, base + 2, 7, qT[:, ch * 512:(ch + 1) * 512], o_q))
                    for (i, bk, wb, dst, dobj) in jobs:
                        op("pe", lambda e, i=i, wb=wb: e.matmul(bank[wb], lhsT=perm[:], rhs=rbf[:, i, :],
                                                                 start=True, stop=True),
                           [o_rbf[i], o_const], [bobj[wb]])
                    for (i, bk, wb, dst, dobj) in jobs:
                        op("dve", lambda e, i=i, bk=bk: e.tensor_tensor(
                            out=t1b[i], in0=bank[bk], in1=cst[:, tab0:tab0 + 512], op=ALU.mult),
                           [bobj[bk], o_tab], [o_t1[i]])
                        op("dve", lambda e, i=i, wb=wb: e.tensor_tensor(
                            out=t2b[i], in0=bank[wb], in1=snt[:, tab0:tab0 + 512], op=ALU.mult),
                           [bobj[wb], o_tab], [o_t2[i]])
                        op("pool", lambda e, i=i, dst=dst: e.tensor_tensor(out=dst, in0=t1b[i], in1=t2b[i],
                                                                            op=ALU.add),
                           [o_t1[i], o_t2[i]], [dobj])

                proj_mm(0)
                copies(0)
                for ci in range(len(chunks)):
                    if ci + 1 < len(chunks):
                        proj_mm(ci + 1)
                    rope_rest(ci)
                    if ci + 1 < len(chunks):
                        copies(ci + 1)

            actr = [0]

            def stage_a(b, g):
                first_cfg = True
                for d in (1, 4, 16):
                    nbh = 16 // d
                    span = 128 * d

                    def toks(r, n):
                        if n < 0:
                            return 0, (nbh - 1) * span + r
                        return 1, n * span + r

                    def sl(t0):
                        return slice(t0, t0 + 127 * d + 1, d)

                    klist = [(r, n) for r in range(d) for n in range(-1, nbh)]
                    vslot = {kn: i for i, kn in enumerate(klist)}
                    for j0 in range(0, len(klist), 8):
                        batch = klist[j0:j0 + 8]
                        bk = 6 + (actr[0] % 2)
                        actr[0] += 1
                        for j, (r, n) in enumerate(batch):
                            half, t0 = toks(r, n)
                            op("pe", lambda e, j=j, half=half, t0=t0, bk=bk: e.transpose(
                                bankbf[bk][:, j * 128:(j + 1) * 128], vT[:, half, sl(t0)], ident[:]),
                               [o_v, o_id], [bobj[bk]], signal=(j == len(batch) - 1))
                        nbt = len(batch)
                        dst = Vaug[:, j0:j0 + nbt, :].rearrange("p s (a c) -> p s a c", a=4)[:, :, 0:4:3, :]
                        src = bankbf[bk][:, 0:nbt * 128].rearrange("p (s h c) -> p s h c", s=nbt, h=2)
                        op("act", lambda e, dst=dst, src=src: e.activation(out=dst, in_=src, func=AF.Copy),
                           [bobj[bk]], [o_V])

                    qbs = [(r, n) for r in range(d) for n in range(nbh)]
                    pairs = [qbs[p0:p0 + 2] for p0 in range(0, len(qbs), 2)]
                    base_ctr = actr[0]
                    actr[0] += len(pairs)

                    def emit_qk(pi):
                        pair = pairs[pi]
                        s_ = (base_ctr + pi) % 2
                        halo = [b == 0 and n == 0 for (_, n) in pair]
                        mv = 0 if not any(halo) else (2 if all(halo) else 1)
                        assert not (halo[1] and not halo[0])
                        for h in range(2):
                            bk = 2 * s_ + h
                            for i, (r, n) in enumerate(pair):
                                _, tq = toks(r, n)
                                for kk, nk in enumerate((n - 1, n)):
                                    khalf, tk = toks(r, nk)
                                    col = (2 * i + kk) * 128
                                    last = (i == 1 and kk == 1)
                                    op("pe", lambda e, h=h, bk=bk, col=col, khalf=khalf, tk=tk, tq=tq: e.matmul(
                                        bank[bk][:, col:col + 128],
                                        lhsT=kT[64 * h:64 * h + 64, khalf, sl(tk)],
                                        rhs=qT[64 * h:64 * h + 64, sl(tq)], start=True, stop=True),
                                       [o_k, o_q], [bobj[bk]], signal=last)
                        for h in range(2):
                            bk = 2 * s_ + h
                            op("act", lambda e, h=h, bk=bk: e.activation(
                                out=PT[:, s_, h, :], in_=bank[bk], func=AF.Exp, scale=0.125),
                               [bobj[bk]], [o_PT[s_][h]])
                            op("dve", lambda e, h=h: e.tensor_tensor(
                                out=PT[:, s_, h, :], in0=PT[:, s_, h, :], in1=masks[:, mv, :], op=ALU.mult),
                               [o_PT[s_][h], o_const], [o_PT[s_][h]])

                    def emit_pv(pi, first):
                        pair = pairs[pi]
                        s_ = (base_ctr + pi) % 2
                        ob = 4 + s_
                        for i, (r, n) in enumerate(pair):
                            for h in range(2):
                                col = (2 * i + h) * 128
                                for kk, nk in enumerate((n - 1, n)):
                                    vs = vslot[(r, nk)]
                                    last = (i == 1 and h == 1 and kk == 1)
                                    op("pe", lambda e, vs=vs, h=h, col=col, kk=kk, i=i: e.matmul(
                                        bank[ob][:, col:col + 128],
                                        lhsT=Vaug[:, vs, 128 * h:128 * (h + 1)],
                                        rhs=PT[:, s_, h, (2 * i + kk) * 128:(2 * i + kk + 1) * 128],
                                        start=(kk == 0), stop=(kk == 1)),
                                       [o_V, o_PT[s_][h]], [bobj[ob]], signal=last)
                        for i, (r, n) in enumerate(pair):
                            _, tq = toks(r, n)
                            dst = acc[:, :, sl(tq)]
                            src = bank[ob][:, 256 * i:256 * (i + 1)].rearrange("p (h l) -> p h l", h=2)
                            if first:
                                op("act", lambda e, dst=dst, src=src: e.activation(out=dst, in_=src, func=AF.Copy),
                                   [bobj[ob]], [o_acc])
                            else:
                                op("dve", lambda e, dst=dst, src=src: e.tensor_tensor(
                                    out=dst, in0=src, in1=dst, op=ALU.add),
                                   [bobj[ob], o_acc], [o_acc])

                    emit_qk(0)
                    for pi in range(len(pairs)):
                        if pi + 1 < len(pairs):
                            emit_qk(pi + 1)
                        emit_pv(pi, first_cfg)
                    first_cfg = False
                rn_objs = [o_R, o_t1[1], o_t2[1]]
                kt = 2 + g
                for hv in range(2):
                    tsl = slice(hv * 1024, (hv + 1) * 1024)
                    op("act", lambda e: e.activation(out=Rn[0:64, :], in_=acc[64:128, 0, tsl], func=AF.Ln),
                       [o_acc], rn_objs)
                    op("act", lambda e: e.activation(out=Rn[64:128, :], in_=acc[0:64, 1, tsl], func=AF.Ln),
                       [o_acc], rn_objs)
                    op("act", lambda e: e.activation(out=Rn, in_=Rn, func=AF.Exp, scale=-1.0), rn_objs, rn_objs)
                    op("pool", lambda e: e.tensor_tensor(out=OT[0:64, kt, tsl], in0=acc[0:64, 0, tsl], in1=Rn[0:64, :],
                                                         op=ALU.mult), [o_acc] + rn_objs, [o_OT[kt]])
                    op("dve", lambda e: e.tensor_tensor(out=OT[64:128, kt, tsl], in0=acc[64:128, 1, tsl],
                                                        in1=Rn[64:128, :], op=ALU.mult), [o_acc] + rn_objs, [o_OT[kt]])

            o_ub, o_dT = Obj("ub"), Obj("dT")
            o_lvb = [Obj("lv0"), Obj("lv1")]

            def pool_begin(b):
                pred_slot = b % 2
                dma("pool", wpl_in, win_d.rearrange("(k p) c -> p k c", p=128)[:, :, 0:256],
                    [], [o_wpl, o_PT[0][0], o_PT[0][1], o_PT[1][0], o_PT[1][1]], o_wpl)
                for t in range(2):
                    bk = t
                    for kt in range(8):
                        op("pe", lambda e, kt=kt, bk=bk, t=t: e.matmul(
                            bank[bk][:, 0:16], lhsT=wpl_in[:, kt, 128 * t:128 * (t + 1)],
                            rhs=xT[:, pred_slot, kt, BLK - 16:BLK], start=(kt == 0), stop=(kt == 7)),
                           [o_wpl, o_xT[pred_slot][3]], [bobj[bk]], signal=(kt == 7))
                    op("act", lambda e, bk=bk, t=t: e.activation(out=ub[:, t, 0:16], in_=bank[bk][:, 0:16],
                                                                 func=AF.Copy), [bobj[bk]], [o_ub])

            def pool_chunk(b, ch):
                cur_slot = 1 - (b % 2)
                for t in range(2):
                    bk = t
                    for kt in range(8):
                        op("pe", lambda e, kt=kt, bk=bk, t=t: e.matmul(
                            bank[bk], lhsT=wpl_in[:, kt, 128 * t:128 * (t + 1)],
                            rhs=xT[:, cur_slot, kt, ch * 512:(ch + 1) * 512], start=(kt == 0), stop=(kt == 7)),
                           [o_wpl, o_xT[cur_slot][ch]], [bobj[bk]], signal=(kt == 7))
                    op("act", lambda e, bk=bk, t=t: e.activation(out=ub[:, t, 16:528], in_=bank[bk],
                                                                 func=AF.Copy), [bobj[bk]], [o_ub])
                prev, o_prev = ub, o_ub
                for k in range(4):
                    sh = 1 << k
                    lo = (1 << (k + 1)) - 1
                    cur, o_cur = lvb[k % 2], o_lvb[k % 2]
                    op("dve", lambda e, prev=prev, cur=cur, sh=sh, lo=lo: e.tensor_tensor(
                        out=cur[:, :, lo:528], in0=prev[:, :, lo:528], in1=prev[:, :, lo - sh:528 - sh],
                        op=ALU.add), [o_prev], [o_cur])
                    t, hh = k // 2, k % 2
                    p0 = 64 * hh
                    op("dve", lambda e, t=t, cur=cur, p0=p0: e.scalar_tensor_tensor(
                        out=dTb[p0:p0 + 64, t, :], in0=cur[p0:p0 + 64, t, 16:528],
                        scalar=invw[p0:p0 + 64, t:t + 1], in1=ub[p0:p0 + 64, t, 16:528],
                        op0=ALU.mult, op1=ALU.subtract), [o_cur, o_ub, o_const], [o_dT])
                    if b == 0 and ch == 0:
                        op("dve", lambda e, t=t, cur=cur, p0=p0: e.tensor_tensor(
                            out=ptmp[p0:p0 + 64, t, 0:16], in0=cur[p0:p0 + 64, t, 16:32],
                            in1=invc[p0:p0 + 64, t, :], op=ALU.mult), [o_cur, o_const], [o_dT])
                        op("dve", lambda e, t=t, p0=p0: e.tensor_tensor(
                            out=dTb[p0:p0 + 64, t, 0:16], in0=ptmp[p0:p0 + 64, t, 0:16],
                            in1=ub[p0:p0 + 64, t, 16:32], op=ALU.subtract), [o_dT, o_ub], [o_dT])
                    prev, o_prev = cur, o_cur
                for t in range(2):
                    bk = t
                    op("pe", lambda e, t=t, bk=bk: e.matmul(bank[bk], lhsT=wpbd[:, t, :], rhs=dTb[:, t, :],
                                                           start=True, stop=True),
                       [o_dT, o_const], [bobj[bk]])
                    op("act", lambda e, t=t, bk=bk: e.activation(
                        out=OT[:, t, ch * 512:(ch + 1) * 512], in_=bank[bk], func=AF.Copy,
                        scale=psc[:, t:t + 1]), [bobj[bk], o_const], [o_OT[t]])
                op("dve", lambda e: e.tensor_copy(out=ptmp[:, :, 16:32], in_=ub[:, :, 512:528]), [o_ub], [o_dT])
                op("act", lambda e: e.activation(out=ub[:, :, 0:16], in_=ptmp[:, :, 16:32], func=AF.Copy),
                   [o_dT], [o_ub])

            o_wo, o_g2, o_yb = Obj("wout"), Obj("gt2"), Obj("ys")

            def wout_view(b):
                return xT[:, b % 2, 0:4, :].rearrange("p k (a c) -> p (k a) c", a=2)

            def gt2_view(b):
                return xT[:, b % 2, 4, :].bitcast(F32)

            def prefetch_o(b):
                dma("pool", wout_view(b), wout_d.rearrange("(k p) c -> p k c", p=128),
                    [], [o_wo] + o_xT[b % 2], o_wo)
                dma("sp", gt2_view(b), gb_d[1], [], [o_g2] + o_xT[b % 2], o_g2)

            def stage_o(b):
                wout = wout_view(b)
                gt2 = gt2_view(b)
                o_xs2_l = [Obj("xs2_0"), Obj("xs2_1")]
                junk = bank2[3]
                o_j = bobj[7]
                for tt in range(BLK // 128):
                    st = tt % 3
                    xb = xs2[tt % 2]
                    yb = ys[0]
                    o_xb = o_xs2_l[tt % 2]
                    tok0 = b * BLK + tt * 128
                    dma("sp", xb, x_d[HALO + tok0:HALO + tok0 + 128, :], [], [o_xb], o_xb)
                    for hh in range(2):
                        bk = 2 * st + hh
                        for kt in range(8):
                            op("pe", lambda e, kt=kt, bk=bk, hh=hh, tt=tt: e.matmul(
                                bank[bk], lhsT=OT[:, kt, tt * 128:(tt + 1) * 128],
                                rhs=wout[:, kt, hh * 512:(hh + 1) * 512], start=(kt == 0), stop=(kt == 7)),
                               [o_OT[kt], o_wo], [bobj[bk]], signal=(kt == 7))
                    src = bank2[st]
                    rs, o_rs = rstd_from(lambda src=src: src, junk, [bobj[2 * st], bobj[2 * st + 1], bobj[6]], o_j)
                    op("dve", lambda e, yb=yb, src=src, rs=rs: e.scalar_tensor_tensor(
                        out=yb, in0=src, scalar=rs, in1=gt2, op0=ALU.mult, op1=ALU.mult),
                       [bobj[2 * st], bobj[2 * st + 1], o_rs, o_g2], [o_yb])
                    op("dve", lambda e, yb=yb, xb=xb: e.tensor_tensor(out=xb, in0=yb, in1=xb, op=ALU.add),
                       [o_yb, o_xb], [o_xb])
                    ti = tok0 // 128
                    dma("sp", out_d[tok0:tok0 + 128, :], xb, [o_xb], [o_out[ti]], o_xb)

            def fence():
                pass

            for b in range(2):
                dma("sp", cst[:], cs_d[:, b * BLK:b * BLK + 2 * BLK], [], [o_tab], ds_tab)
                dma("sp", snt[:], sn_d[:, b * BLK:b * BLK + 2 * BLK], [], [o_tab], ds_tab)
                C.barrier()
                load_wg(0)

                def hook(ndone, tiles, b=b):
                    npred = len(tiles) - BLK // 128
                    if ndone == npred or (npred == 0 and ndone == 2):
                        pass
                    if ndone == max(npred, 2) and not hook.started:
                        pool_begin(b)
                        hook.started = True
                    k = ndone - npred
                    if hook.started and k > 0 and k % 4 == 0:
                        while hook.next_ch < k // 4:
                            pool_chunk(b, hook.next_ch)
                            hook.next_ch += 1
                hook.started = False
                hook.next_ch = 0
                stage_x(b, prefetched=(b == 1), after_group=hook)
                C.barrier()
                op("pool", lambda e: e.memset(Vaug[:, :, 64:192], 1.0), [], [o_V])
                for g in range(6):
                    stage_p(b, g)
                    if g + 1 < 6:
                        load_wg(g + 1)
                    else:
                        prefetch_o(b)
                    stage_a(b, g)
                C.barrier()
                if dbg and b == dbg_blk:
                    ds_dbg = C.dsem("dbg")
                    od = Obj("dbg")
                    srcs = {"OT": OT[:].rearrange("p k t -> p (k t)"), "rsb": rsb[:], "kT": kT[:].rearrange("p h t -> p (h t)"),
                            "vT": vT[:].rearrange("p h t -> p (h t)"), "qT": qT[:], "acc": scr[:, 0:4096],
                            "xT": xT[:].rearrange("p s k t -> p (s k t)")}
                    for n in dbg_d:
                        dma("sp", dbg_d[n], srcs[n], [], [od], ds_dbg)
                    C.barrier()
                if b == 0:
                    stage_x(1, only_prefetch=True)
                stage_o(b)
                C.barrier()
            if dbg and "x1" in dbg:
                pass
        C.barrier()
        with ExitStack() as p2:
            sb2 = lambda n, s, d: p2.enter_context(nc.sbuf_tensor(n, list(s), d))
            wg = sb2("wg", (128, 8, DFF), BF16)
            wu = sb2("wu", (128, 8, DFF), BF16)
            wd = sb2("wd", (128, NFT, D), BF16)
            x1b = sb2("x1b", (128, 4, D), F32)
            yb2 = sb2("yb2", (128, D), F32)
            g3 = sb2("g3", (128, D), F32)
            g4 = sb2("g4", (128, D), F32)
            sg = sb2("sg", (128, 2, 512), F32)
            h2 = sb2("h2", (128, 2, D), BF16)
            h2T = sb2("h2T", (128, 8, 512), BF16)
            actT = sb2("actT", (128, NFT, 512), BF16)
            o_wgu, o_wd, o_g34 = Obj("wgu"), Obj("wd"), Obj("g34")
            for kt in range(8):
                dma("pool", wg[:, kt, :], wg_d[kt * 128:(kt + 1) * 128, :], [], [o_wgu], o_wgu)
                dma("pool", wu[:, kt, :], wu_d[kt * 128:(kt + 1) * 128, :], [], [o_wgu], o_wgu)
            for ft in range(NFT):
                dma("pool", wd[:, ft, :], wd_d[ft * 128:(ft + 1) * 128, :], [], [o_wd], o_wd)
            dma("sp", g3[:], gb_d[2], [], [o_g34], o_g34)
            dma("sp", g4[:], gb_d[3], [], [o_g34], o_g34)
            o_h2T, o_act = Obj("h2T"), Obj("actT")
            o_sg = [Obj("sg0"), Obj("sg1")]
            o_h2 = [Obj("h2_0"), Obj("h2_1")]
            o_y2 = Obj("yb2")
            o_x1_l = [Obj(f"x1b{i}") for i in range(4)]
            NCH = TOK // 512

            def prep_nonpe(c, s_):
                ti = c * 4 + s_
                bi = s_ % 2
                xb, o_x1 = x1b[:, bi, :], o_x1_l[bi]
                dma("sp", xb, out_d[ti * 128:(ti + 1) * 128, :], [o_out[ti]], [o_x1], o_x1)
                hb, o_hb = h2[:, s_ % 2, :], o_h2[s_ % 2]
                rs, o_rs = rstd_from(lambda: xb, hb, [o_x1], o_hb)
                op("dve", lambda e: e.scalar_tensor_tensor(
                    out=hb, in0=xb, scalar=rs, in1=g3[:], op0=ALU.mult, op1=ALU.mult),
                   [o_x1, o_rs, o_g34], [o_hb])

            def prep_pe(c, s_):
                hb, o_hb = h2[:, s_ % 2, :], o_h2[s_ % 2]
                bk = s_ % 2
                for kt in range(8):
                    op("pe", lambda e, kt=kt: e.transpose(
                        bankbf[bk][:, kt * 128:(kt + 1) * 128], hb[:, kt * 128:(kt + 1) * 128], ident[:]),
                       [o_hb, o_id], [bobj[bk]], signal=(kt == 7))
                op("act", lambda e: e.activation(
                    out=h2T[:, :, s_ * 128:(s_ + 1) * 128],
                    in_=bankbf[bk].rearrange("p (k t) -> p k t", k=8), func=AF.Copy),
                   [bobj[bk]], [o_h2T])

            def gateup(c):
                for ft in range(NFT):
                    i = ft % 2
                    bg, bu = i, 2 + i
                    for (wt, bk) in ((wg, bg), (wu, bu)):
                        for kt in range(8):
                            op("pe", lambda e, wt=wt, bk=bk, kt=kt, ft=ft: e.matmul(
                                bank[bk], lhsT=wt[:, kt, ft * 128:(ft + 1) * 128], rhs=h2T[:, kt, :],
                                start=(kt == 0), stop=(kt == 7)),
                               [o_wgu, o_h2T], [bobj[bk]], signal=(kt == 7))
                    op("act", lambda e, i=i, bg=bg: e.activation(out=sg[:, i, :], in_=bank[bg], func=AF.Silu),
                       [bobj[bg]], [o_sg[i]])
                    op("dve", lambda e, i=i, bu=bu, ft=ft: e.tensor_tensor(
                        out=actT[:, ft, :], in0=bank[bu], in1=sg[:, i, :], op=ALU.mult),
                       [bobj[bu], o_sg[i]], [o_act])

            def epi_load(c, s_):
                ti = c * 4 + s_
                bi = 2 + s_ % 2
                dma("sp", x1b[:, bi, :], out_d[ti * 128:(ti + 1) * 128, :], [o_out[ti]], [o_x1_l[bi]], o_x1_l[bi])

            def down_mm(c, s_):
                st = 2 + (s_ % 2)
                for hh in range(2):
                    bk = 2 * st + hh
                    for ft in range(NFT):
                        op("pe", lambda e, ft=ft, bk=bk, hh=hh: e.matmul(
                            bank[bk], lhsT=actT[:, ft, s_ * 128:(s_ + 1) * 128],
                            rhs=wd[:, ft, hh * 512:(hh + 1) * 512], start=(ft == 0), stop=(ft == NFT - 1)),
                           [o_act, o_wd], [bobj[bk]], signal=(ft == NFT - 1))

            def epilogue(c, s_):
                ti = c * 4 + s_
                bi = 2 + s_ % 2
                xb, o_x1 = x1b[:, bi, :], o_x1_l[bi]
                st = 2 + (s_ % 2)
                src = bank2[st]
                rs, o_rs = rstd_from(lambda: src, yb2[:], [bobj[2 * st], bobj[2 * st + 1]], o_y2)
                op("dve", lambda e: e.scalar_tensor_tensor(
                    out=yb2[:], in0=src, scalar=rs, in1=g4[:], op0=ALU.mult, op1=ALU.mult),
                   [bobj[2 * st], bobj[2 * st + 1], o_rs, o_g34], [o_y2])
                op("pool", lambda e: e.tensor_tensor(out=xb, in0=yb2[:], in1=xb, op=ALU.add),
                   [o_y2, o_x1], [o_x1])
                dma("sp", out_d[ti * 128:(ti + 1) * 128, :], xb, [o_x1], [o_out[ti]], o_x1)

            for s_ in range(4):
                prep_nonpe(0, s_)
                prep_pe(0, s_)
            for c in range(NCH):
                if c + 1 < NCH:
                    prep_nonpe(c + 1, 0)
                gateup(c)
                for s_ in range(4):
                    epi_load(c, s_)
                    if c + 1 < NCH and s_ + 1 < 4:
                        prep_nonpe(c + 1, s_ + 1)
                    down_mm(c, s_)
                    if c + 1 < NCH:
                        prep_pe(c + 1, s_)
                    epilogue(c, s_)
            C.barrier()
    return nc


def _tables(hf):
    half = 32
    freqs = (np.float32(10000.0) ** (-np.arange(half, dtype=np.float32) * np.float32(2.0 / 64))).astype(np.float32)
    pos = (hf * TOK - HALO + np.arange(HALO + TOK)).astype(np.float32)
    ang = (pos[:, None] * freqs[None, :]).astype(np.float32).astype(np.float64)
    p = np.arange(128)
    fi = (p % 64) % 32
    sign = np.where((p % 64) < 32, -1.0, 1.0)
    cs = np.cos(ang[:, fi]).T
    sn = np.sin(ang[:, fi]).T * sign[:, None]
    return np.ascontiguousarray(cs.astype(np.float32)), np.ascontiguousarray(sn.astype(np.float32))


def _consts(hf):
    m = np.arange(128)
    partner = np.where((m % 64) < 32, m + 32, m - 32)
    perm = np.zeros((128, 128), np.float32)
    perm[partner, m] = 1.0
    k = np.arange(128)[:, None]
    q = np.arange(128)[None, :]
    mp = (k >= q).astype(np.float32)
    mc = (k <= q).astype(np.float32)
    flag = 0.0 if hf == 0 else 1.0
    mh = mp * flag
    masks = np.stack([np.concatenate([mp, mc, mp, mc], 1),
                      np.concatenate([mh, mc, mp, mc], 1),
                      np.concatenate([mh, mc, mh, mc], 1)]).astype(np.float32)
    wins = np.array([2, 4, 8, 16], np.float32)
    invwin = np.zeros((128, 2), np.float32)
    invcnt = np.zeros((128, 2, 16), np.float32)
    for t in range(2):
        for hh in range(2):
            w = wins[2 * t + hh]
            invwin[64 * hh:64 * hh + 64, t] = 1.0 / w
            posj = hf * TOK + np.arange(16)
            invcnt[64 * hh:64 * hh + 64, t, :] = (1.0 / np.minimum(posj + 1, w)).astype(np.float32)[None, :]
    return perm, masks, invwin, invcnt


_NC_CACHE = {}


def kernel(x, ln_pre_mix, w_in, w_pool, pool_scale, w_out, ln_post_mix, ln_pre_ffn, w_gate, w_up, w_down,
           ln_post_ffn):
    x = np.asarray(x, np.float32)
    f = lambda a: np.ascontiguousarray(np.asarray(a, np.float32))
    w_in0, w_out0, wg0, wu0, wd0 = f(w_in[0]), f(w_out[0]), f(w_gate[0]), f(w_up[0]), f(w_down[0])
    wp = np.asarray(w_pool[0], np.float32)
    wpbd = np.zeros((2, 128, 128), np.float32)
    for t in range(2):
        for hh in range(2):
            wpbd[t, 64 * hh:64 * hh + 64, 64 * hh:64 * hh + 64] = wp[2 * t + hh]
    gb = np.stack([np.broadcast_to(np.asarray(g[0], np.float32)[None, :], (128, D))
                   for g in (ln_pre_mix, ln_post_mix, ln_pre_ffn, ln_post_ffn)]).astype(np.float32)
    gb = np.ascontiguousarray(gb)
    ps_ = np.asarray(pool_scale[0], np.float32)
    pscale = np.ascontiguousarray(np.stack([ps_[0:128], ps_[128:256]], 1))
    in_maps = []
    for c in range(NCORES):
        b, hf = c // 2, c % 2
        xe = np.zeros((HALO + TOK, D), np.float32)
        xe[HALO:] = x[b, hf * TOK:(hf + 1) * TOK]
        if hf == 1:
            xe[:HALO] = x[b, TOK - HALO:TOK]
        cs, sn = _tables(hf)
        perm, masks, invwin, invcnt = _consts(hf)
        in_maps.append({"x": xe, "cs": cs, "sn": sn, "w_in": w_in0, "w_out": w_out0, "w_gate": wg0, "w_up": wu0,
                        "w_down": wd0, "wpool_bd": wpbd, "gb": gb, "pscale": pscale, "perm": perm, "masks": masks,
                        "invwin": invwin, "invcnt": invcnt})
    if "nc" not in _NC_CACHE:
        _NC_CACHE["nc"] = build_program()
    nc = _NC_CACHE["nc"]
    res = run_bass_kernel_spmd(nc, in_maps, core_ids=list(range(NCORES)))
    out = np.zeros((4, SEQ, D), np.float32)
    for c in range(NCORES):
        b, hf = c // 2, c % 2
        out[b, hf * TOK:(hf + 1) * TOK] = res.results[c]["out"]
    return out
```

```python
import os
from contextlib import ExitStack
import numpy as np
import concourse.bass as bass
import concourse.mybir as mybir
from concourse.bass_utils import run_bass_kernel_spmd

F32 = mybir.dt.float32
BF16 = mybir.dt.bfloat16
AF = mybir.ActivationFunctionType
ALU = mybir.AluOpType

D = 1024
SEQ = 8192
TOK = 4096
HALO = 2048
BLK = 2048
DFF = 2816
NFT = DFF // 128
EPS = 1e-6
NCORES = 8


class Sem:
    def __init__(self, h, name):
        self.h = h
        self.name = name
        self.count = 0


class Obj:
    __slots__ = ("name", "w", "r")

    def __init__(self, name):
        self.name = name
        self.w = {}
        self.r = {}


class Ctx:
    def __init__(self, nc, es):
        self.nc = nc
        self.es = es
        self.eng = {}
        for n, h in (("pe", nc.tensor), ("act", nc.scalar), ("dve", nc.vector), ("pool", nc.gpsimd), ("sp", nc.sync)):
            s = Sem(es.enter_context(nc.semaphore("sem_" + n)), "sem_" + n)
            self.eng[n] = (h, s, {})
        self.dsems = []
        self._ds = {}
        self.nwaits = 0

    def dsem(self, name):
        s = Sem(self.es.enter_context(self.nc.semaphore("dsem_" + name)), "dsem_" + name)
        self.dsems.append(s)
        return s

    def _wait(self, en, sem, val):
        h, _, waited = self.eng[en]
        if waited.get(sem.name, 0) >= val:
            return
        assert val <= sem.count, f"wait on unsignaled {sem.name} {val}>{sem.count} from {en}"
        h.wait_ge(sem.h, val)
        waited[sem.name] = val
        self.nwaits += 1

    def _deps(self, en, reads, writes, is_dma, ds=None):
        _, own, _ = self.eng[en]
        deps = {}

        def need(s, v, raw, waw=False):
            if s is own and not is_dma:
                if en == "pe":
                    return
            if waw and s is ds:
                return
            if deps.get(s.name, (None, 0))[1] < v:
                deps[s.name] = (s, v)

        for o in reads:
            for s, v in o.w.values():
                need(s, v, True)
        for o in writes:
            for s, v in o.w.values():
                need(s, v, False, True)
            for s, v in o.r.values():
                need(s, v, False)
        for s, v in deps.values():
            self._wait(en, s, v)

    def op(self, en, fn, reads=(), writes=(), signal=True):
        h, own, _ = self.eng[en]
        writes = list(writes) + [o for o in reads if o.name.startswith("bank") and o not in writes]
        self._deps(en, reads, writes, False)
        inst = fn(h)
        if signal:
            inst.then_inc(own.h, 1)
            own.count += 1
            val = own.count
        else:
            val = own.count + 1
        for o in reads:
            o.r[own.name] = (own, val)
        for o in writes:
            o.w[own.name] = (own, val)
        return inst

    def ds_of(self, obj):
        if obj.name not in self._ds:
            self._ds[obj.name] = self.dsem(obj.name)
        return self._ds[obj.name]

    def dma(self, en, out, in_, reads, writes, key):
        h, own, _ = self.eng[en]
        ds = key if isinstance(key, Sem) else self.ds_of(key)
        self._deps(en, reads, writes, True, ds)
        inst = h.dma_start(out=out, in_=in_)
        inst.then_inc(ds.h, 16)
        ds.count += 16
        for o in reads:
            o.r[ds.name] = (ds, ds.count)
        for o in writes:
            o.w[ds.name] = (ds, ds.count)
        return inst

    def barrier(self):
        sems = [e[1] for e in self.eng.values()] + self.dsems
        for en in self.eng:
            for s in sems:
                if s is self.eng[en][1]:
                    continue
                if s.count > 0:
                    self._wait(en, s, s.count)


def build_program(dbg=None, dbg_blk=0):
    nc = bass.Bass("TRN2", target_bir_lowering=False)
    dt_in = lambda n, s: nc.dram_tensor(n, list(s), F32, kind="ExternalInput").ap()
    x_d = dt_in("x", (HALO + TOK, D))
    cs_d = dt_in("cs", (128, HALO + TOK))
    sn_d = dt_in("sn", (128, HALO + TOK))
    win_d = dt_in("w_in", (D, 2560))
    wout_d = dt_in("w_out", (D, D))
    wg_d = dt_in("w_gate", (D, DFF))
    wu_d = dt_in("w_up", (D, DFF))
    wd_d = dt_in("w_down", (DFF, D))
    wpbd_d = dt_in("wpool_bd", (2, 128, 128))
    gb_d = dt_in("gb", (4, 128, D))
    psc_d = dt_in("pscale", (128, 2))
    perm_d = dt_in("perm", (128, 128))
    mask_d = dt_in("masks", (3, 128, 512))
    invw_d = dt_in("invwin", (128, 2))
    invc_d = dt_in("invcnt", (128, 2, 16))
    out_d = nc.dram_tensor("out", [TOK, D], F32, kind="ExternalOutput").ap()
    dbg_d = {}
    if dbg:
        for n, (s, dty) in dbg.items():
            dbg_d[n] = nc.dram_tensor("dbg_" + n, list(s), dty, kind="ExternalOutput").ap()

    with ExitStack() as es:
        C = Ctx(nc, es)
        op, dma = C.op, C.dma
        sb = lambda n, s, d: es.enter_context(nc.sbuf_tensor(n, list(s), d))
        ps = es.enter_context(nc.psum_tensor("ps", [128, 4096], F32))
        bank = [ps[:, 512 * i:512 * (i + 1)] for i in range(8)]
        bankbf = [ps[:, 512 * i:512 * (i + 1)].bitcast(BF16) for i in range(8)]
        bobj = [Obj(f"bank{i}") for i in range(8)]
        bank2 = [ps[:, 1024 * i:1024 * (i + 1)] for i in range(4)]

        ds_const = C.dsem("const")
        o_const = Obj("const")
        ident = sb("ident", (128, 128), BF16)
        mhalf = sb("mhalf", (128, 1), F32)
        ssb = sb("ssb", (128, 160), F32)
        tsb = sb("tsb", (128, 160), F32)
        rsb = sb("rsb", (128, 160), F32)
        o_ss = Obj("ss")
        p1 = ExitStack()
        sbc = lambda n, s, d: p1.enter_context(nc.sbuf_tensor(n, list(s), d))
        ident_f = sbc("ident_f", (128, 128), F32)
        perm = sbc("perm_sb", (128, 128), BF16)
        masks = sbc("masks_sb", (128, 3, 512), BF16)
        wpbd = sbc("wpbd_sb", (128, 2, 128), BF16)
        psc = sbc("psc_sb", (128, 2), F32)
        invw = sbc("invw_sb", (128, 2), F32)
        invc = sbc("invc_sb", (128, 2, 16), F32)
        dma("pool", perm[:], perm_d, [], [o_const], ds_const)
        dma("pool", masks[:], mask_d.rearrange("v p c -> p v c"), [], [o_const], ds_const)
        dma("pool", wpbd[:], wpbd_d.rearrange("t p c -> p t c"), [], [o_const], ds_const)
        ds_const2 = C.dsem("const2")
        dma("sp", psc[:], psc_d, [], [o_const], ds_const2)
        dma("sp", invw[:], invw_d, [], [o_const], ds_const2)
        dma("sp", invc[:], invc_d, [], [o_const], ds_const2)
        o_id = Obj("ident")
        op("pool", lambda e: e.memset(ident_f[:], 1.0), [], [o_id])
        op("pool", lambda e: e.affine_select(out=ident_f[:], in_=ident_f[:], pattern=[[-1, 128]],
                                             compare_op=ALU.is_equal, fill=0.0, base=0, channel_multiplier=1),
           [o_id], [o_id])
        op("dve", lambda e: e.tensor_copy(out=ident[:], in_=ident_f[:]), [o_id], [o_id])
        op("dve", lambda e: e.memset(mhalf[:], -0.5), [], [o_const])
        op("dve", lambda e: e.memset(ssb[:], 0.0), [], [o_ss])

        sscol = [0]

        def rstd_from(psrc_ap_fn, junk_ap, src_objs, junk_obj, n_feat=D):
            c = sscol[0]
            sscol[0] += 1
            assert c < 160
            oc = Obj(f"rs{sscol[0]}")
            op("act", lambda e: e.activation(out=junk_ap, in_=psrc_ap_fn(), func=AF.Square,
                                             accum_out=ssb[:, c:c + 1]),
               list(src_objs) + [o_ss], [junk_obj, oc])
            op("dve", lambda e: e.tensor_scalar(out=tsb[:, c:c + 1], in0=ssb[:, c:c + 1], scalar1=1.0 / n_feat,
                                                scalar2=EPS, op0=ALU.mult, op1=ALU.add), [oc], [oc])
            op("pool", lambda e: e.tensor_tensor(out=rsb[:, c:c + 1], in0=tsb[:, c:c + 1], in1=mhalf[:],
                                                 op=ALU.pow), [oc, o_const], [oc])
            return rsb[:, c:c + 1], oc

        def rstd_multi(items, n_feat=D):
            n = len(items)
            c0 = sscol[0]
            sscol[0] += n
            assert sscol[0] <= 160
            oc = Obj(f"rsm{c0}")
            for j, (src, junk_ap, src_objs, junk_obj) in enumerate(items):
                op("act", lambda e, src=src, junk_ap=junk_ap, j=j: e.activation(
                    out=junk_ap, in_=src, func=AF.Square, accum_out=ssb[:, c0 + j:c0 + j + 1]),
                   list(src_objs) + [o_ss], [junk_obj, oc])
            op("dve", lambda e: e.tensor_scalar(out=tsb[:, c0:c0 + n], in0=ssb[:, c0:c0 + n], scalar1=1.0 / n_feat,
                                                scalar2=EPS, op0=ALU.mult, op1=ALU.add), [oc], [oc])
            op("pool", lambda e: e.tensor_tensor(out=rsb[:, c0:c0 + n], in0=tsb[:, c0:c0 + n],
                                                 in1=mhalf[:].to_broadcast([128, n]), op=ALU.pow),
               [oc, o_const], [oc])
            return [rsb[:, c0 + j:c0 + j + 1] for j in range(n)], oc

        with p1:
            sb1 = lambda n, s, d: p1.enter_context(nc.sbuf_tensor(n, list(s), d))
            xT = sb1("xT", (128, 2, 8, BLK), BF16)
            OT = sb1("OT", (128, 8, BLK), BF16)
            cst = sb1("cst", (128, 2 * BLK), F32)
            snt = sb1("snt", (128, 2 * BLK), F32)
            kT = sb1("kT", (128, 2, BLK), BF16)
            vT = sb1("vT", (128, 2, BLK), BF16)
            qT = sb1("qT", (128, BLK), BF16)
            wg_in = sb1("wg_in", (128, 8, 384), BF16)
            scr = sb1("scr", (128, 10240), F32)
            rbf = sb1("rbf", (128, 2, 512), BF16)
            PT = sb1("PT", (128, 2, 2, 512), BF16)
            acc = scr[:, 0:4096].rearrange("p (h t) -> p h t", h=2)
            Vaug = scr[:, 4096:8192].bitcast(BF16).rearrange("p (s c) -> p s c", s=32)
            Rn = scr[:, 9216:10240]
            t1b = [scr[:, 8192:8704], scr[:, 9216:9728]]
            t2b = [scr[:, 8704:9216], scr[:, 9728:10240]]
            NXS = 5
            xs = [scr[:, 1024 * i:1024 * (i + 1)] for i in range(NXS)]
            NXN = 2
            xn = [scr[:, 5120 + 512 * i:5120 + 512 * (i + 1)].bitcast(BF16) for i in range(NXN)]
            gt = qT[:].bitcast(F32)
            gt2 = scr[:, 6144:7168]
            xs2 = [scr[:, 7168 + 1024 * i:7168 + 1024 * (i + 1)] for i in range(2)]
            ys = [scr[:, 9216:10240]] * 2
            ub = scr[:, 6144:7200].rearrange("p (t c) -> p t c", t=2)
            lvb = [scr[:, 7200 + 1056 * i:7200 + 1056 * (i + 1)].rearrange("p (t c) -> p t c", t=2) for i in range(2)]
            dTb = scr[:, 9312:9824].bitcast(BF16).rearrange("p (t c) -> p t c", t=2)
            ptmp = scr[:, 9824:9888].rearrange("p (t c) -> p t c", t=2)
            wpl_in = PT[:].rearrange("p a b c -> p (a b c)").rearrange("p (k c) -> p k c", k=8)

            o_scr = Obj("scr")
            o_xT = [[Obj(f"xT{s}_{c}") for c in range(4)] for s in range(2)]
            o_OT = [Obj(f"OT{k}") for k in range(8)]
            o_tab = Obj("tab")
            o_k, o_v, o_q = Obj("kT"), Obj("vT"), Obj("qT")
            o_wg, o_wpl = Obj("wg_in"), Obj("wpl_in")
            o_acc, o_V, o_R = Obj("acc"), Obj("Vaug"), Obj("Rn")
            o_t1 = [Obj("t1k"), Obj("t1q")]
            o_t2 = [Obj("t2k"), Obj("t2q")]
            o_rbf = [Obj("rbf0"), Obj("rbf1")]
            o_PT = [[Obj(f"PT{s}{h}") for h in range(2)] for s in range(2)]
            ds_tab = C.dsem("tab")
            o_out = [Obj(f"out{t}") for t in range(TOK // 128)]

            def stage_fence(reads_scr=False):
                pass

            o_g = Obj("gt")
            o_xs_l = [Obj(f"xs{i}") for i in range(NXS)]
            o_xn_l = [Obj(f"xn{i}") for i in range(NXN)]

            xnext = [0, 0]

            def stage_x(b, prefetched=False, only_prefetch=False, after_group=None):
                jobs = [(0, 0), (BLK, 1)] if b == 0 else [(2 * BLK, 0)]
                tiles = [(e0 + tt * 128, slot, tt) for e0, slot in jobs for tt in range(BLK // 128)]
                GS = 2

                def ensure_loaded(upto):
                    while xnext[b] <= min(upto, len(tiles) - 1):
                        r0 = tiles[xnext[b]][0]
                        i = xnext[b] % NXS
                        dma("sp", xs[i], x_d[r0:r0 + 128, :], [], [o_xs_l[i]], o_xs_l[i])
                        xnext[b] += 1

                if not prefetched:
                    dma("sp", gt, gb_d[0], [], [o_g, o_q], o_g)
                ensure_loaded(3)
                if only_prefetch:
                    return
                for g0 in range(0, len(tiles), GS):
                    ensure_loaded(g0 + GS - 1 + 3)
                    grp = tiles[g0:g0 + GS]
                    items = [(xs[(g0 + j) % NXS], bank2[1], [o_xs_l[(g0 + j) % NXS], bobj[2]], bobj[3])
                             for j in range(len(grp))]
                    rss, o_rs = rstd_multi(items)
                    for j, (r0, slot, tt) in enumerate(grp):
                        it = g0 + j
                        xb, o_xs = xs[it % NXS], o_xs_l[it % NXS]
                        nb, o_xn = xn[it % NXN], o_xn_l[it % NXN]
                        bk = 4 + (it % 4)
                        op("dve", lambda e, xb=xb, nb=nb, rs=rss[j]: e.scalar_tensor_tensor(
                            out=nb, in0=xb, scalar=rs, in1=gt, op0=ALU.mult, op1=ALU.mult),
                           [o_xs, o_rs, o_g], [o_xn])
                        for kt in range(8):
                            op("pe", lambda e, nb=nb, kt=kt, bk=bk: e.transpose(
                                bankbf[bk][:, kt * 128:(kt + 1) * 128], nb[:, kt * 128:(kt + 1) * 128], ident[:]),
                               [o_xn, o_id], [bobj[bk]], signal=(kt == 7))
                        ch = tt // 4
                        if it % 2 == 0:
                            op("act", lambda e, bk=bk, slot=slot, tt=tt: e.activation(
                                out=xT[:, slot, :, tt * 128:(tt + 1) * 128],
                                in_=bankbf[bk].rearrange("p (k t) -> p k t", k=8), func=AF.Copy),
                               [bobj[bk]], [o_xT[slot][ch]])
                        else:
                            op("dve", lambda e, bk=bk, slot=slot, tt=tt: e.tensor_copy(
                                out=xT[:, slot, :, tt * 128:(tt + 1) * 128],
                                in_=bankbf[bk].rearrange("p (k t) -> p k t", k=8)),
                               [bobj[bk]], [o_xT[slot][ch]])
                    if after_group is not None:
                        after_group(g0 + GS, tiles)

            def load_wg(g):
                for j, c0 in enumerate((256 + 128 * g, 1024 + 128 * g, 1792 + 128 * g)):
                    dma("pool", wg_in[:, :, 128 * j:128 * (j + 1)],
                        win_d.rearrange("(k p) c -> p k c", p=128)[:, :, c0:c0 + 128], [], [o_wg], o_wg)

            def stage_p(b, g):
                pred_slot = b % 2
                cur_slot = 1 - pred_slot
                chunks = [(half, ch) for half in range(2) for ch in range(4)]

                def pmm(wcol0, slot, ch, bk):
                    for kt in range(8):
                        op("pe", lambda e, kt=kt: e.matmul(
                            bank[bk], lhsT=wg_in[:, kt, wcol0:wcol0 + 128],
                            rhs=xT[:, slot, kt, ch * 512:(ch + 1) * 512], start=(kt == 0), stop=(kt == 7)),
                           [o_wg, o_xT[slot][ch]], [bobj[bk]], signal=(kt == 7))

                def proj_mm(ci):
                    half, ch = chunks[ci]
                    slot = pred_slot if half == 0 else cur_slot
                    base = 3 * (ci % 2)
                    pmm(128, slot, ch, base)
                    pmm(256, slot, ch, base + 1)
                    if half == 1:
                        pmm(0, slot, ch, base + 2)

                def copies(ci):
                    half, ch = chunks[ci]
                    base = 3 * (ci % 2)
                    op("act", lambda e: e.activation(out=rbf[:, 0, :], in_=bank[base], func=AF.Copy),
                       [bobj[base]], [o_rbf[0]])
                    if half == 1:
                        op("act", lambda e: e.activation(out=rbf[:, 1, :], in_=bank[base + 2], func=AF.Copy),
                           [bobj[base + 2]], [o_rbf[1]])
                    op("act", lambda e: e.activation(out=vT[:, half, ch * 512:(ch + 1) * 512], in_=bank[base + 1],
                                                     func=AF.Copy), [bobj[base + 1]], [o_v])

                def rope_rest(ci):
                    half, ch = chunks[ci]
                    base = 3 * (ci % 2)
                    tab0 = half * BLK + ch * 512
                    jobs = [(0, base, 6, kT[:, half, ch * 512:(ch + 1) * 512], o_k)]
                    if half == 1:
                        jobs.append((1, base + 2, 7, qT[:, ch * 512:(ch + 1) * 512], o_q))
                    for (i, bk, wb, dst, dobj) in jobs:
                        op("pe", lambda e, i=i, wb=wb: e.matmul(bank[wb], lhsT=perm[:], rhs=rbf[:, i, :],
                                                                 start=True, stop=True),
                           [o_rbf[i], o_const], [bobj[wb]])
                    for (i, bk, wb, dst, dobj) in jobs:
                        op("dve", lambda e, i=i, bk=bk: e.tensor_tensor(
                            out=t1b[i], in0=bank[bk], in1=cst[:, tab0:tab0 + 512], op=ALU.mult),
                           [bobj[bk], o_tab], [o_t1[i]])
                        op("dve", lambda e, i=i, wb=wb: e.tensor_tensor(
                            out=t2b[i], in0=bank[wb], in1=snt[:, tab0:tab0 + 512], op=ALU.mult),
                           [bobj[wb], o_tab], [o_t2[i]])
                        op("pool", lambda e, i=i, dst=dst: e.tensor_tensor(out=dst, in0=t1b[i], in1=t2b[i],
                                                                            op=ALU.add),
                           [o_t1[i], o_t2[i]], [dobj])

                proj_mm(0)
                copies(0)
                for ci in range(len(chunks)):
                    if ci + 1 < len(chunks):
                        proj_mm(ci + 1)
                    rope_rest(ci)
                    if ci + 1 < len(chunks):
                        copies(ci + 1)

            actr = [0]

            def stage_a(b, g):
                first_cfg = True
                for d in (1, 4, 16):
                    nbh = 16 // d
                    span = 128 * d

                    def toks(r, n):
                        if n < 0:
                            return 0, (nbh - 1) * span + r
                        return 1, n * span + r

                    def sl(t0):
                        return slice(t0, t0 + 127 * d + 1, d)

                    klist = [(r, n) for r in range(d) for n in range(-1, nbh)]
                    vslot = {kn: i for i, kn in enumerate(klist)}
                    for j0 in range(0, len(klist), 8):
                        batch = klist[j0:j0 + 8]
                        bk = 6 + (actr[0] % 2)
                        actr[0] += 1
                        for j, (r, n) in enumerate(batch):
                            half, t0 = toks(r, n)
                            op("pe", lambda e, j=j, half=half, t0=t0, bk=bk: e.transpose(
                                bankbf[bk][:, j * 128:(j + 1) * 128], vT[:, half, sl(t0)], ident[:]),
                               [o_v, o_id], [bobj[bk]], signal=(j == len(batch) - 1))
                        nbt = len(batch)
                        dst = Vaug[:, j0:j0 + nbt, :].rearrange("p s (a c) -> p s a c", a=4)[:, :, 0:4:3, :]
                        src = bankbf[bk][:, 0:nbt * 128].rearrange("p (s h c) -> p s h c", s=nbt, h=2)
                        op("act", lambda e, dst=dst, src=src: e.activation(out=dst, in_=src, func=AF.Copy),
                           [bobj[bk]], [o_V])

                    qbs = [(r, n) for r in range(d) for n in range(nbh)]
                    pairs = [qbs[p0:p0 + 2] for p0 in range(0, len(qbs), 2)]
                    base_ctr = actr[0]
                    actr[0] += len(pairs)

                    def emit_qk(pi):
                        pair = pairs[pi]
                        s_ = (base_ctr + pi) % 2
                        halo = [b == 0 and n == 0 for (_, n) in pair]
                        mv = 0 if not any(halo) else (2 if all(halo) else 1)
                        assert not (halo[1] and not halo[0])
                        for h in range(2):
                            bk = 2 * s_ + h
                            for i, (r, n) in enumerate(pair):
                                _, tq = toks(r, n)
                                for kk, nk in enumerate((n - 1, n)):
                                    khalf, tk = toks(r, nk)
                                    col = (2 * i + kk) * 128
                                    last = (i == 1 and kk == 1)
                                    op("pe", lambda e, h=h, bk=bk, col=col, khalf=khalf, tk=tk, tq=tq: e.matmul(
                                        bank[bk][:, col:col + 128],
                                        lhsT=kT[64 * h:64 * h + 64, khalf, sl(tk)],
                                        rhs=qT[64 * h:64 * h + 64, sl(tq)], start=True, stop=True),
                                       [o_k, o_q], [bobj[bk]], signal=last)
                        for h in range(2):
                            bk = 2 * s_ + h
                            op("act", lambda e, h=h, bk=bk: e.activation(
                                out=PT[:, s_, h, :], in_=bank[bk], func=AF.Exp, scale=0.125),
                               [bobj[bk]], [o_PT[s_][h]])
                            op("dve", lambda e, h=h: e.tensor_tensor(
                                out=PT[:, s_, h, :], in0=PT[:, s_, h, :], in1=masks[:, mv, :], op=ALU.mult),
                               [o_PT[s_][h], o_const], [o_PT[s_][h]])

                    def emit_pv(pi, first):
                        pair = pairs[pi]
                        s_ = (base_ctr + pi) % 2
                        ob = 4 + s_
                        for i, (r, n) in enumerate(pair):
                            for h in range(2):
                                col = (2 * i + h) * 128
                                for kk, nk in enumerate((n - 1, n)):
                                    vs = vslot[(r, nk)]
                                    last = (i == 1 and h == 1 and kk == 1)
                                    op("pe", lambda e, vs=vs, h=h, col=col, kk=kk, i=i: e.matmul(
                                        bank[ob][:, col:col + 128],
                                        lhsT=Vaug[:, vs, 128 * h:128 * (h + 1)],
                                        rhs=PT[:, s_, h, (2 * i + kk) * 128:(2 * i + kk + 1) * 128],
                                        start=(kk == 0), stop=(kk == 1)),
                                       [o_V, o_PT[s_][h]], [bobj[ob]], signal=last)
                        for i, (r, n) in enumerate(pair):
                            _, tq = toks(r, n)
                            dst = acc[:, :, sl(tq)]
                            src = bank[ob][:, 256 * i:256 * (i + 1)].rearrange("p (h l) -> p h l", h=2)
                            if first:
                                op("act", lambda e, dst=dst, src=src: e.activation(out=dst, in_=src, func=AF.Copy),
                                   [bobj[ob]], [o_acc])
                            else:
                                op("dve", lambda e, dst=dst, src=src: e.tensor_tensor(
                                    out=dst, in0=src, in1=dst, op=ALU.add),
                                   [bobj[ob], o_acc], [o_acc])

                    emit_qk(0)
                    for pi in range(len(pairs)):
                        if pi + 1 < len(pairs):
                            emit_qk(pi + 1)
                        emit_pv(pi, first_cfg)
                    first_cfg = False
                rn_objs = [o_R, o_t1[1], o_t2[1]]
                kt = 2 + g
                for hv in range(2):
                    tsl = slice(hv * 1024, (hv + 1) * 1024)
                    op("act", lambda e: e.activation(out=Rn[0:64, :], in_=acc[64:128, 0, tsl], func=AF.Ln),
                       [o_acc], rn_objs)
                    op("act", lambda e: e.activation(out=Rn[64:128, :], in_=acc[0:64, 1, tsl], func=AF.Ln),
                       [o_acc], rn_objs)
                    op("act", lambda e: e.activation(out=Rn, in_=Rn, func=AF.Exp, scale=-1.0), rn_objs, rn_objs)
                    op("pool", lambda e: e.tensor_tensor(out=OT[0:64, kt, tsl], in0=acc[0:64, 0, tsl], in1=Rn[0:64, :],
                                                         op=ALU.mult), [o_acc] + rn_objs, [o_OT[kt]])
                    op("dve", lambda e: e.tensor_tensor(out=OT[64:128, kt, tsl], in0=acc[64:128, 1, tsl],
                                                        in1=Rn[64:128, :], op=ALU.mult), [o_acc] + rn_objs, [o_OT[kt]])

            o_ub, o_dT = Obj("ub"), Obj("dT")
            o_lvb = [Obj("lv0"), Obj("lv1")]

            def pool_begin(b):
                pred_slot = b % 2
                dma("pool", wpl_in, win_d.rearrange("(k p) c -> p k c", p=128)[:, :, 0:256],
                    [], [o_wpl, o_PT[0][0], o_PT[0][1], o_PT[1][0], o_PT[1][1]], o_wpl)
                for t in range(2):
                    bk = t
                    for kt in range(8):
                        op("pe", lambda e, kt=kt, bk=bk, t=t: e.matmul(
                            bank[bk][:, 0:16], lhsT=wpl_in[:, kt, 128 * t:128 * (t + 1)],
                            rhs=xT[:, pred_slot, kt, BLK - 16:BLK], start=(kt == 0), stop=(kt == 7)),
                           [o_wpl, o_xT[pred_slot][3]], [bobj[bk]], signal=(kt == 7))
                    op("act", lambda e, bk=bk, t=t: e.activation(out=ub[:, t, 0:16], in_=bank[bk][:, 0:16],
                                                                 func=AF.Copy), [bobj[bk]], [o_ub])

            def pool_chunk(b, ch):
                cur_slot = 1 - (b % 2)
                for t in range(2):
                    bk = t
                    for kt in range(8):
                        op("pe", lambda e, kt=kt, bk=bk, t=t: e.matmul(
                            bank[bk], lhsT=wpl_in[:, kt, 128 * t:128 * (t + 1)],
                            rhs=xT[:, cur_slot, kt, ch * 512:(ch + 1) * 512], start=(kt == 0), stop=(kt == 7)),
                           [o_wpl, o_xT[cur_slot][ch]], [bobj[bk]], signal=(kt == 7))
                    op("act", lambda e, bk=bk, t=t: e.activation(out=ub[:, t, 16:528], in_=bank[bk],
                                                                 func=AF.Copy), [bobj[bk]], [o_ub])
                prev, o_prev = ub, o_ub
                for k in range(4):
                    sh = 1 << k
                    lo = (1 << (k + 1)) - 1
                    cur, o_cur = lvb[k % 2], o_lvb[k % 2]
                    op("dve", lambda e, prev=prev, cur=cur, sh=sh, lo=lo: e.tensor_tensor(
                        out=cur[:, :, lo:528], in0=prev[:, :, lo:528], in1=prev[:, :, lo - sh:528 - sh],
                        op=ALU.add), [o_prev], [o_cur])
                    t, hh = k // 2, k % 2
                    p0 = 64 * hh
                    op("dve", lambda e, t=t, cur=cur, p0=p0: e.scalar_tensor_tensor(
                        out=dTb[p0:p0 + 64, t, :], in0=cur[p0:p0 + 64, t, 16:528],
                        scalar=invw[p0:p0 + 64, t:t + 1], in1=ub[p0:p0 + 64, t, 16:528],
                        op0=ALU.mult, op1=ALU.subtract), [o_cur, o_ub, o_const], [o_dT])
                    if b == 0 and ch == 0:
                        op("dve", lambda e, t=t, cur=cur, p0=p0: e.tensor_tensor(
                            out=ptmp[p0:p0 + 64, t, 0:16], in0=cur[p0:p0 + 64, t, 16:32],
                            in1=invc[p0:p0 + 64, t, :], op=ALU.mult), [o_cur, o_const], [o_dT])
                        op("dve", lambda e, t=t, p0=p0: e.tensor_tensor(
                            out=dTb[p0:p0 + 64, t, 0:16], in0=ptmp[p0:p0 + 64, t, 0:16],
                            in1=ub[p0:p0 + 64, t, 16:32], op=ALU.subtract), [o_dT, o_ub], [o_dT])
                    prev, o_prev = cur, o_cur
                for t in range(2):
                    bk = t
                    op("pe", lambda e, t=t, bk=bk: e.matmul(bank[bk], lhsT=wpbd[:, t, :], rhs=dTb[:, t, :],
                                                           start=True, stop=True),
                       [o_dT, o_const], [bobj[bk]])
                    op("act", lambda e, t=t, bk=bk: e.activation(
                        out=OT[:, t, ch * 512:(ch + 1) * 512], in_=bank[bk], func=AF.Copy,
                        scale=psc[:, t:t + 1]), [bobj[bk], o_const], [o_OT[t]])
                op("dve", lambda e: e.tensor_copy(out=ptmp[:, :, 16:32], in_=ub[:, :, 512:528]), [o_ub], [o_dT])
                op("act", lambda e: e.activation(out=ub[:, :, 0:16], in_=ptmp[:, :, 16:32], func=AF.Copy),
                   [o_dT], [o_ub])

            o_wo, o_g2, o_yb = Obj("wout"), Obj("gt2"), Obj("ys")

            def wout_view(b):
                return xT[:, b % 2, 0:4, :].rearrange("p k (a c) -> p (k a) c", a=2)

            def gt2_view(b):
                return xT[:, b % 2, 4, :].bitcast(F32)

            def prefetch_o(b):
                dma("pool", wout_view(b), wout_d.rearrange("(k p) c -> p k c", p=128),
                    [], [o_wo] + o_xT[b % 2], o_wo)
                dma("sp", gt2_view(b), gb_d[1], [], [o_g2] + o_xT[b % 2], o_g2)

            def stage_o(b):
                wout = wout_view(b)
                gt2 = gt2_view(b)
                o_xs2_l = [Obj("xs2_0"), Obj("xs2_1")]
                junk = bank2[3]
                o_j = bobj[7]
                for tt in range(BLK // 128):
                    st = tt % 3
                    xb = xs2[tt % 2]
                    yb = ys[0]
                    o_xb = o_xs2_l[tt % 2]
                    tok0 = b * BLK + tt * 128
                    dma("sp", xb, x_d[HALO + tok0:HALO + tok0 + 128, :], [], [o_xb], o_xb)
                    for hh in range(2):
                        bk = 2 * st + hh
                        for kt in range(8):
                            op("pe", lambda e, kt=kt, bk=bk, hh=hh, tt=tt: e.matmul(
                                bank[bk], lhsT=OT[:, kt, tt * 128:(tt + 1) * 128],
                                rhs=wout[:, kt, hh * 512:(hh + 1) * 512], start=(kt == 0), stop=(kt == 7)),
                               [o_OT[kt], o_wo], [bobj[bk]], signal=(kt == 7))
                    src = bank2[st]
                    rs, o_rs = rstd_from(lambda src=src: src, junk, [bobj[2 * st], bobj[2 * st + 1], bobj[6]], o_j)
                    op("dve", lambda e, yb=yb, src=src, rs=rs: e.scalar_tensor_tensor(
                        out=yb, in0=src, scalar=rs, in1=gt2, op0=ALU.mult, op1=ALU.mult),
                       [bobj[2 * st], bobj[2 * st + 1], o_rs, o_g2], [o_yb])
                    op("dve", lambda e, yb=yb, xb=xb: e.tensor_tensor(out=xb, in0=yb, in1=xb, op=ALU.add),
                       [o_yb, o_xb], [o_xb])
                    ti = tok0 // 128
                    dma("sp", out_d[tok0:tok0 + 128, :], xb, [o_xb], [o_out[ti]], o_xb)

            def fence():
                pass

            for b in range(2):
                dma("sp", cst[:], cs_d[:, b * BLK:b * BLK + 2 * BLK], [], [o_tab], ds_tab)
                dma("sp", snt[:], sn_d[:, b * BLK:b * BLK + 2 * BLK], [], [o_tab], ds_tab)
                C.barrier()
                load_wg(0)

                def hook(ndone, tiles, b=b):
                    npred = len(tiles) - BLK // 128
                    if ndone == npred or (npred == 0 and ndone == 2):
                        pass
                    if ndone == max(npred, 2) and not hook.started:
                        pool_begin(b)
                        hook.started = True
                    k = ndone - npred
                    if hook.started and k > 0 and k % 4 == 0:
                        while hook.next_ch < k // 4:
                            pool_chunk(b, hook.next_ch)
                            hook.next_ch += 1
                hook.started = False
                hook.next_ch = 0
                stage_x(b, prefetched=(b == 1), after_group=hook)
                C.barrier()
                op("pool", lambda e: e.memset(Vaug[:, :, 64:192], 1.0), [], [o_V])
                for g in range(6):
                    stage_p(b, g)
                    if g + 1 < 6:
                        load_wg(g + 1)
                    else:
                        prefetch_o(b)
                    stage_a(b, g)
                C.barrier()
                if dbg and b == dbg_blk:
                    ds_dbg = C.dsem("dbg")
                    od = Obj("dbg")
                    srcs = {"OT": OT[:].rearrange("p k t -> p (k t)"), "rsb": rsb[:], "kT": kT[:].rearrange("p h t -> p (h t)"),
                            "vT": vT[:].rearrange("p h t -> p (h t)"), "qT": qT[:], "acc": scr[:, 0:4096],
                            "xT": xT[:].rearrange("p s k t -> p (s k t)")}
                    for n in dbg_d:
                        dma("sp", dbg_d[n], srcs[n], [], [od], ds_dbg)
                    C.barrier()
                if b == 0:
                    stage_x(1, only_prefetch=True)
                stage_o(b)
                C.barrier()
            if dbg and "x1" in dbg:
                pass
        C.barrier()
        with ExitStack() as p2:
            sb2 = lambda n, s, d: p2.enter_context(nc.sbuf_tensor(n, list(s), d))
            wg = sb2("wg", (128, 8, DFF), BF16)
            wu = sb2("wu", (128, 8, DFF), BF16)
            wd = sb2("wd", (128, NFT, D), BF16)
            x1b = sb2("x1b", (128, 4, D), F32)
            yb2 = sb2("yb2", (128, D), F32)
            g3 = sb2("g3", (128, D), F32)
            g4 = sb2("g4", (128, D), F32)
            sg = sb2("sg", (128, 2, 512), F32)
            h2 = sb2("h2", (128, 2, D), BF16)
            h2T = sb2("h2T", (128, 8, 512), BF16)
            actT = sb2("actT", (128, NFT, 512), BF16)
            o_wgu, o_wd, o_g34 = Obj("wgu"), Obj("wd"), Obj("g34")
            for kt in range(8):
                dma("pool", wg[:, kt, :], wg_d[kt * 128:(kt + 1) * 128, :], [], [o_wgu], o_wgu)
                dma("pool", wu[:, kt, :], wu_d[kt * 128:(kt + 1) * 128, :], [], [o_wgu], o_wgu)
            for ft in range(NFT):
                dma("pool", wd[:, ft, :], wd_d[ft * 128:(ft + 1) * 128, :], [], [o_wd], o_wd)
            dma("sp", g3[:], gb_d[2], [], [o_g34], o_g34)
            dma("sp", g4[:], gb_d[3], [], [o_g34], o_g34)
            o_h2T, o_act = Obj("h2T"), Obj("actT")
            o_sg = [Obj("sg0"), Obj("sg1")]
            o_h2 = [Obj("h2_0"), Obj("h2_1")]
            o_y2 = Obj("yb2")
            o_x1_l = [Obj(f"x1b{i}") for i in range(4)]
            NCH = TOK // 512

            def prep_nonpe(c, s_):
                ti = c * 4 + s_
                bi = s_ % 2
                xb, o_x1 = x1b[:, bi, :], o_x1_l[bi]
                dma("sp", xb, out_d[ti * 128:(ti + 1) * 128, :], [o_out[ti]], [o_x1], o_x1)
                hb, o_hb = h2[:, s_ % 2, :], o_h2[s_ % 2]
                rs, o_rs = rstd_from(lambda: xb, hb, [o_x1], o_hb)
                op("dve", lambda e: e.scalar_tensor_tensor(
                    out=hb, in0=xb, scalar=rs, in1=g3[:], op0=ALU.mult, op1=ALU.mult),
                   [o_x1, o_rs, o_g34], [o_hb])

            def prep_pe(c, s_):
                hb, o_hb = h2[:, s_ % 2, :], o_h2[s_ % 2]
                bk = s_ % 2
                for kt in range(8):
                    op("pe", lambda e, kt=kt: e.transpose(
                        bankbf[bk][:, kt * 128:(kt + 1) * 128], hb[:, kt * 128:(kt + 1) * 128], ident[:]),
                       [o_hb, o_id], [bobj[bk]], signal=(kt == 7))
                op("act", lambda e: e.activation(
                    out=h2T[:, :, s_ * 128:(s_ + 1) * 128],
                    in_=bankbf[bk].rearrange("p (k t) -> p k t", k=8), func=AF.Copy),
                   [bobj[bk]], [o_h2T])

            def gateup(c):
                for ft in range(NFT):
                    i = ft % 2
                    bg, bu = i, 2 + i
                    for (wt, bk) in ((wg, bg), (wu, bu)):
                        for kt in range(8):
                            op("pe", lambda e, wt=wt, bk=bk, kt=kt, ft=ft: e.matmul(
                                bank[bk], lhsT=wt[:, kt, ft * 128:(ft + 1) * 128], rhs=h2T[:, kt, :],
                                start=(kt == 0), stop=(kt == 7)),
                               [o_wgu, o_h2T], [bobj[bk]], signal=(kt == 7))
                    op("act", lambda e, i=i, bg=bg: e.activation(out=sg[:, i, :], in_=bank[bg], func=AF.Silu),
                       [bobj[bg]], [o_sg[i]])
                    op("dve", lambda e, i=i, bu=bu, ft=ft: e.tensor_tensor(
                        out=actT[:, ft, :], in0=bank[bu], in1=sg[:, i, :], op=ALU.mult),
                       [bobj[bu], o_sg[i]], [o_act])

            def epi_load(c, s_):
                ti = c * 4 + s_
                bi = 2 + s_ % 2
                dma("sp", x1b[:, bi, :], out_d[ti * 128:(ti + 1) * 128, :], [o_out[ti]], [o_x1_l[bi]], o_x1_l[bi])

            def down_mm(c, s_):
                st = 2 + (s_ % 2)
                for hh in range(2):
                    bk = 2 * st + hh
                    for ft in range(NFT):
                        op("pe", lambda e, ft=ft, bk=bk, hh=hh: e.matmul(
                            bank[bk], lhsT=actT[:, ft, s_ * 128:(s_ + 1) * 128],
                            rhs=wd[:, ft, hh * 512:(hh + 1) * 512], start=(ft == 0), stop=(ft == NFT - 1)),
                           [o_act, o_wd], [bobj[bk]], signal=(ft == NFT - 1))

            def epilogue(c, s_):
                ti = c * 4 + s_
                bi = 2 + s_ % 2
                xb, o_x1 = x1b[:, bi, :], o_x1_l[bi]
                st = 2 + (s_ % 2)
                src = bank2[st]
                rs, o_rs = rstd_from(lambda: src, yb2[:], [bobj[2 * st], bobj[2 * st + 1]], o_y2)
                op("dve", lambda e: e.scalar_tensor_tensor(
                    out=yb2[:], in0=src, scalar=rs, in1=g4[:], op0=ALU.mult, op1=ALU.mult),
                   [bobj[2 * st], bobj[2 * st + 1], o_rs, o_g34], [o_y2])
                op("pool", lambda e: e.tensor_tensor(out=xb, in0=yb2[:], in1=xb, op=ALU.add),
                   [o_y2, o_x1], [o_x1])
                dma("sp", out_d[ti * 128:(ti + 1) * 128, :], xb, [o_x1], [o_out[ti]], o_x1)

            for s_ in range(4):
                prep_nonpe(0, s_)
                prep_pe(0, s_)
            for c in range(NCH):
                if c + 1 < NCH:
                    prep_nonpe(c + 1, 0)
                gateup(c)
                for s_ in range(4):
                    epi_load(c, s_)
                    if c + 1 < NCH and s_ + 1 < 4:
                        prep_nonpe(c + 1, s_ + 1)
                    down_mm(c, s_)
                    if c + 1 < NCH:
                        prep_pe(c + 1, s_)
                    epilogue(c, s_)
            C.barrier()
    return nc


def _tables(hf):
    half = 32
    freqs = (np.float32(10000.0) ** (-np.arange(half, dtype=np.float32) * np.float32(2.0 / 64))).astype(np.float32)
    pos = (hf * TOK - HALO + np.arange(HALO + TOK)).astype(np.float32)
    ang = (pos[:, None] * freqs[None, :]).astype(np.float32).astype(np.float64)
    p = np.arange(128)
    fi = (p % 64) % 32
    sign = np.where((p % 64) < 32, -1.0, 1.0)
    cs = np.cos(ang[:, fi]).T
    sn = np.sin(ang[:, fi]).T * sign[:, None]
    return np.ascontiguousarray(cs.astype(np.float32)), np.ascontiguousarray(sn.astype(np.float32))


def _consts(hf):
    m = np.arange(128)
    partner = np.where((m % 64) < 32, m + 32, m - 32)
    perm = np.zeros((128, 128), np.float32)
    perm[partner, m] = 1.0
    k = np.arange(128)[:, None]
    q = np.arange(128)[None, :]
    mp = (k >= q).astype(np.float32)
    mc = (k <= q).astype(np.float32)
    flag = 0.0 if hf == 0 else 1.0
    mh = mp * flag
    masks = np.stack([np.concatenate([mp, mc, mp, mc], 1),
                      np.concatenate([mh, mc, mp, mc], 1),
                      np.concatenate([mh, mc, mh, mc], 1)]).astype(np.float32)
    wins = np.array([2, 4, 8, 16], np.float32)
    invwin = np.zeros((128, 2), np.float32)
    invcnt = np.zeros((128, 2, 16), np.float32)
    for t in range(2):
        for hh in range(2):
            w = wins[2 * t + hh]
            invwin[64 * hh:64 * hh + 64, t] = 1.0 / w
            posj = hf * TOK + np.arange(16)
            invcnt[64 * hh:64 * hh + 64, t, :] = (1.0 / np.minimum(posj + 1, w)).astype(np.float32)[None, :]
    return perm, masks, invwin, invcnt


_NC_CACHE = {}


def kernel(x, ln_pre_mix, w_in, w_pool, pool_scale, w_out, ln_post_mix, ln_pre_ffn, w_gate, w_up, w_down,
           ln_post_ffn):
    x = np.asarray(x, np.float32)
    f = lambda a: np.ascontiguousarray(np.asarray(a, np.float32))
    w_in0, w_out0, wg0, wu0, wd0 = f(w_in[0]), f(w_out[0]), f(w_gate[0]), f(w_up[0]), f(w_down[0])
    wp = np.asarray(w_pool[0], np.float32)
    wpbd = np.zeros((2, 128, 128), np.float32)
    for t in range(2):
        for hh in range(2):
            wpbd[t, 64 * hh:64 * hh + 64, 64 * hh:64 * hh + 64] = wp[2 * t + hh]
    gb = np.stack([np.broadcast_to(np.asarray(g[0], np.float32)[None, :], (128, D))
                   for g in (ln_pre_mix, ln_post_mix, ln_pre_ffn, ln_post_ffn)]).astype(np.float32)
    gb = np.ascontiguousarray(gb)
    ps_ = np.asarray(pool_scale[0], np.float32)
    pscale = np.ascontiguousarray(np.stack([ps_[0:128], ps_[128:256]], 1))
    in_maps = []
    for c in range(NCORES):
        b, hf = c // 2, c % 2
        xe = np.zeros((HALO + TOK, D), np.float32)
        xe[HALO:] = x[b, hf * TOK:(hf + 1) * TOK]
        if hf == 1:
            xe[:HALO] = x[b, TOK - HALO:TOK]
        cs, sn = _tables(hf)
        perm, masks, invwin, invcnt = _consts(hf)
        in_maps.append({"x": xe, "cs": cs, "sn": sn, "w_in": w_in0, "w_out": w_out0, "w_gate": wg0, "w_up": wu0,
                        "w_down": wd0, "wpool_bd": wpbd, "gb": gb, "pscale": pscale, "perm": perm, "masks": masks,
                        "invwin": invwin, "invcnt": invcnt})
    if "nc" not in _NC_CACHE:
        _NC_CACHE["nc"] = build_program()
    nc = _NC_CACHE["nc"]
    res = run_bass_kernel_spmd(nc, in_maps, core_ids=list(range(NCORES)))
    out = np.zeros((4, SEQ, D), np.float32)
    for c in range(NCORES):
        b, hf = c // 2, c % 2
        out[b, hf * TOK:(hf + 1) * TOK] = res.results[c]["out"]
    return out
```

```python
import os
from contextlib import ExitStack
import numpy as np
import concourse.bass as bass
import concourse.mybir as mybir
from concourse.bass_utils import run_bass_kernel_spmd

F32 = mybir.dt.float32
BF16 = mybir.dt.bfloat16
AF = mybir.ActivationFunctionType
ALU = mybir.AluOpType

D = 1024
SEQ = 8192
TOK = 4096
HALO = 2048
BLK = 2048
DFF = 2816
NFT = DFF // 128
EPS = 1e-6
NCORES = 8


class Sem:
    def __init__(self, h, name):
        self.h = h
        self.name = name
        self.count = 0


class Obj:
    __slots__ = ("name", "w", "r")

    def __init__(self, name):
        self.name = name
        self.w = {}
        self.r = {}


class Ctx:
    def __init__(self, nc, es):
        self.nc = nc
        self.es = es
        self.eng = {}
        for n, h in (("pe", nc.tensor), ("act", nc.scalar), ("dve", nc.vector), ("pool", nc.gpsimd), ("sp", nc.sync)):
            s = Sem(es.enter_context(nc.semaphore("sem_" + n)), "sem_" + n)
            self.eng[n] = (h, s, {})
        self.dsems = []
        self._ds = {}
        self.nwaits = 0

    def dsem(self, name):
        s = Sem(self.es.enter_context(self.nc.semaphore("dsem_" + name)), "dsem_" + name)
        self.dsems.append(s)
        return s

    def _wait(self, en, sem, val):
        h, _, waited = self.eng[en]
        if waited.get(sem.name, 0) >= val:
            return
        assert val <= sem.count, f"wait on unsignaled {sem.name} {val}>{sem.count} from {en}"
        h.wait_ge(sem.h, val)
        waited[sem.name] = val
        self.nwaits += 1

    def _deps(self, en, reads, writes, is_dma, ds=None):
        _, own, _ = self.eng[en]
        deps = {}

        def need(s, v, raw, waw=False):
            if s is own and not is_dma:
                if en == "pe":
                    return
            if waw and s is ds:
                return
            if deps.get(s.name, (None, 0))[1] < v:
                deps[s.name] = (s, v)

        for o in reads:
            for s, v in o.w.values():
                need(s, v, True)
        for o in writes:
            for s, v in o.w.values():
                need(s, v, False, True)
            for s, v in o.r.values():
                need(s, v, False)
        for s, v in deps.values():
            self._wait(en, s, v)

    def op(self, en, fn, reads=(), writes=(), signal=True):
        h, own, _ = self.eng[en]
        writes = list(writes) + [o for o in reads if o.name.startswith("bank") and o not in writes]
        self._deps(en, reads, writes, False)
        inst = fn(h)
        if signal:
            inst.then_inc(own.h, 1)
            own.count += 1
            val = own.count
        else:
            val = own.count + 1
        for o in reads:
            o.r[own.name] = (own, val)
        for o in writes:
            o.w[own.name] = (own, val)
        return inst

    def ds_of(self, obj):
        if obj.name not in self._ds:
            self._ds[obj.name] = self.dsem(obj.name)
        return self._ds[obj.name]

    def dma(self, en, out, in_, reads, writes, key):
        h, own, _ = self.eng[en]
        ds = key if isinstance(key, Sem) else self.ds_of(key)
        self._deps(en, reads, writes, True, ds)
        inst = h.dma_start(out=out, in_=in_)
        inst.then_inc(ds.h, 16)
        ds.count += 16
        for o in reads:
            o.r[ds.name] = (ds, ds.count)
        for o in writes:
            o.w[ds.name] = (ds, ds.count)
        return inst

    def barrier(self):
        sems = [e[1] for e in self.eng.values()] + self.dsems
        for en in self.eng:
            for s in sems:
                if s is self.eng[en][1]:
                    continue
                if s.count > 0:
                    self._wait(en, s, s.count)


def build_program(dbg=None, dbg_blk=0):
    nc = bass.Bass("TRN2", target_bir_lowering=False)
    dt_in = lambda n, s: nc.dram_tensor(n, list(s), F32, kind="ExternalInput").ap()
    x_d = dt_in("x", (HALO + TOK, D))
    cs_d = dt_in("cs", (128, HALO + TOK))
    sn_d = dt_in("sn", (128, HALO + TOK))
    win_d = dt_in("w_in", (D, 2560))
    wout_d = dt_in("w_out", (D, D))
    wg_d = dt_in("w_gate", (D, DFF))
    wu_d = dt_in("w_up", (D, DFF))
    wd_d = dt_in("w_down", (DFF, D))
    wpbd_d = dt_in("wpool_bd", (2, 128, 128))
    gb_d = dt_in("gb", (4, 128, D))
    psc_d = dt_in("pscale", (128, 2))
    perm_d = dt_in("perm", (128, 128))
    mask_d = dt_in("masks", (3, 128, 512))
    invw_d = dt_in("invwin", (128, 2))
    invc_d = dt_in("invcnt", (128, 2, 16))
    out_d = nc.dram_tensor("out", [TOK, D], F32, kind="ExternalOutput").ap()
    dbg_d = {}
    if dbg:
        for n, (s, dty) in dbg.items():
            dbg_d[n] = nc.dram_tensor("dbg_" + n, list(s), dty, kind="ExternalOutput").ap()

    with ExitStack() as es:
        C = Ctx(nc, es)
        op, dma = C.op, C.dma
        sb = lambda n, s, d: es.enter_context(nc.sbuf_tensor(n, list(s), d))
        ps = es.enter_context(nc.psum_tensor("ps", [128, 4096], F32))
        bank = [ps[:, 512 * i:512 * (i + 1)] for i in range(8)]
        bankbf = [ps[:, 512 * i:512 * (i + 1)].bitcast(BF16) for i in range(8)]
        bobj = [Obj(f"bank{i}") for i in range(8)]
        bank2 = [ps[:, 1024 * i:1024 * (i + 1)] for i in range(4)]

        ds_const = C.dsem("const")
        o_const = Obj("const")
        ident = sb("ident", (128, 128), BF16)
        mhalf = sb("mhalf", (128, 1), F32)
        ssb = sb("ssb", (128, 160), F32)
        tsb = sb("tsb", (128, 160), F32)
        rsb = sb("rsb", (128, 160), F32)
        o_ss = Obj("ss")
        p1 = ExitStack()
        sbc = lambda n, s, d: p1.enter_context(nc.sbuf_tensor(n, list(s), d))
        ident_f = sbc("ident_f", (128, 128), F32)
        perm = sbc("perm_sb", (128, 128), BF16)
        masks = sbc("masks_sb", (128, 3, 512), BF16)
        wpbd = sbc("wpbd_sb", (128, 2, 128), BF16)
        psc = sbc("psc_sb", (128, 2), F32)
        invw = sbc("invw_sb", (128, 2), F32)
        invc = sbc("invc_sb", (128, 2, 16), F32)
        dma("pool", perm[:], perm_d, [], [o_const], ds_const)
        dma("pool", masks[:], mask_d.rearrange("v p c -> p v c"), [], [o_const], ds_const)
        dma("pool", wpbd[:], wpbd_d.rearrange("t p c -> p t c"), [], [o_const], ds_const)
        ds_const2 = C.dsem("const2")
        dma("sp", psc[:], psc_d, [], [o_const], ds_const2)
        dma("sp", invw[:], invw_d, [], [o_const], ds_const2)
        dma("sp", invc[:], invc_d, [], [o_const], ds_const2)
        o_id = Obj("ident")
        op("pool", lambda e: e.memset(ident_f[:], 1.0), [], [o_id])
        op("pool", lambda e: e.affine_select(out=ident_f[:], in_=ident_f[:], pattern=[[-1, 128]],
                                             compare_op=ALU.is_equal, fill=0.0, base=0, channel_multiplier=1),
           [o_id], [o_id])
        op("dve", lambda e: e.tensor_copy(out=ident[:], in_=ident_f[:]), [o_id], [o_id])
        op("dve", lambda e: e.memset(mhalf[:], -0.5), [], [o_const])
        op("dve", lambda e: e.memset(ssb[:], 0.0), [], [o_ss])

        sscol = [0]

        def rstd_from(psrc_ap_fn, junk_ap, src_objs, junk_obj, n_feat=D):
            c = sscol[0]
            sscol[0] += 1
            assert c < 160
            oc = Obj(f"rs{sscol[0]}")
            op("act", lambda e: e.activation(out=junk_ap, in_=psrc_ap_fn(), func=AF.Square,
                                             accum_out=ssb[:, c:c + 1]),
               list(src_objs) + [o_ss], [junk_obj, oc])
            op("dve", lambda e: e.tensor_scalar(out=tsb[:, c:c + 1], in0=ssb[:, c:c + 1], scalar1=1.0 / n_feat,
                                                scalar2=EPS, op0=ALU.mult, op1=ALU.add), [oc], [oc])
            op("pool", lambda e: e.tensor_tensor(out=rsb[:, c:c + 1], in0=tsb[:, c:c + 1], in1=mhalf[:],
                                                 op=ALU.pow), [oc, o_const], [oc])
            return rsb[:, c:c + 1], oc

        def rstd_multi(items, n_feat=D):
            n = len(items)
            c0 = sscol[0]
            sscol[0] += n
            assert sscol[0] <= 160
            oc = Obj(f"rsm{c0}")
            for j, (src, junk_ap, src_objs, junk_obj) in enumerate(items):
                op("act", lambda e, src=src, junk_ap=junk_ap, j=j: e.activation(
                    out=junk_ap, in_=src, func=AF.Square, accum_out=ssb[:, c0 + j:c0 + j + 1]),
                   list(src_objs) + [o_ss], [junk_obj, oc])
            op("dve", lambda e: e.tensor_scalar(out=tsb[:, c0:c0 + n], in0=ssb[:, c0:c0 + n], scalar1=1.0 / n_feat,
                                                scalar2=EPS, op0=ALU.mult, op1=ALU.add), [oc], [oc])
            op("pool", lambda e: e.tensor_tensor(out=rsb[:, c0:c0 + n], in0=tsb[:, c0:c0 + n],
                                                 in1=mhalf[:].to_broadcast([128, n]), op=ALU.pow),
               [oc, o_const], [oc])
            return [rsb[:, c0 + j:c0 + j + 1] for j in range(n)], oc

        with p1:
            sb1 = lambda n, s, d: p1.enter_context(nc.sbuf_tensor(n, list(s), d))
            xT = sb1("xT", (128, 2, 8, BLK), BF16)
            OT = sb1("OT", (128, 8, BLK), BF16)
            cst = sb1("cst", (128, 2 * BLK), F32)
            snt = sb1("snt", (128, 2 * BLK), F32)
            kT = sb1("kT", (128, 2, BLK), BF16)
            vT = sb1("vT", (128, 2, BLK), BF16)
            qT = sb1("qT", (128, BLK), BF16)
            wg_in = sb1("wg_in", (128, 8, 384), BF16)
            scr = sb1("scr", (128, 10240), F32)
            rbf = sb1("rbf", (128, 2, 512), BF16)
            PT = sb1("PT", (128, 2, 2, 512), BF16)
            acc = scr[:, 0:4096].rearrange("p (h t) -> p h t", h=2)
            Vaug = scr[:, 4096:8192].bitcast(BF16).rearrange("p (s c) -> p s c", s=32)
            Rn = scr[:, 9216:10240]
            t1b = [scr[:, 8192:8704], scr[:, 9216:9728]]
            t2b = [scr[:, 8704:9216], scr[:, 9728:10240]]
            NXS = 5
            xs = [scr[:, 1024 * i:1024 * (i + 1)] for i in range(NXS)]
            NXN = 2
            xn = [scr[:, 5120 + 512 * i:5120 + 512 * (i + 1)].bitcast(BF16) for i in range(NXN)]
            gt = qT[:].bitcast(F32)
            gt2 = scr[:, 6144:7168]
            xs2 = [scr[:, 7168 + 1024 * i:7168 + 1024 * (i + 1)] for i in range(2)]
            ys = [scr[:, 9216:10240]] * 2
            ub = scr[:, 6144:7200].rearrange("p (t c) -> p t c", t=2)
            lvb = [scr[:, 7200 + 1056 * i:7200 + 1056 * (i + 1)].rearrange("p (t c) -> p t c", t=2) for i in range(2)]
            dTb = scr[:, 9312:9824].bitcast(BF16).rearrange("p (t c) -> p t c", t=2)
            ptmp = scr[:, 9824:9888].rearrange("p (t c) -> p t c", t=2)
            wpl_in = PT[:].rearrange("p a b c -> p (a b c)").rearrange("p (k c) -> p k c", k=8)

            o_scr = Obj("scr")
            o_xT = [[Obj(f"xT{s}_{c}") for c in range(4)] for s in range(2)]
            o_OT = [Obj(f"OT{k}") for k in range(8)]
            o_tab = Obj("tab")
            o_k, o_v, o_q = Obj("kT"), Obj("vT"), Obj("qT")
            o_wg, o_wpl = Obj("wg_in"), Obj("wpl_in")
            o_acc, o_V, o_R = Obj("acc"), Obj("Vaug"), Obj("Rn")
            o_t1 = [Obj("t1k"), Obj("t1q")]
            o_t2 = [Obj("t2k"), Obj("t2q")]
            o_rbf = [Obj("rbf0"), Obj("rbf1")]
            o_PT = [[Obj(f"PT{s}{h}") for h in range(2)] for s in range(2)]
            ds_tab = C.dsem("tab")
            o_out = [Obj(f"out{t}") for t in range(TOK // 128)]

            def stage_fence(reads_scr=False):
                pass

            o_g = Obj("gt")
            o_xs_l = [Obj(f"xs{i}") for i in range(NXS)]
            o_xn_l = [Obj(f"xn{i}") for i in range(NXN)]

            xnext = [0, 0]

            def stage_x(b, prefetched=False, only_prefetch=False, after_group=None):
                jobs = [(0, 0), (BLK, 1)] if b == 0 else [(2 * BLK, 0)]
                tiles = [(e0 + tt * 128, slot, tt) for e0, slot in jobs for tt in range(BLK // 128)]
                GS = 2

                def ensure_loaded(upto):
                    while xnext[b] <= min(upto, len(tiles) - 1):
                        r0 = tiles[xnext[b]][0]
                        i = xnext[b] % NXS
                        dma("sp", xs[i], x_d[r0:r0 + 128, :], [], [o_xs_l[i]], o_xs_l[i])
                        xnext[b] += 1

                if not prefetched:
                    dma("sp", gt, gb_d[0], [], [o_g, o_q], o_g)
                ensure_loaded(3)
                if only_prefetch:
                    return
                for g0 in range(0, len(tiles), GS):
                    ensure_loaded(g0 + GS - 1 + 3)
                    grp = tiles[g0:g0 + GS]
                    items = [(xs[(g0 + j) % NXS], bank2[1], [o_xs_l[(g0 + j) % NXS], bobj[2]], bobj[3])
                             for j in range(len(grp))]
                    rss, o_rs = rstd_multi(items)
                    for j, (r0, slot, tt) in enumerate(grp):
                        it = g0 + j
                        xb, o_xs = xs[it % NXS], o_xs_l[it % NXS]
                        nb, o_xn = xn[it % NXN], o_xn_l[it % NXN]
                        bk = 4 + (it % 4)
                        op("dve", lambda e, xb=xb, nb=nb, rs=rss[j]: e.scalar_tensor_tensor(
                            out=nb, in0=xb, scalar=rs, in1=gt, op0=ALU.mult, op1=ALU.mult),
                           [o_xs, o_rs, o_g], [o_xn])
                        for kt in range(8):
                            op("pe", lambda e, nb=nb, kt=kt, bk=bk: e.transpose(
                                bankbf[bk][:, kt * 128:(kt + 1) * 128], nb[:, kt * 128:(kt + 1) * 128], ident[:]),
                               [o_xn, o_id], [bobj[bk]], signal=(kt == 7))
                        ch = tt // 4
                        if it % 2 == 0:
                            op("act", lambda e, bk=bk, slot=slot, tt=tt: e.activation(
                                out=xT[:, slot, :, tt * 128:(tt + 1) * 128],
                                in_=bankbf[bk].rearrange("p (k t) -> p k t", k=8), func=AF.Copy),
                               [bobj[bk]], [o_xT[slot][ch]])
                        else:
                            op("dve", lambda e, bk=bk, slot=slot, tt=tt: e.tensor_copy(
                                out=xT[:, slot, :, tt * 128:(tt + 1) * 128],
                                in_=bankbf[bk].rearrange("p (k t) -> p k t", k=8)),
                               [bobj[bk]], [o_xT[slot][ch]])
                    if after_group is not None:
                        after_group(g0 + GS, tiles)

            def load_wg(g):
                for j, c0 in enumerate((256 + 128 * g, 1024 + 128 * g, 1792 + 128 * g)):
                    dma("pool", wg_in[:, :, 128 * j:128 * (j + 1)],
                        win_d.rearrange("(k p) c -> p k c", p=128)[:, :, c0:c0 + 128], [], [o_wg], o_wg)

            def stage_p(b, g):
                pred_slot = b % 2
                cur_slot = 1 - pred_slot
                chunks = [(half, ch) for half in range(2) for ch in range(4)]

                def pmm(wcol0, slot, ch, bk):
                    for kt in range(8):
                        op("pe", lambda e, kt=kt: e.matmul(
                            bank[bk], lhsT=wg_in[:, kt, wcol0:wcol0 + 128],
                            rhs=xT[:, slot, kt, ch * 512:(ch + 1) * 512], start=(kt == 0), stop=(kt == 7)),
                           [o_wg, o_xT[slot][ch]], [bobj[bk]], signal=(kt == 7))

                def proj_mm(ci):
                    half, ch = chunks[ci]
                    slot = pred_slot if half == 0 else cur_slot
                    base = 3 * (ci % 2)
                    pmm(128, slot, ch, base)
                    pmm(256, slot, ch, base + 1)
                    if half == 1:
                        pmm(0, slot, ch, base + 2)

                def copies(ci):
                    half, ch = chunks[ci]
                    base = 3 * (ci % 2)
                    op("act", lambda e: e.activation(out=rbf[:, 0, :], in_=bank[base], func=AF.Copy),
                       [bobj[base]], [o_rbf[0]])
                    if half == 1:
                        op("act", lambda e: e.activation(out=rbf[:, 1, :], in_=bank[base + 2], func=AF.Copy),
                           [bobj[base + 2]], [o_rbf[1]])
                    op("act", lambda e: e.activation(out=vT[:, half, ch * 512:(ch + 1) * 512], in_=bank[base + 1],
                                                     func=AF.Copy), [bobj[base + 1]], [o_v])

                def rope_rest(ci):
                    half, ch = chunks[ci]
                    base = 3 * (ci % 2)
                    tab0 = half * BLK + ch * 512
                    jobs = [(0, base, 6, kT[:, half, ch * 512:(ch + 1) * 512], o_k)]
                    if half == 1:
                        jobs.append((1, base + 2, 7, qT[:, ch * 512:(ch + 1) * 512], o_q))
                    for (i, bk, wb, dst, dobj) in jobs:
                        op("pe", lambda e, i=i, wb=wb: e.matmul(bank[wb], lhsT=perm[:], rhs=rbf[:, i, :],
                                                                 start=True, stop=True),
                           [o_rbf[i], o_const], [bobj[wb]])
                    for (i, bk, wb, dst, dobj) in jobs:
                        op("dve", lambda e, i=i, bk=bk: e.tensor_tensor(
                            out=t1b[i], in0=bank[bk], in1=cst[:, tab0:tab0 + 512], op=ALU.mult),
                           [bobj[bk], o_tab], [o_t1[i]])
                        op("dve", lambda e, i=i, wb=wb: e.tensor_tensor(
                            out=t2b[i], in0=bank[wb], in1=snt[:, tab0:tab0 + 512], op=ALU.mult),
                           [bobj[wb], o_tab], [o_t2[i]])
                        op("pool", lambda e, i=i, dst=dst: e.tensor_tensor(out=dst, in0=t1b[i], in1=t2b[i],
                                                                            op=ALU.add),
                           [o_t1[i], o_t2[i]], [dobj])

                proj_mm(0)
                copies(0)
                for ci in range(len(chunks)):
                    if ci + 1 < len(chunks):
                        proj_mm(ci + 1)
                    rope_rest(ci)
                    if ci + 1 < len(chunks):
                        copies(ci + 1)

            actr = [0]

            def stage_a(b, g):
                first_cfg = True
                for d in (1, 4, 16):
                    nbh = 16 // d
                    span = 128 * d

                    def toks(r, n):
                        if n < 0:
                            return 0, (nbh - 1) * span + r
                        return 1, n * span + r

                    def sl(t0):
                        return slice(t0, t0 + 127 * d + 1, d)

                    klist = [(r, n) for r in range(d) for n in range(-1, nbh)]
                    vslot = {kn: i for i, kn in enumerate(klist)}
                    for j0 in range(0, len(klist), 8):
                        batch = klist[j0:j0 + 8]
                        bk = 6 + (actr[0] % 2)
                        actr[0] += 1
                        for j, (r, n) in enumerate(batch):
                            half, t0 = toks(r, n)
                            op("pe", lambda e, j=j, half=half, t0=t0, bk=bk: e.transpose(
                                bankbf[bk][:, j * 128:(j + 1) * 128], vT[:, half, sl(t0)], ident[:]),
                               [o_v, o_id], [bobj[bk]], signal=(j == len(batch) - 1))
                        nbt = len(batch)
                        dst = Vaug[:, j0:j0 + nbt, :].rearrange("p s (a c) -> p s a c", a=4)[:, :, 0:4:3, :]
                        src = bankbf[bk][:, 0:nbt * 128].rearrange("p (s h c) -> p s h c", s=nbt, h=2)
                        op("act", lambda e, dst=dst, src=src: e.activation(out=dst, in_=src, func=AF.Copy),
                           [bobj[bk]], [o_V])

                    qbs = [(r, n) for r in range(d) for n in range(nbh)]
                    pairs = [qbs[p0:p0 + 2] for p0 in range(0, len(qbs), 2)]
                    base_ctr = actr[0]
                    actr[0] += len(pairs)

                    def emit_qk(pi):
                        pair = pairs[pi]
                        s_ = (base_ctr + pi) % 2
                        halo = [b == 0 and n == 0 for (_, n) in pair]
                        mv = 0 if not any(halo) else (2 if all(halo) else 1)
                        assert not (halo[1] and not halo[0])
                        for h in range(2):
                            bk = 2 * s_ + h
                            for i, (r, n) in enumerate(pair):
                                _, tq = toks(r, n)
                                for kk, nk in enumerate((n - 1, n)):
                                    khalf, tk = toks(r, nk)
                                    col = (2 * i + kk) * 128
                                    last = (i == 1 and kk == 1)
                                    op("pe", lambda e, h=h, bk=bk, col=col, khalf=khalf, tk=tk, tq=tq: e.matmul(
                                        bank[bk][:, col:col + 128],
                                        lhsT=kT[64 * h:64 * h + 64, khalf, sl(tk)],
                                        rhs=qT[64 * h:64 * h + 64, sl(tq)], start=True, stop=True),
                                       [o_k, o_q], [bobj[bk]], signal=last)
                        for h in range(2):
                            bk = 2 * s_ + h
                            op("act", lambda e, h=h, bk=bk: e.activation(
                                out=PT[:, s_, h, :], in_=bank[bk], func=AF.Exp, scale=0.125),
                               [bobj[bk]], [o_PT[s_][h]])
                            op("dve", lambda e, h=h: e.tensor_tensor(
                                out=PT[:, s_, h, :], in0=PT[:, s_, h, :], in1=masks[:, mv, :], op=ALU.mult),
                               [o_PT[s_][h], o_const], [o_PT[s_][h]])

                    def emit_pv(pi, first):
                        pair = pairs[pi]
                        s_ = (base_ctr + pi) % 2
                        ob = 4 + s_
                        for i, (r, n) in enumerate(pair):
                            for h in range(2):
                                col = (2 * i + h) * 128
                                for kk, nk in enumerate((n - 1, n)):
                                    vs = vslot[(r, nk)]
                                    last = (i == 1 and h == 1 and kk == 1)
                                    op("pe", lambda e, vs=vs, h=h, col=col, kk=kk, i=i: e.matmul(
                                        bank[ob][:, col:col + 128],
                                        lhsT=Vaug[:, vs, 128 * h:128 * (h + 1)],
                                        rhs=PT[:, s_, h, (2 * i + kk) * 128:(2 * i + kk + 1) * 128],
                                        start=(kk == 0), stop=(kk == 1)),
                                       [o_V, o_PT[s_][h]], [bobj[ob]], signal=last)
                        for i, (r, n) in enumerate(pair):
                            _, tq = toks(r, n)
                            dst = acc[:, :, sl(tq)]
                            src = bank[ob][:, 256 * i:256 * (i + 1)].rearrange("p (h l) -> p h l", h=2)
                            if first:
                                op("act", lambda e, dst=dst, src=src: e.activation(out=dst, in_=src, func=AF.Copy),
                                   [bobj[ob]], [o_acc])
                            else:
                                op("dve", lambda e, dst=dst, src=src: e.tensor_tensor(
                                    out=dst, in0=src, in1=dst, op=ALU.add),
                                   [bobj[ob], o_acc], [o_acc])

                    emit_qk(0)
                    for pi in range(len(pairs)):
                        if pi + 1 < len(pairs):
                            emit_qk(pi + 1)
                        emit_pv(pi, first_cfg)
                    first_cfg = False
                rn_objs = [o_R, o_t1[1], o_t2[1]]
                kt = 2 + g
                for hv in range(2):
                    tsl = slice(hv * 1024, (hv + 1) * 1024)
                    op("act", lambda e: e.activation(out=Rn[0:64, :], in_=acc[64:128, 0, tsl], func=AF.Ln),
                       [o_acc], rn_objs)
                    op("act", lambda e: e.activation(out=Rn[64:128, :], in_=acc[0:64, 1, tsl], func=AF.Ln),
                       [o_acc], rn_objs)
                    op("act", lambda e: e.activation(out=Rn, in_=Rn, func=AF.Exp, scale=-1.0), rn_objs, rn_objs)
                    op("pool", lambda e: e.tensor_tensor(out=OT[0:64, kt, tsl], in0=acc[0:64, 0, tsl], in1=Rn[0:64, :],
                                                         op=ALU.mult), [o_acc] + rn_objs, [o_OT[kt]])
                    op("dve", lambda e: e.tensor_tensor(out=OT[64:128, kt, tsl], in0=acc[64:128, 1, tsl],
                                                        in1=Rn[64:128, :], op=ALU.mult), [o_acc] + rn_objs, [o_OT[kt]])

            o_ub, o_dT = Obj("ub"), Obj("dT")
            o_lvb = [Obj("lv0"), Obj("lv1")]

            def pool_begin(b):
                pred_slot = b % 2
                dma("pool", wpl_in, win_d.rearrange("(k p) c -> p k c", p=128)[:, :, 0:256],
                    [], [o_wpl, o_PT[0][0], o_PT[0][1], o_PT[1][0], o_PT[1][1]], o_wpl)
                for t in range(2):
                    bk = t
                    for kt in range(8):
                        op("pe", lambda e, kt=kt, bk=bk, t=t: e.matmul(
                            bank[bk][:, 0:16], lhsT=wpl_in[:, kt, 128 * t:128 * (t + 1)],
                            rhs=xT[:, pred_slot, kt, BLK - 16:BLK], start=(kt == 0), stop=(kt == 7)),
                           [o_wpl, o_xT[pred_slot][3]], [bobj[bk]], signal=(kt == 7))
                    op("act", lambda e, bk=bk, t=t: e.activation(out=ub[:, t, 0:16], in_=bank[bk][:, 0:16],
                                                                 func=AF.Copy), [bobj[bk]], [o_ub])

            def pool_chunk(b, ch):
                cur_slot = 1 - (b % 2)
                for t in range(2):
                    bk = t
                    for kt in range(8):
                        op("pe", lambda e, kt=kt, bk=bk, t=t: e.matmul(
                            bank[bk], lhsT=wpl_in[:, kt, 128 * t:128 * (t + 1)],
                            rhs=xT[:, cur_slot, kt, ch * 512:(ch + 1) * 512], start=(kt == 0), stop=(kt == 7)),
                           [o_wpl, o_xT[cur_slot][ch]], [bobj[bk]], signal=(kt == 7))
                    op("act", lambda e, bk=bk, t=t: e.activation(out=ub[:, t, 16:528], in_=bank[bk],
                                                                 func=AF.Copy), [bobj[bk]], [o_ub])
                prev, o_prev = ub, o_ub
                for k in range(4):
                    sh = 1 << k
                    lo = (1 << (k + 1)) - 1
                    cur, o_cur = lvb[k % 2], o_lvb[k % 2]
                    op("dve", lambda e, prev=prev, cur=cur, sh=sh, lo=lo: e.tensor_tensor(
                        out=cur[:, :, lo:528], in0=prev[:, :, lo:528], in1=prev[:, :, lo - sh:528 - sh],
                        op=ALU.add), [o_prev], [o_cur])
                    t, hh = k // 2, k % 2
                    p0 = 64 * hh
                    op("dve", lambda e, t=t, cur=cur, p0=p0: e.scalar_tensor_tensor(
                        out=dTb[p0:p0 + 64, t, :], in0=cur[p0:p0 + 64, t, 16:528],
                        scalar=invw[p0:p0 + 64, t:t + 1], in1=ub[p0:p0 + 64, t, 16:528],
                        op0=ALU.mult, op1=ALU.subtract), [o_cur, o_ub, o_const], [o_dT])
                    if b == 0 and ch == 0:
                        op("dve", lambda e, t=t, cur=cur, p0=p0: e.tensor_tensor(
                            out=ptmp[p0:p0 + 64, t, 0:16], in0=cur[p0:p0 + 64, t, 16:32],
                            in1=invc[p0:p0 + 64, t, :], op=ALU.mult), [o_cur, o_const], [o_dT])
                        op("dve", lambda e, t=t, p0=p0: e.tensor_tensor(
                            out=dTb[p0:p0 + 64, t, 0:16], in0=ptmp[p0:p0 + 64, t, 0:16],
                            in1=ub[p0:p0 + 64, t, 16:32], op=ALU.subtract), [o_dT, o_ub], [o_dT])
                    prev, o_prev = cur, o_cur
                for t in range(2):
                    bk = t
                    op("pe", lambda e, t=t, bk=bk: e.matmul(bank[bk], lhsT=wpbd[:, t, :], rhs=dTb[:, t, :],
                                                           start=True, stop=True),
                       [o_dT, o_const], [bobj[bk]])
                    op("act", lambda e, t=t, bk=bk: e.activation(
                        out=OT[:, t, ch * 512:(ch + 1) * 512], in_=bank[bk], func=AF.Copy,
                        scale=psc[:, t:t + 1]), [bobj[bk], o_const], [o_OT[t]])
                op("dve", lambda e: e.tensor_copy(out=ptmp[:, :, 16:32], in_=ub[:, :, 512:528]), [o_ub], [o_dT])
                op("act", lambda e: e.activation(out=ub[:, :, 0:16], in_=ptmp[:, :, 16:32], func=AF.Copy),
                   [o_dT], [o_ub])

            o_wo, o_g2, o_yb = Obj("wout"), Obj("gt2"), Obj("ys")

            def wout_view(b):
                return xT[:, b % 2, 0:4, :].rearrange("p k (a c) -> p (k a) c", a=2)

            def gt2_view(b):
                return xT[:, b % 2, 4, :].bitcast(F32)

            def prefetch_o(b):
                dma("pool", wout_view(b), wout_d.rearrange("(k p) c -> p k c", p=128),
                    [], [o_wo] + o_xT[b % 2], o_wo)
                dma("sp", gt2_view(b), gb_d[1], [], [o_g2] + o_xT[b % 2], o_g2)

            def stage_o(b):
                wout = wout_view(b)
                gt2 = gt2_view(b)
                o_xs2_l = [Obj("xs2_0"), Obj("xs2_1")]
                junk = bank2[3]
                o_j = bobj[7]
                for tt in range(BLK // 128):
                    st = tt % 3
                    xb = xs2[tt % 2]
                    yb = ys[0]
                    o_xb = o_xs2_l[tt % 2]
                    tok0 = b * BLK + tt * 128
                    dma("sp", xb, x_d[HALO + tok0:HALO + tok0 + 128, :], [], [o_xb], o_xb)
                    for hh in range(2):
                        bk = 2 * st + hh
                        for kt in range(8):
                            op("pe", lambda e, kt=kt, bk=bk, hh=hh, tt=tt: e.matmul(
                                bank[bk], lhsT=OT[:, kt, tt * 128:(tt + 1) * 128],
                                rhs=wout[:, kt, hh * 512:(hh + 1) * 512], start=(kt == 0), stop=(kt == 7)),
                               [o_OT[kt], o_wo], [bobj[bk]], signal=(kt == 7))
                    src = bank2[st]
                    rs, o_rs = rstd_from(lambda src=src: src, junk, [bobj[2 * st], bobj[2 * st + 1], bobj[6]], o_j)
                    op("dve", lambda e, yb=yb, src=src, rs=rs: e.scalar_tensor_tensor(
                        out=yb, in0=src, scalar=rs, in1=gt2, op0=ALU.mult, op1=ALU.mult),
                       [bobj[2 * st], bobj[2 * st + 1], o_rs, o_g2], [o_yb])
                    op("dve", lambda e, yb=yb, xb=xb: e.tensor_tensor(out=xb, in0=yb, in1=xb, op=ALU.add),
                       [o_yb, o_xb], [o_xb])
                    ti = tok0 // 128
                    dma("sp", out_d[tok0:tok0 + 128, :], xb, [o_xb], [o_out[ti]], o_xb)

            def fence():
                pass

            for b in range(2):
                dma("sp", cst[:], cs_d[:, b * BLK:b * BLK + 2 * BLK], [], [o_tab], ds_tab)
                dma("sp", snt[:], sn_d[:, b * BLK:b * BLK + 2 * BLK], [], [o_tab], ds_tab)
                C.barrier()
                load_wg(0)

                def hook(ndone, tiles, b=b):
                    npred = len(tiles) - BLK // 128
                    if ndone == npred or (npred == 0 and ndone == 2):
                        pass
                    if ndone == max(npred, 2) and not hook.started:
                        pool_begin(b)
                        hook.started = True
                    k = ndone - npred
                    if hook.started and k > 0 and k % 4 == 0:
                        while hook.next_ch < k // 4:
                            pool_chunk(b, hook.next_ch)
                            hook.next_ch += 1
                hook.started = False
                hook.next_ch = 0
                stage_x(b, prefetched=(b == 1), after_group=hook)
                C.barrier()
                op("pool", lambda e: e.memset(Vaug[:, :, 64:192], 1.0), [], [o_V])
                for g in range(6):
                    stage_p(b, g)
                    if g + 1 < 6:
                        load_wg(g + 1)
                    else:
                        prefetch_o(b)
                    stage_a(b, g)
                C.barrier()
                if dbg and b == dbg_blk:
                    ds_dbg = C.dsem("dbg")
                    od = Obj("dbg")
                    srcs = {"OT": OT[:].rearrange("p k t -> p (k t)"), "rsb": rsb[:], "kT": kT[:].rearrange("p h t -> p (h t)"),
                            "vT": vT[:].rearrange("p h t -> p (h t)"), "qT": qT[:], "acc": scr[:, 0:4096],
                            "xT": xT[:].rearrange("p s k t -> p (s k t)")}
                    for n in dbg_d:
                        dma("sp", dbg_d[n], srcs[n], [], [od], ds_dbg)
                    C.barrier()
                if b == 0:
                    stage_x(1, only_prefetch=True)
                stage_o(b)
                C.barrier()
            if dbg and "x1" in dbg:
                pass
        C.barrier()
        with ExitStack() as p2:
            sb2 = lambda n, s, d: p2.enter_context(nc.sbuf_tensor(n, list(s), d))
            wg = sb2("wg", (128, 8, DFF), BF16)
            wu = sb2("wu", (128, 8, DFF), BF16)
            wd = sb2("wd", (128, NFT, D), BF16)
            x1b = sb2("x1b", (128, 4, D), F32)
            yb2 = sb2("yb2", (128, D), F32)
            g3 = sb2("g3", (128, D), F32)
            g4 = sb2("g4", (128, D), F32)
            sg = sb2("sg", (128, 2, 512), F32)
            h2 = sb2("h2", (128, 2, D), BF16)
            h2T = sb2("h2T", (128, 8, 512), BF16)
            actT = sb2("actT", (128, NFT, 512), BF16)
            o_wgu, o_wd, o_g34 = Obj("wgu"), Obj("wd"), Obj("g34")
            FTG = [(0, 3), (3, 8), (8, 15), (15, NFT)]
            o_wgu_l = [Obj(f"wgu{i}") for i in range(len(FTG))]
            ftg_of = {}
            for gi, (f0, f1) in enumerate(FTG):
                for ft in range(f0, f1):
                    ftg_of[ft] = gi
                dma("pool", wg[:, :, f0 * 128:f1 * 128],
                    wg_d.rearrange("(k p) c -> p k c", p=128)[:, :, f0 * 128:f1 * 128], [], [o_wgu_l[gi]], o_wgu_l[gi])
                dma("pool", wu[:, :, f0 * 128:f1 * 128],
                    wu_d.rearrange("(k p) c -> p k c", p=128)[:, :, f0 * 128:f1 * 128], [], [o_wgu_l[gi]], o_wgu_l[gi])
            for ft in range(NFT):
                dma("pool", wd[:, ft, :], wd_d[ft * 128:(ft + 1) * 128, :], [], [o_wd], o_wd)
            dma("sp", g3[:], gb_d[2], [], [o_g34], o_g34)
            dma("sp", g4[:], gb_d[3], [], [o_g34], o_g34)
            o_h2T, o_act = Obj("h2T"), Obj("actT")
            o_sg = [Obj("sg0"), Obj("sg1")]
            o_h2 = [Obj("h2_0"), Obj("h2_1")]
            o_y2 = Obj("yb2")
            o_x1_l = [Obj(f"x1b{i}") for i in range(4)]
            NCH = TOK // 512

            def prep_nonpe(c, s_):
                ti = c * 4 + s_
                bi = s_ % 2
                xb, o_x1 = x1b[:, bi, :], o_x1_l[bi]
                dma("sp", xb, out_d[ti * 128:(ti + 1) * 128, :], [o_out[ti]], [o_x1], o_x1)
                hb, o_hb = h2[:, s_ % 2, :], o_h2[s_ % 2]
                rs, o_rs = rstd_from(lambda: xb, hb, [o_x1], o_hb)
                op("dve", lambda e: e.scalar_tensor_tensor(
                    out=hb, in0=xb, scalar=rs, in1=g3[:], op0=ALU.mult, op1=ALU.mult),
                   [o_x1, o_rs, o_g34], [o_hb])

            def prep_pe(c, s_):
                hb, o_hb = h2[:, s_ % 2, :], o_h2[s_ % 2]
                bk = s_ % 2
                for kt in range(8):
                    op("pe", lambda e, kt=kt: e.transpose(
                        bankbf[bk][:, kt * 128:(kt + 1) * 128], hb[:, kt * 128:(kt + 1) * 128], ident[:]),
                       [o_hb, o_id], [bobj[bk]], signal=(kt == 7))
                op("act", lambda e: e.activation(
                    out=h2T[:, :, s_ * 128:(s_ + 1) * 128],
                    in_=bankbf[bk].rearrange("p (k t) -> p k t", k=8), func=AF.Copy),
                   [bobj[bk]], [o_h2T])

            def gateup(c):
                for ft in range(NFT):
                    i = ft % 2
                    bg, bu = i, 2 + i
                    for (wt, bk) in ((wg, bg), (wu, bu)):
                        for kt in range(8):
                            op("pe", lambda e, wt=wt, bk=bk, kt=kt, ft=ft: e.matmul(
                                bank[bk], lhsT=wt[:, kt, ft * 128:(ft + 1) * 128], rhs=h2T[:, kt, :],
                                start=(kt == 0), stop=(kt == 7)),
                               [o_wgu_l[ftg_of[ft]], o_h2T], [bobj[bk]], signal=(kt == 7))
                    op("act", lambda e, i=i, bg=bg: e.activation(out=sg[:, i, :], in_=bank[bg], func=AF.Silu),
                       [bobj[bg]], [o_sg[i]])
                    op("dve", lambda e, i=i, bu=bu, ft=ft: e.tensor_tensor(
                        out=actT[:, ft, :], in0=bank[bu], in1=sg[:, i, :], op=ALU.mult),
                       [bobj[bu], o_sg[i]], [o_act])

            def epi_load(c, s_):
                ti = c * 4 + s_
                bi = 2 + s_ % 2
                dma("sp", x1b[:, bi, :], out_d[ti * 128:(ti + 1) * 128, :], [o_out[ti]], [o_x1_l[bi]], o_x1_l[bi])

            def down_mm(c, s_):
                st = 2 + (s_ % 2)
                for hh in range(2):
                    bk = 2 * st + hh
                    for ft in range(NFT):
                        op("pe", lambda e, ft=ft, bk=bk, hh=hh: e.matmul(
                            bank[bk], lhsT=actT[:, ft, s_ * 128:(s_ + 1) * 128],
                            rhs=wd[:, ft, hh * 512:(hh + 1) * 512], start=(ft == 0), stop=(ft == NFT - 1)),
                           [o_act, o_wd], [bobj[bk]], signal=(ft == NFT - 1))

            def epilogue(c, s_):
                ti = c * 4 + s_
                bi = 2 + s_ % 2
                xb, o_x1 = x1b[:, bi, :], o_x1_l[bi]
                st = 2 + (s_ % 2)
                src = bank2[st]
                rs, o_rs = rstd_from(lambda: src, yb2[:], [bobj[2 * st], bobj[2 * st + 1]], o_y2)
                op("dve", lambda e: e.scalar_tensor_tensor(
                    out=yb2[:], in0=src, scalar=rs, in1=g4[:], op0=ALU.mult, op1=ALU.mult),
                   [bobj[2 * st], bobj[2 * st + 1], o_rs, o_g34], [o_y2])
                op("pool", lambda e: e.tensor_tensor(out=xb, in0=yb2[:], in1=xb, op=ALU.add),
                   [o_y2, o_x1], [o_x1])
                dma("sp", out_d[ti * 128:(ti + 1) * 128, :], xb, [o_x1], [o_out[ti]], o_x1)

            for s_ in range(4):
                prep_nonpe(0, s_)
                prep_pe(0, s_)
            for c in range(NCH):
                if c + 1 < NCH:
                    prep_nonpe(c + 1, 0)
                gateup(c)
                for s_ in range(4):
                    epi_load(c, s_)
                    if c + 1 < NCH and s_ + 1 < 4:
                        prep_nonpe(c + 1, s_ + 1)
                    down_mm(c, s_)
                    if c + 1 < NCH:
                        prep_pe(c + 1, s_)
                    epilogue(c, s_)
            C.barrier()
    return nc


def _tables(hf):
    half = 32
    freqs = (np.float32(10000.0) ** (-np.arange(half, dtype=np.float32) * np.float32(2.0 / 64))).astype(np.float32)
    pos = (hf * TOK - HALO + np.arange(HALO + TOK)).astype(np.float32)
    ang = (pos[:, None] * freqs[None, :]).astype(np.float32).astype(np.float64)
    p = np.arange(128)
    fi = (p % 64) % 32
    sign = np.where((p % 64) < 32, -1.0, 1.0)
    cs = np.cos(ang[:, fi]).T
    sn = np.sin(ang[:, fi]).T * sign[:, None]
    return np.ascontiguousarray(cs.astype(np.float32)), np.ascontiguousarray(sn.astype(np.float32))


def _consts(hf):
    m = np.arange(128)
    partner = np.where((m % 64) < 32, m + 32, m - 32)
    perm = np.zeros((128, 128), np.float32)
    perm[partner, m] = 1.0
    k = np.arange(128)[:, None]
    q = np.arange(128)[None, :]
    mp = (k >= q).astype(np.float32)
    mc = (k <= q).astype(np.float32)
    flag = 0.0 if hf == 0 else 1.0
    mh = mp * flag
    masks = np.stack([np.concatenate([mp, mc, mp, mc], 1),
                      np.concatenate([mh, mc, mp, mc], 1),
                      np.concatenate([mh, mc, mh, mc], 1)]).astype(np.float32)
    wins = np.array([2, 4, 8, 16], np.float32)
    invwin = np.zeros((128, 2), np.float32)
    invcnt = np.zeros((128, 2, 16), np.float32)
    for t in range(2):
        for hh in range(2):
            w = wins[2 * t + hh]
            invwin[64 * hh:64 * hh + 64, t] = 1.0 / w
            posj = hf * TOK + np.arange(16)
            invcnt[64 * hh:64 * hh + 64, t, :] = (1.0 / np.minimum(posj + 1, w)).astype(np.float32)[None, :]
    return perm, masks, invwin, invcnt


_NC_CACHE = {}


def kernel(x, ln_pre_mix, w_in, w_pool, pool_scale, w_out, ln_post_mix, ln_pre_ffn, w_gate, w_up, w_down,
           ln_post_ffn):
    x = np.asarray(x, np.float32)
    f = lambda a: np.ascontiguousarray(np.asarray(a, np.float32))
    w_in0, w_out0, wg0, wu0, wd0 = f(w_in[0]), f(w_out[0]), f(w_gate[0]), f(w_up[0]), f(w_down[0])
    wp = np.asarray(w_pool[0], np.float32)
    wpbd = np.zeros((2, 128, 128), np.float32)
    for t in range(2):
        for hh in range(2):
            wpbd[t, 64 * hh:64 * hh + 64, 64 * hh:64 * hh + 64] = wp[2 * t + hh]
    gb = np.stack([np.broadcast_to(np.asarray(g[0], np.float32)[None, :], (128, D))
                   for g in (ln_pre_mix, ln_post_mix, ln_pre_ffn, ln_post_ffn)]).astype(np.float32)
    gb = np.ascontiguousarray(gb)
    ps_ = np.asarray(pool_scale[0], np.float32)
    pscale = np.ascontiguousarray(np.stack([ps_[0:128], ps_[128:256]], 1))
    in_maps = []
    for c in range(NCORES):
        b, hf = c // 2, c % 2
        xe = np.zeros((HALO + TOK, D), np.float32)
        xe[HALO:] = x[b, hf * TOK:(hf + 1) * TOK]
        if hf == 1:
            xe[:HALO] = x[b, TOK - HALO:TOK]
        cs, sn = _tables(hf)
        perm, masks, invwin, invcnt = _consts(hf)
        in_maps.append({"x": xe, "cs": cs, "sn": sn, "w_in": w_in0, "w_out": w_out0, "w_gate": wg0, "w_up": wu0,
                        "w_down": wd0, "wpool_bd": wpbd, "gb": gb, "pscale": pscale, "perm": perm, "masks": masks,
                        "invwin": invwin, "invcnt": invcnt})
    if "nc" not in _NC_CACHE:
        _NC_CACHE["nc"] = build_program()
    nc = _NC_CACHE["nc"]
    res = run_bass_kernel_spmd(nc, in_maps, core_ids=list(range(NCORES)))
    out = np.zeros((4, SEQ, D), np.float32)
    for c in range(NCORES):
        b, hf = c // 2, c % 2
        out[b, hf * TOK:(hf + 1) * TOK] = res.results[c]["out"]
    return out
```

```python
import os
from contextlib import ExitStack
import numpy as np
import concourse.bass as bass
import concourse.mybir as mybir
from concourse.bass_utils import run_bass_kernel_spmd

F32 = mybir.dt.float32
BF16 = mybir.dt.bfloat16
AF = mybir.ActivationFunctionType
ALU = mybir.AluOpType

D = 1024
SEQ = 8192
TOK = 4096
HALO = 2048
BLK = 2048
DFF = 2816
NFT = DFF // 128
EPS = 1e-6
NCORES = 8


class Sem:
    def __init__(self, h, name):
        self.h = h
        self.name = name
        self.count = 0


class Obj:
    __slots__ = ("name", "w", "r")

    def __init__(self, name):
        self.name = name
        self.w = {}
        self.r = {}


class Ctx:
    def __init__(self, nc, es):
        self.nc = nc
        self.es = es
        self.eng = {}
        for n, h in (("pe", nc.tensor), ("act", nc.scalar), ("dve", nc.vector), ("pool", nc.gpsimd), ("sp", nc.sync)):
            s = Sem(es.enter_context(nc.semaphore("sem_" + n)), "sem_" + n)
            self.eng[n] = (h, s, {})
        self.dsems = []
        self._ds = {}
        self.nwaits = 0

    def dsem(self, name):
        s = Sem(self.es.enter_context(self.nc.semaphore("dsem_" + name)), "dsem_" + name)
        self.dsems.append(s)
        return s

    def _wait(self, en, sem, val):
        h, _, waited = self.eng[en]
        if waited.get(sem.name, 0) >= val:
            return
        assert val <= sem.count, f"wait on unsignaled {sem.name} {val}>{sem.count} from {en}"
        h.wait_ge(sem.h, val)
        waited[sem.name] = val
        self.nwaits += 1

    def _deps(self, en, reads, writes, is_dma, ds=None):
        _, own, _ = self.eng[en]
        deps = {}

        def need(s, v, raw, waw=False):
            if s is own and not is_dma:
                if en == "pe":
                    return
            if waw and s is ds:
                return
            if deps.get(s.name, (None, 0))[1] < v:
                deps[s.name] = (s, v)

        for o in reads:
            for s, v in o.w.values():
                need(s, v, True)
        for o in writes:
            for s, v in o.w.values():
                need(s, v, False, True)
            for s, v in o.r.values():
                need(s, v, False)
        for s, v in deps.values():
            self._wait(en, s, v)

    def op(self, en, fn, reads=(), writes=(), signal=True):
        h, own, _ = self.eng[en]
        writes = list(writes) + [o for o in reads if o.name.startswith("bank") and o not in writes]
        self._deps(en, reads, writes, False)
        inst = fn(h)
        if signal:
            inst.then_inc(own.h, 1)
            own.count += 1
            val = own.count
        else:
            val = own.count + 1
        for o in reads:
            o.r[own.name] = (own, val)
        for o in writes:
            o.w[own.name] = (own, val)
        return inst

    def ds_of(self, obj):
        if obj.name not in self._ds:
            self._ds[obj.name] = self.dsem(obj.name)
        return self._ds[obj.name]

    def dma(self, en, out, in_, reads, writes, key):
        h, own, _ = self.eng[en]
        ds = key if isinstance(key, Sem) else self.ds_of(key)
        self._deps(en, reads, writes, True, ds)
        inst = h.dma_start(out=out, in_=in_)
        inst.then_inc(ds.h, 16)
        ds.count += 16
        for o in reads:
            o.r[ds.name] = (ds, ds.count)
        for o in writes:
            o.w[ds.name] = (ds, ds.count)
        return inst

    def barrier(self):
        sems = [e[1] for e in self.eng.values()] + self.dsems
        for en in self.eng:
            for s in sems:
                if s is self.eng[en][1]:
                    continue
                if s.count > 0:
                    self._wait(en, s, s.count)


def build_program(dbg=None, dbg_blk=0):
    nc = bass.Bass("TRN2", target_bir_lowering=False)
    dt_in = lambda n, s: nc.dram_tensor(n, list(s), F32, kind="ExternalInput").ap()
    x_d = dt_in("x", (HALO + TOK, D))
    cs_d = dt_in("cs", (128, HALO + TOK))
    sn_d = dt_in("sn", (128, HALO + TOK))
    win_d = dt_in("w_in", (D, 2560))
    wout_d = dt_in("w_out", (D, D))
    wg_d = dt_in("w_gate", (D, DFF))
    wu_d = dt_in("w_up", (D, DFF))
    wd_d = dt_in("w_down", (DFF, D))
    wpbd_d = dt_in("wpool_bd", (2, 128, 128))
    gb_d = dt_in("gb", (4, 128, D))
    psc_d = dt_in("pscale", (128, 2))
    perm_d = dt_in("perm", (128, 128))
    mask_d = dt_in("masks", (3, 128, 512))
    invw_d = dt_in("invwin", (128, 2))
    invc_d = dt_in("invcnt", (128, 2, 16))
    out_d = nc.dram_tensor("out", [TOK, D], F32, kind="ExternalOutput").ap()
    dbg_d = {}
    if dbg:
        for n, (s, dty) in dbg.items():
            dbg_d[n] = nc.dram_tensor("dbg_" + n, list(s), dty, kind="ExternalOutput").ap()

    with ExitStack() as es:
        C = Ctx(nc, es)
        op, dma = C.op, C.dma
        sb = lambda n, s, d: es.enter_context(nc.sbuf_tensor(n, list(s), d))
        ps = es.enter_context(nc.psum_tensor("ps", [128, 4096], F32))
        bank = [ps[:, 512 * i:512 * (i + 1)] for i in range(8)]
        bankbf = [ps[:, 512 * i:512 * (i + 1)].bitcast(BF16) for i in range(8)]
        bobj = [Obj(f"bank{i}") for i in range(8)]
        bank2 = [ps[:, 1024 * i:1024 * (i + 1)] for i in range(4)]

        ds_const = C.dsem("const")
        o_const = Obj("const")
        ident = sb("ident", (128, 128), BF16)
        mhalf = sb("mhalf", (128, 1), F32)
        ssb = sb("ssb", (128, 160), F32)
        tsb = sb("tsb", (128, 160), F32)
        rsb = sb("rsb", (128, 160), F32)
        o_ss = Obj("ss")
        p1 = ExitStack()
        sbc = lambda n, s, d: p1.enter_context(nc.sbuf_tensor(n, list(s), d))
        ident_f = sbc("ident_f", (128, 128), F32)
        perm = sbc("perm_sb", (128, 128), BF16)
        masks = sbc("masks_sb", (128, 3, 512), BF16)
        wpbd = sbc("wpbd_sb", (128, 2, 128), BF16)
        psc = sbc("psc_sb", (128, 2), F32)
        invw = sbc("invw_sb", (128, 2), F32)
        invc = sbc("invc_sb", (128, 2, 16), F32)
        dma("pool", perm[:], perm_d, [], [o_const], ds_const)
        dma("pool", masks[:], mask_d.rearrange("v p c -> p v c"), [], [o_const], ds_const)
        dma("pool", wpbd[:], wpbd_d.rearrange("t p c -> p t c"), [], [o_const], ds_const)
        ds_const2 = C.dsem("const2")
        dma("sp", psc[:], psc_d, [], [o_const], ds_const2)
        dma("sp", invw[:], invw_d, [], [o_const], ds_const2)
        dma("sp", invc[:], invc_d, [], [o_const], ds_const2)
        o_id = Obj("ident")
        op("pool", lambda e: e.memset(ident_f[:], 1.0), [], [o_id])
        op("pool", lambda e: e.affine_select(out=ident_f[:], in_=ident_f[:], pattern=[[-1, 128]],
                                             compare_op=ALU.is_equal, fill=0.0, base=0, channel_multiplier=1),
           [o_id], [o_id])
        op("dve", lambda e: e.tensor_copy(out=ident[:], in_=ident_f[:]), [o_id], [o_id])
        op("dve", lambda e: e.memset(mhalf[:], -0.5), [], [o_const])
        op("dve", lambda e: e.memset(ssb[:], 0.0), [], [o_ss])

        sscol = [0]

        def rstd_from(psrc_ap_fn, junk_ap, src_objs, junk_obj, n_feat=D):
            c = sscol[0]
            sscol[0] += 1
            assert c < 160
            oc = Obj(f"rs{sscol[0]}")
            op("act", lambda e: e.activation(out=junk_ap, in_=psrc_ap_fn(), func=AF.Square,
                                             accum_out=ssb[:, c:c + 1]),
               list(src_objs) + [o_ss], [junk_obj, oc])
            op("dve", lambda e: e.tensor_scalar(out=tsb[:, c:c + 1], in0=ssb[:, c:c + 1], scalar1=1.0 / n_feat,
                                                scalar2=EPS, op0=ALU.mult, op1=ALU.add), [oc], [oc])
            op("pool", lambda e: e.tensor_tensor(out=rsb[:, c:c + 1], in0=tsb[:, c:c + 1], in1=mhalf[:],
                                                 op=ALU.pow), [oc, o_const], [oc])
            return rsb[:, c:c + 1], oc

        def rstd_multi(items, n_feat=D):
            n = len(items)
            c0 = sscol[0]
            sscol[0] += n
            assert sscol[0] <= 160
            oc = Obj(f"rsm{c0}")
            for j, (src, junk_ap, src_objs, junk_obj) in enumerate(items):
                op("act", lambda e, src=src, junk_ap=junk_ap, j=j: e.activation(
                    out=junk_ap, in_=src, func=AF.Square, accum_out=ssb[:, c0 + j:c0 + j + 1]),
                   list(src_objs) + [o_ss], [junk_obj, oc])
            op("dve", lambda e: e.tensor_scalar(out=tsb[:, c0:c0 + n], in0=ssb[:, c0:c0 + n], scalar1=1.0 / n_feat,
                                                scalar2=EPS, op0=ALU.mult, op1=ALU.add), [oc], [oc])
            op("pool", lambda e: e.tensor_tensor(out=rsb[:, c0:c0 + n], in0=tsb[:, c0:c0 + n],
                                                 in1=mhalf[:].to_broadcast([128, n]), op=ALU.pow),
               [oc, o_const], [oc])
            return [rsb[:, c0 + j:c0 + j + 1] for j in range(n)], oc

        with p1:
            sb1 = lambda n, s, d: p1.enter_context(nc.sbuf_tensor(n, list(s), d))
            xT = sb1("xT", (128, 2, 8, BLK), BF16)
            OT = sb1("OT", (128, 8, BLK), BF16)
            cst = sb1("cst", (128, 2 * BLK), F32)
            snt = sb1("snt", (128, 2 * BLK), F32)
            kT = sb1("kT", (128, 2, BLK), BF16)
            vT = sb1("vT", (128, 2, BLK), BF16)
            qT = sb1("qT", (128, BLK), BF16)
            wg_in = sb1("wg_in", (128, 8, 384), BF16)
            scr = sb1("scr", (128, 10240), F32)
            rbf = sb1("rbf", (128, 2, 512), BF16)
            PT = sb1("PT", (128, 2, 2, 512), BF16)
            acc = scr[:, 0:4096].rearrange("p (h t) -> p h t", h=2)
            Vaug = scr[:, 4096:8192].bitcast(BF16).rearrange("p (s c) -> p s c", s=32)
            Rn = scr[:, 9216:10240]
            t1b = [scr[:, 8192:8704], scr[:, 9216:9728]]
            t2b = [scr[:, 8704:9216], scr[:, 9728:10240]]
            NXS = 5
            xs = [scr[:, 1024 * i:1024 * (i + 1)] for i in range(NXS)]
            NXN = 2
            xn = [scr[:, 5120 + 512 * i:5120 + 512 * (i + 1)].bitcast(BF16) for i in range(NXN)]
            gt = qT[:].bitcast(F32)
            gt2 = scr[:, 6144:7168]
            xs2 = [scr[:, 7168 + 1024 * i:7168 + 1024 * (i + 1)] for i in range(2)]
            ys = [scr[:, 9216:10240]] * 2
            ub = scr[:, 6144:7200].rearrange("p (t c) -> p t c", t=2)
            lvb = [scr[:, 7200 + 1056 * i:7200 + 1056 * (i + 1)].rearrange("p (t c) -> p t c", t=2) for i in range(2)]
            dTb = scr[:, 9312:9824].bitcast(BF16).rearrange("p (t c) -> p t c", t=2)
            ptmp = scr[:, 9824:9888].rearrange("p (t c) -> p t c", t=2)
            wpl_in = PT[:].rearrange("p a b c -> p (a b c)").rearrange("p (k c) -> p k c", k=8)

            o_scr = Obj("scr")
            o_xT = [[Obj(f"xT{s}_{c}") for c in range(4)] for s in range(2)]
            o_OT = [Obj(f"OT{k}") for k in range(8)]
            o_tab = Obj("tab")
            o_k, o_v, o_q = Obj("kT"), Obj("vT"), Obj("qT")
            o_wg, o_wpl = Obj("wg_in"), Obj("wpl_in")
            o_acc, o_V, o_R = Obj("acc"), Obj("Vaug"), Obj("Rn")
            o_t1 = [Obj("t1k"), Obj("t1q")]
            o_t2 = [Obj("t2k"), Obj("t2q")]
            o_rbf = [Obj("rbf0"), Obj("rbf1")]
            o_PT = [[Obj(f"PT{s}{h}") for h in range(2)] for s in range(2)]
            ds_tab = C.dsem("tab")
            o_out = [Obj(f"out{t}") for t in range(TOK // 128)]

            def stage_fence(reads_scr=False):
                pass

            o_g = Obj("gt")
            o_xs_l = [Obj(f"xs{i}") for i in range(NXS)]
            o_xn_l = [Obj(f"xn{i}") for i in range(NXN)]

            xnext = [0, 0]

            def stage_x(b, prefetched=False, only_prefetch=False, after_group=None):
                jobs = [(0, 0), (BLK, 1)] if b == 0 else [(2 * BLK, 0)]
                tiles = [(e0 + tt * 128, slot, tt) for e0, slot in jobs for tt in range(BLK // 128)]
                GS = 2

                def ensure_loaded(upto):
                    while xnext[b] <= min(upto, len(tiles) - 1):
                        r0 = tiles[xnext[b]][0]
                        i = xnext[b] % NXS
                        dma("sp", xs[i], x_d[r0:r0 + 128, :], [], [o_xs_l[i]], o_xs_l[i])
                        xnext[b] += 1

                if not prefetched:
                    dma("sp", gt, gb_d[0], [], [o_g, o_q], o_g)
                ensure_loaded(3)
                if only_prefetch:
                    return
                def rstd_group(g0):
                    n = len(tiles[g0:g0 + GS])
                    items = [(xs[(g0 + j) % NXS], bank2[1], [o_xs_l[(g0 + j) % NXS], bobj[2]], bobj[3])
                             for j in range(n)]
                    return rstd_multi(items)

                rs_cache = {0: rstd_group(0)}
                for g0 in range(0, len(tiles), GS):
                    ensure_loaded(g0 + 2 * GS - 1 + 1)
                    if g0 + GS < len(tiles):
                        rs_cache[g0 + GS] = rstd_group(g0 + GS)
                    grp = tiles[g0:g0 + GS]
                    rss, o_rs = rs_cache.pop(g0)
                    for j, (r0, slot, tt) in enumerate(grp):
                        it = g0 + j
                        xb, o_xs = xs[it % NXS], o_xs_l[it % NXS]
                        nb, o_xn = xn[it % NXN], o_xn_l[it % NXN]
                        bk = 4 + (it % 4)
                        op("dve", lambda e, xb=xb, nb=nb, rs=rss[j]: e.scalar_tensor_tensor(
                            out=nb, in0=xb, scalar=rs, in1=gt, op0=ALU.mult, op1=ALU.mult),
                           [o_xs, o_rs, o_g], [o_xn])
                        for kt in range(8):
                            op("pe", lambda e, nb=nb, kt=kt, bk=bk: e.transpose(
                                bankbf[bk][:, kt * 128:(kt + 1) * 128], nb[:, kt * 128:(kt + 1) * 128], ident[:]),
                               [o_xn, o_id], [bobj[bk]], signal=(kt == 7))
                        ch = tt // 4
                        if it % 2 == 0:
                            op("act", lambda e, bk=bk, slot=slot, tt=tt: e.activation(
                                out=xT[:, slot, :, tt * 128:(tt + 1) * 128],
                                in_=bankbf[bk].rearrange("p (k t) -> p k t", k=8), func=AF.Copy),
                               [bobj[bk]], [o_xT[slot][ch]])
                        else:
                            op("dve", lambda e, bk=bk, slot=slot, tt=tt: e.tensor_copy(
                                out=xT[:, slot, :, tt * 128:(tt + 1) * 128],
                                in_=bankbf[bk].rearrange("p (k t) -> p k t", k=8)),
                               [bobj[bk]], [o_xT[slot][ch]])
                    if after_group is not None:
                        after_group(g0 + GS, tiles)

            def load_wg(g):
                for j, c0 in enumerate((256 + 128 * g, 1024 + 128 * g, 1792 + 128 * g)):
                    dma("pool", wg_in[:, :, 128 * j:128 * (j + 1)],
                        win_d.rearrange("(k p) c -> p k c", p=128)[:, :, c0:c0 + 128], [], [o_wg], o_wg)

            def stage_p(b, g):
                pred_slot = b % 2
                cur_slot = 1 - pred_slot
                chunks = [(half, ch) for half in range(2) for ch in range(4)]

                def pmm(wcol0, slot, ch, bk):
                    for kt in range(8):
                        op("pe", lambda e, kt=kt: e.matmul(
                            bank[bk], lhsT=wg_in[:, kt, wcol0:wcol0 + 128],
                            rhs=xT[:, slot, kt, ch * 512:(ch + 1) * 512], start=(kt == 0), stop=(kt == 7)),
                           [o_wg, o_xT[slot][ch]], [bobj[bk]], signal=(kt == 7))

                def proj_mm(ci):
                    half, ch = chunks[ci]
                    slot = pred_slot if half == 0 else cur_slot
                    base = 3 * (ci % 2)
                    pmm(128, slot, ch, base)
                    pmm(256, slot, ch, base + 1)
                    if half == 1:
                        pmm(0, slot, ch, base + 2)

                def copies(ci):
                    half, ch = chunks[ci]
                    base = 3 * (ci % 2)
                    op("act", lambda e: e.activation(out=rbf[:, 0, :], in_=bank[base], func=AF.Copy),
                       [bobj[base]], [o_rbf[0]])
                    if half == 1:
                        op("act", lambda e: e.activation(out=rbf[:, 1, :], in_=bank[base + 2], func=AF.Copy),
                           [bobj[base + 2]], [o_rbf[1]])
                    op("act", lambda e: e.activation(out=vT[:, half, ch * 512:(ch + 1) * 512], in_=bank[base + 1],
                                                     func=AF.Copy), [bobj[base + 1]], [o_v])

                def rope_rest(ci):
                    half, ch = chunks[ci]
                    base = 3 * (ci % 2)
                    tab0 = half * BLK + ch * 512
                    jobs = [(0, base, 6, kT[:, half, ch * 512:(ch + 1) * 512], o_k)]
                    if half == 1:
                        jobs.append((1, base + 2, 7, qT[:, ch * 512:(ch + 1) * 512], o_q))
                    for (i, bk, wb, dst, dobj) in jobs:
                        op("pe", lambda e, i=i, wb=wb: e.matmul(bank[wb], lhsT=perm[:], rhs=rbf[:, i, :],
                                                                 start=True, stop=True),
                           [o_rbf[i], o_const], [bobj[wb]])
                    for (i, bk, wb, dst, dobj) in jobs:
                        op("dve", lambda e, i=i, bk=bk: e.tensor_tensor(
                            out=t1b[i], in0=bank[bk], in1=cst[:, tab0:tab0 + 512], op=ALU.mult),
                           [bobj[bk], o_tab], [o_t1[i]])
                        op("dve", lambda e, i=i, wb=wb: e.tensor_tensor(
                            out=t2b[i], in0=bank[wb], in1=snt[:, tab0:tab0 + 512], op=ALU.mult),
                           [bobj[wb], o_tab], [o_t2[i]])
                        op("pool", lambda e, i=i, dst=dst: e.tensor_tensor(out=dst, in0=t1b[i], in1=t2b[i],
                                                                            op=ALU.add),
                           [o_t1[i], o_t2[i]], [dobj])

                proj_mm(0)
                copies(0)
                for ci in range(len(chunks)):
                    if ci + 1 < len(chunks):
                        proj_mm(ci + 1)
                    rope_rest(ci)
                    if ci + 1 < len(chunks):
                        copies(ci + 1)

            actr = [0]

            def stage_a(b, g):
                first_cfg = True
                for d in (1, 4, 16):
                    nbh = 16 // d
                    span = 128 * d

                    def toks(r, n):
                        if n < 0:
                            return 0, (nbh - 1) * span + r
                        return 1, n * span + r

                    def sl(t0):
                        return slice(t0, t0 + 127 * d + 1, d)

                    klist = [(r, n) for r in range(d) for n in range(-1, nbh)]
                    vslot = {kn: i for i, kn in enumerate(klist)}
                    for j0 in range(0, len(klist), 8):
                        batch = klist[j0:j0 + 8]
                        bk = 6 + (actr[0] % 2)
                        actr[0] += 1
                        for j, (r, n) in enumerate(batch):
                            half, t0 = toks(r, n)
                            op("pe", lambda e, j=j, half=half, t0=t0, bk=bk: e.transpose(
                                bankbf[bk][:, j * 128:(j + 1) * 128], vT[:, half, sl(t0)], ident[:]),
                               [o_v, o_id], [bobj[bk]], signal=(j == len(batch) - 1))
                        nbt = len(batch)
                        dst = Vaug[:, j0:j0 + nbt, :].rearrange("p s (a c) -> p s a c", a=4)[:, :, 0:4:3, :]
                        src = bankbf[bk][:, 0:nbt * 128].rearrange("p (s h c) -> p s h c", s=nbt, h=2)
                        op("act", lambda e, dst=dst, src=src: e.activation(out=dst, in_=src, func=AF.Copy),
                           [bobj[bk]], [o_V])

                    qbs = [(r, n) for r in range(d) for n in range(nbh)]
                    pairs = [qbs[p0:p0 + 2] for p0 in range(0, len(qbs), 2)]
                    base_ctr = actr[0]
                    actr[0] += len(pairs)

                    def emit_qk(pi):
                        pair = pairs[pi]
                        s_ = (base_ctr + pi) % 2
                        halo = [b == 0 and n == 0 for (_, n) in pair]
                        mv = 0 if not any(halo) else (2 if all(halo) else 1)
                        assert not (halo[1] and not halo[0])
                        for h in range(2):
                            bk = 2 * s_ + h
                            for i, (r, n) in enumerate(pair):
                                _, tq = toks(r, n)
                                for kk, nk in enumerate((n - 1, n)):
                                    khalf, tk = toks(r, nk)
                                    col = (2 * i + kk) * 128
                                    last = (i == 1 and kk == 1)
                                    op("pe", lambda e, h=h, bk=bk, col=col, khalf=khalf, tk=tk, tq=tq: e.matmul(
                                        bank[bk][:, col:col + 128],
                                        lhsT=kT[64 * h:64 * h + 64, khalf, sl(tk)],
                                        rhs=qT[64 * h:64 * h + 64, sl(tq)], start=True, stop=True),
                                       [o_k, o_q], [bobj[bk]], signal=last)
                        for h in range(2):
                            bk = 2 * s_ + h
                            op("act", lambda e, h=h, bk=bk: e.activation(
                                out=PT[:, s_, h, :], in_=bank[bk], func=AF.Exp, scale=0.125),
                               [bobj[bk]], [o_PT[s_][h]])
                            op("dve", lambda e, h=h: e.tensor_tensor(
                                out=PT[:, s_, h, :], in0=PT[:, s_, h, :], in1=masks[:, mv, :], op=ALU.mult),
                               [o_PT[s_][h], o_const], [o_PT[s_][h]])

                    def emit_pv(pi, first):
                        pair = pairs[pi]
                        s_ = (base_ctr + pi) % 2
                        ob = 4 + s_
                        for i, (r, n) in enumerate(pair):
                            for h in range(2):
                                col = (2 * i + h) * 128
                                for kk, nk in enumerate((n - 1, n)):
                                    vs = vslot[(r, nk)]
                                    last = (i == 1 and h == 1 and kk == 1)
                                    op("pe", lambda e, vs=vs, h=h, col=col, kk=kk, i=i: e.matmul(
                                        bank[ob][:, col:col + 128],
                                        lhsT=Vaug[:, vs, 128 * h:128 * (h + 1)],
                                        rhs=PT[:, s_, h, (2 * i + kk) * 128:(2 * i + kk + 1) * 128],
                                        start=(kk == 0), stop=(kk == 1)),
                                       [o_V, o_PT[s_][h]], [bobj[ob]], signal=last)
                        for i, (r, n) in enumerate(pair):
                            _, tq = toks(r, n)
                            dst = acc[:, :, sl(tq)]
                            src = bank[ob][:, 256 * i:256 * (i + 1)].rearrange("p (h l) -> p h l", h=2)
                            if first:
                                op("act", lambda e, dst=dst, src=src: e.activation(out=dst, in_=src, func=AF.Copy),
                                   [bobj[ob]], [o_acc])
                            else:
                                op("dve", lambda e, dst=dst, src=src: e.tensor_tensor(
                                    out=dst, in0=src, in1=dst, op=ALU.add),
                                   [bobj[ob], o_acc], [o_acc])

                    emit_qk(0)
                    for pi in range(len(pairs)):
                        if pi + 1 < len(pairs):
                            emit_qk(pi + 1)
                        emit_pv(pi, first_cfg)
                    first_cfg = False
                rn_objs = [o_R, o_t1[1], o_t2[1]]
                kt = 2 + g
                for hv in range(2):
                    tsl = slice(hv * 1024, (hv + 1) * 1024)
                    op("act", lambda e: e.activation(out=Rn[0:64, :], in_=acc[64:128, 0, tsl], func=AF.Ln),
                       [o_acc], rn_objs)
                    op("act", lambda e: e.activation(out=Rn[64:128, :], in_=acc[0:64, 1, tsl], func=AF.Ln),
                       [o_acc], rn_objs)
                    op("act", lambda e: e.activation(out=Rn, in_=Rn, func=AF.Exp, scale=-1.0), rn_objs, rn_objs)
                    op("pool", lambda e: e.tensor_tensor(out=OT[0:64, kt, tsl], in0=acc[0:64, 0, tsl], in1=Rn[0:64, :],
                                                         op=ALU.mult), [o_acc] + rn_objs, [o_OT[kt]])
                    op("dve", lambda e: e.tensor_tensor(out=OT[64:128, kt, tsl], in0=acc[64:128, 1, tsl],
                                                        in1=Rn[64:128, :], op=ALU.mult), [o_acc] + rn_objs, [o_OT[kt]])

            o_ub, o_dT = Obj("ub"), Obj("dT")
            o_lvb = [Obj("lv0"), Obj("lv1")]

            def pool_begin(b):
                pred_slot = b % 2
                dma("pool", wpl_in, win_d.rearrange("(k p) c -> p k c", p=128)[:, :, 0:256],
                    [], [o_wpl, o_PT[0][0], o_PT[0][1], o_PT[1][0], o_PT[1][1]], o_wpl)
                for t in range(2):
                    bk = t
                    for kt in range(8):
                        op("pe", lambda e, kt=kt, bk=bk, t=t: e.matmul(
                            bank[bk][:, 0:16], lhsT=wpl_in[:, kt, 128 * t:128 * (t + 1)],
                            rhs=xT[:, pred_slot, kt, BLK - 16:BLK], start=(kt == 0), stop=(kt == 7)),
                           [o_wpl, o_xT[pred_slot][3]], [bobj[bk]], signal=(kt == 7))
                    op("act", lambda e, bk=bk, t=t: e.activation(out=ub[:, t, 0:16], in_=bank[bk][:, 0:16],
                                                                 func=AF.Copy), [bobj[bk]], [o_ub])

            def pool_chunk(b, ch):
                cur_slot = 1 - (b % 2)
                for t in range(2):
                    bk = t
                    for kt in range(8):
                        op("pe", lambda e, kt=kt, bk=bk, t=t: e.matmul(
                            bank[bk], lhsT=wpl_in[:, kt, 128 * t:128 * (t + 1)],
                            rhs=xT[:, cur_slot, kt, ch * 512:(ch + 1) * 512], start=(kt == 0), stop=(kt == 7)),
                           [o_wpl, o_xT[cur_slot][ch]], [bobj[bk]], signal=(kt == 7))
                    op("act", lambda e, bk=bk, t=t: e.activation(out=ub[:, t, 16:528], in_=bank[bk],
                                                                 func=AF.Copy), [bobj[bk]], [o_ub])
                prev, o_prev = ub, o_ub
                for k in range(4):
                    sh = 1 << k
                    lo = (1 << (k + 1)) - 1
                    cur, o_cur = lvb[k % 2], o_lvb[k % 2]
                    op("dve", lambda e, prev=prev, cur=cur, sh=sh, lo=lo: e.tensor_tensor(
                        out=cur[:, :, lo:528], in0=prev[:, :, lo:528], in1=prev[:, :, lo - sh:528 - sh],
                        op=ALU.add), [o_prev], [o_cur])
                    t, hh = k // 2, k % 2
                    p0 = 64 * hh
                    op("dve", lambda e, t=t, cur=cur, p0=p0: e.scalar_tensor_tensor(
                        out=dTb[p0:p0 + 64, t, :], in0=cur[p0:p0 + 64, t, 16:528],
                        scalar=invw[p0:p0 + 64, t:t + 1], in1=ub[p0:p0 + 64, t, 16:528],
                        op0=ALU.mult, op1=ALU.subtract), [o_cur, o_ub, o_const], [o_dT])
                    if b == 0 and ch == 0:
                        op("dve", lambda e, t=t, cur=cur, p0=p0: e.tensor_tensor(
                            out=ptmp[p0:p0 + 64, t, 0:16], in0=cur[p0:p0 + 64, t, 16:32],
                            in1=invc[p0:p0 + 64, t, :], op=ALU.mult), [o_cur, o_const], [o_dT])
                        op("dve", lambda e, t=t, p0=p0: e.tensor_tensor(
                            out=dTb[p0:p0 + 64, t, 0:16], in0=ptmp[p0:p0 + 64, t, 0:16],
                            in1=ub[p0:p0 + 64, t, 16:32], op=ALU.subtract), [o_dT, o_ub], [o_dT])
                    prev, o_prev = cur, o_cur
                for t in range(2):
                    bk = t
                    op("pe", lambda e, t=t, bk=bk: e.matmul(bank[bk], lhsT=wpbd[:, t, :], rhs=dTb[:, t, :],
                                                           start=True, stop=True),
                       [o_dT, o_const], [bobj[bk]])
                    op("act", lambda e, t=t, bk=bk: e.activation(
                        out=OT[:, t, ch * 512:(ch + 1) * 512], in_=bank[bk], func=AF.Copy,
                        scale=psc[:, t:t + 1]), [bobj[bk], o_const], [o_OT[t]])
                op("dve", lambda e: e.tensor_copy(out=ptmp[:, :, 16:32], in_=ub[:, :, 512:528]), [o_ub], [o_dT])
                op("act", lambda e: e.activation(out=ub[:, :, 0:16], in_=ptmp[:, :, 16:32], func=AF.Copy),
                   [o_dT], [o_ub])

            o_wo, o_g2, o_yb = Obj("wout"), Obj("gt2"), Obj("ys")

            def wout_view(b):
                return xT[:, b % 2, 0:4, :].rearrange("p k (a c) -> p (k a) c", a=2)

            def gt2_view(b):
                return xT[:, b % 2, 4, :].bitcast(F32)

            def prefetch_o(b):
                dma("pool", wout_view(b), wout_d.rearrange("(k p) c -> p k c", p=128),
                    [], [o_wo] + o_xT[b % 2], o_wo)
                dma("sp", gt2_view(b), gb_d[1], [], [o_g2] + o_xT[b % 2], o_g2)

            def stage_o(b):
                wout = wout_view(b)
                gt2 = gt2_view(b)
                o_xs2_l = [Obj("xs2_0"), Obj("xs2_1")]
                junk = scr[:, 5632:6144].bitcast(BF16)
                o_j = Obj("junk_o")
                for tt in range(BLK // 128):
                    st = tt % 4
                    xb = xs2[tt % 2]
                    yb = ys[0]
                    o_xb = o_xs2_l[tt % 2]
                    tok0 = b * BLK + tt * 128
                    dma("sp", xb, x_d[HALO + tok0:HALO + tok0 + 128, :], [], [o_xb], o_xb)
                    for hh in range(2):
                        bk = 2 * st + hh
                        for kt in range(8):
                            op("pe", lambda e, kt=kt, bk=bk, hh=hh, tt=tt: e.matmul(
                                bank[bk], lhsT=OT[:, kt, tt * 128:(tt + 1) * 128],
                                rhs=wout[:, kt, hh * 512:(hh + 1) * 512], start=(kt == 0), stop=(kt == 7)),
                               [o_OT[kt], o_wo], [bobj[bk]], signal=(kt == 7))
                    src = bank2[st]
                    rs, o_rs = rstd_from(lambda src=src: src, junk, [bobj[2 * st], bobj[2 * st + 1]], o_j)
                    op("dve", lambda e, yb=yb, src=src, rs=rs: e.scalar_tensor_tensor(
                        out=yb, in0=src, scalar=rs, in1=gt2, op0=ALU.mult, op1=ALU.mult),
                       [bobj[2 * st], bobj[2 * st + 1], o_rs, o_g2], [o_yb])
                    op("dve", lambda e, yb=yb, xb=xb: e.tensor_tensor(out=xb, in0=yb, in1=xb, op=ALU.add),
                       [o_yb, o_xb], [o_xb])
                    ti = tok0 // 128
                    dma("sp", out_d[tok0:tok0 + 128, :], xb, [o_xb], [o_out[ti]], o_xb)

            def fence():
                pass

            for b in range(2):
                dma("sp", cst[:], cs_d[:, b * BLK:b * BLK + 2 * BLK], [], [o_tab], ds_tab)
                dma("sp", snt[:], sn_d[:, b * BLK:b * BLK + 2 * BLK], [], [o_tab], ds_tab)
                C.barrier()
                load_wg(0)

                def hook(ndone, tiles, b=b):
                    npred = len(tiles) - BLK // 128
                    if ndone == npred or (npred == 0 and ndone == 2):
                        pass
                    if ndone == max(npred, 2) and not hook.started:
                        pool_begin(b)
                        hook.started = True
                    k = ndone - npred
                    if hook.started and k > 0 and k % 4 == 0:
                        while hook.next_ch < k // 4:
                            pool_chunk(b, hook.next_ch)
                            hook.next_ch += 1
                hook.started = False
                hook.next_ch = 0
                stage_x(b, prefetched=(b == 1), after_group=hook)
                C.barrier()
                op("pool", lambda e: e.memset(Vaug[:, :, 64:192], 1.0), [], [o_V])
                for g in range(6):
                    stage_p(b, g)
                    if g + 1 < 6:
                        load_wg(g + 1)
                    else:
                        prefetch_o(b)
                    stage_a(b, g)
                C.barrier()
                if dbg and b == dbg_blk:
                    ds_dbg = C.dsem("dbg")
                    od = Obj("dbg")
                    srcs = {"OT": OT[:].rearrange("p k t -> p (k t)"), "rsb": rsb[:], "kT": kT[:].rearrange("p h t -> p (h t)"),
                            "vT": vT[:].rearrange("p h t -> p (h t)"), "qT": qT[:], "acc": scr[:, 0:4096],
                            "xT": xT[:].rearrange("p s k t -> p (s k t)")}
                    for n in dbg_d:
                        dma("sp", dbg_d[n], srcs[n], [], [od], ds_dbg)
                    C.barrier()
                if b == 0:
                    stage_x(1, only_prefetch=True)
                stage_o(b)
                C.barrier()
            if dbg and "x1" in dbg:
                pass
        C.barrier()
        with ExitStack() as p2:
            sb2 = lambda n, s, d: p2.enter_context(nc.sbuf_tensor(n, list(s), d))
            wg = sb2("wg", (128, 8, DFF), BF16)
            wu = sb2("wu", (128, 8, DFF), BF16)
            wd = sb2("wd", (128, NFT, D), BF16)
            x1b = sb2("x1b", (128, 4, D), F32)
            yb2 = sb2("yb2", (128, D), F32)
            g3 = sb2("g3", (128, D), F32)
            g4 = sb2("g4", (128, D), F32)
            sg = sb2("sg", (128, 2, 512), F32)
            h2 = sb2("h2", (128, 2, D), BF16)
            h2T = sb2("h2T", (128, 8, 512), BF16)
            actT = sb2("actT", (128, NFT, 512), BF16)
            o_wgu, o_wd, o_g34 = Obj("wgu"), Obj("wd"), Obj("g34")
            FTG = [(0, 3), (3, 8), (8, 15), (15, NFT)]
            o_wgu_l = [Obj(f"wgu{i}") for i in range(len(FTG))]
            ftg_of = {}
            for gi, (f0, f1) in enumerate(FTG):
                for ft in range(f0, f1):
                    ftg_of[ft] = gi
                dma("pool", wg[:, :, f0 * 128:f1 * 128],
                    wg_d.rearrange("(k p) c -> p k c", p=128)[:, :, f0 * 128:f1 * 128], [], [o_wgu_l[gi]], o_wgu_l[gi])
                dma("pool", wu[:, :, f0 * 128:f1 * 128],
                    wu_d.rearrange("(k p) c -> p k c", p=128)[:, :, f0 * 128:f1 * 128], [], [o_wgu_l[gi]], o_wgu_l[gi])
            for ft in range(NFT):
                dma("pool", wd[:, ft, :], wd_d[ft * 128:(ft + 1) * 128, :], [], [o_wd], o_wd)
            dma("sp", g3[:], gb_d[2], [], [o_g34], o_g34)
            dma("sp", g4[:], gb_d[3], [], [o_g34], o_g34)
            o_h2T, o_act = Obj("h2T"), Obj("actT")
            o_sg = [Obj("sg0"), Obj("sg1")]
            o_h2 = [Obj("h2_0"), Obj("h2_1")]
            o_y2 = Obj("yb2")
            o_x1_l = [Obj(f"x1b{i}") for i in range(4)]
            NCH = TOK // 512

            def prep_nonpe(c, s_):
                ti = c * 4 + s_
                bi = s_ % 2
                xb, o_x1 = x1b[:, bi, :], o_x1_l[bi]
                dma("sp", xb, out_d[ti * 128:(ti + 1) * 128, :], [o_out[ti]], [o_x1], o_x1)
                hb, o_hb = h2[:, s_ % 2, :], o_h2[s_ % 2]
                rs, o_rs = rstd_from(lambda: xb, hb, [o_x1], o_hb)
                op("dve", lambda e: e.scalar_tensor_tensor(
                    out=hb, in0=xb, scalar=rs, in1=g3[:], op0=ALU.mult, op1=ALU.mult),
                   [o_x1, o_rs, o_g34], [o_hb])

            def prep_pe(c, s_):
                hb, o_hb = h2[:, s_ % 2, :], o_h2[s_ % 2]
                bk = s_ % 2
                for kt in range(8):
                    op("pe", lambda e, kt=kt: e.transpose(
                        bankbf[bk][:, kt * 128:(kt + 1) * 128], hb[:, kt * 128:(kt + 1) * 128], ident[:]),
                       [o_hb, o_id], [bobj[bk]], signal=(kt == 7))
                op("act", lambda e: e.activation(
                    out=h2T[:, :, s_ * 128:(s_ + 1) * 128],
                    in_=bankbf[bk].rearrange("p (k t) -> p k t", k=8), func=AF.Copy),
                   [bobj[bk]], [o_h2T])

            def gateup(c):
                for ft in range(NFT):
                    i = ft % 2
                    bg, bu = i, 2 + i
                    for (wt, bk) in ((wg, bg), (wu, bu)):
                        for kt in range(8):
                            op("pe", lambda e, wt=wt, bk=bk, kt=kt, ft=ft: e.matmul(
                                bank[bk], lhsT=wt[:, kt, ft * 128:(ft + 1) * 128], rhs=h2T[:, kt, :],
                                start=(kt == 0), stop=(kt == 7)),
                               [o_wgu_l[ftg_of[ft]], o_h2T], [bobj[bk]], signal=(kt == 7))
                    op("act", lambda e, i=i, bg=bg: e.activation(out=sg[:, i, :], in_=bank[bg], func=AF.Silu),
                       [bobj[bg]], [o_sg[i]])
                    op("dve", lambda e, i=i, bu=bu, ft=ft: e.tensor_tensor(
                        out=actT[:, ft, :], in0=bank[bu], in1=sg[:, i, :], op=ALU.mult),
                       [bobj[bu], o_sg[i]], [o_act])

            def epi_load(c, s_):
                ti = c * 4 + s_
                bi = 2 + s_ % 2
                dma("sp", x1b[:, bi, :], out_d[ti * 128:(ti + 1) * 128, :], [o_out[ti]], [o_x1_l[bi]], o_x1_l[bi])

            def down_mm(c, s_):
                st = 2 + (s_ % 2)
                for hh in range(2):
                    bk = 2 * st + hh
                    for ft in range(NFT):
                        op("pe", lambda e, ft=ft, bk=bk, hh=hh: e.matmul(
                            bank[bk], lhsT=actT[:, ft, s_ * 128:(s_ + 1) * 128],
                            rhs=wd[:, ft, hh * 512:(hh + 1) * 512], start=(ft == 0), stop=(ft == NFT - 1)),
                           [o_act, o_wd], [bobj[bk]], signal=(ft == NFT - 1))

            def epilogue(c, s_):
                ti = c * 4 + s_
                bi = 2 + s_ % 2
                xb, o_x1 = x1b[:, bi, :], o_x1_l[bi]
                st = 2 + (s_ % 2)
                src = bank2[st]
                rs, o_rs = rstd_from(lambda: src, yb2[:], [bobj[2 * st], bobj[2 * st + 1]], o_y2)
                op("dve", lambda e: e.scalar_tensor_tensor(
                    out=yb2[:], in0=src, scalar=rs, in1=g4[:], op0=ALU.mult, op1=ALU.mult),
                   [bobj[2 * st], bobj[2 * st + 1], o_rs, o_g34], [o_y2])
                op("pool", lambda e: e.tensor_tensor(out=xb, in0=yb2[:], in1=xb, op=ALU.add),
                   [o_y2, o_x1], [o_x1])
                dma("sp", out_d[ti * 128:(ti + 1) * 128, :], xb, [o_x1], [o_out[ti]], o_x1)

            for s_ in range(4):
                prep_nonpe(0, s_)
                prep_pe(0, s_)
            for c in range(NCH):
                if c + 1 < NCH:
                    prep_nonpe(c + 1, 0)
                gateup(c)
                for s_ in range(4):
                    epi_load(c, s_)
                    if c + 1 < NCH and s_ + 1 < 4:
                        prep_nonpe(c + 1, s_ + 1)
                    down_mm(c, s_)
                    if c + 1 < NCH:
                        prep_pe(c + 1, s_)
                    epilogue(c, s_)
            C.barrier()
    return nc


def _tables(hf):
    half = 32
    freqs = (np.float32(10000.0) ** (-np.arange(half, dtype=np.float32) * np.float32(2.0 / 64))).astype(np.float32)
    pos = (hf * TOK - HALO + np.arange(HALO + TOK)).astype(np.float32)
    ang = (pos[:, None] * freqs[None, :]).astype(np.float32).astype(np.float64)
    p = np.arange(128)
    fi = (p % 64) % 32
    sign = np.where((p % 64) < 32, -1.0, 1.0)
    cs = np.cos(ang[:, fi]).T
    sn = np.sin(ang[:, fi]).T * sign[:, None]
    return np.ascontiguousarray(cs.astype(np.float32)), np.ascontiguousarray(sn.astype(np.float32))


def _consts(hf):
    m = np.arange(128)
    partner = np.where((m % 64) < 32, m + 32, m - 32)
    perm = np.zeros((128, 128), np.float32)
    perm[partner, m] = 1.0
    k = np.arange(128)[:, None]
    q = np.arange(128)[None, :]
    mp = (k >= q).astype(np.float32)
    mc = (k <= q).astype(np.float32)
    flag = 0.0 if hf == 0 else 1.0
    mh = mp * flag
    masks = np.stack([np.concatenate([mp, mc, mp, mc], 1),
                      np.concatenate([mh, mc, mp, mc], 1),
                      np.concatenate([mh, mc, mh, mc], 1)]).astype(np.float32)
    wins = np.array([2, 4, 8, 16], np.float32)
    invwin = np.zeros((128, 2), np.float32)
    invcnt = np.zeros((128, 2, 16), np.float32)
    for t in range(2):
        for hh in range(2):
            w = wins[2 * t + hh]
            invwin[64 * hh:64 * hh + 64, t] = 1.0 / w
            posj = hf * TOK + np.arange(16)
            invcnt[64 * hh:64 * hh + 64, t, :] = (1.0 / np.minimum(posj + 1, w)).astype(np.float32)[None, :]
    return perm, masks, invwin, invcnt


_NC_CACHE = {}


def kernel(x, ln_pre_mix, w_in, w_pool, pool_scale, w_out, ln_post_mix, ln_pre_ffn, w_gate, w_up, w_down,
           ln_post_ffn):
    x = np.asarray(x, np.float32)
    f = lambda a: np.ascontiguousarray(np.asarray(a, np.float32))
    w_in0, w_out0, wg0, wu0, wd0 = f(w_in[0]), f(w_out[0]), f(w_gate[0]), f(w_up[0]), f(w_down[0])
    wp = np.asarray(w_pool[0], np.float32)
    wpbd = np.zeros((2, 128, 128), np.float32)
    for t in range(2):
        for hh in range(2):
            wpbd[t, 64 * hh:64 * hh + 64, 64 * hh:64 * hh + 64] = wp[2 * t + hh]
    gb = np.stack([np.broadcast_to(np.asarray(g[0], np.float32)[None, :], (128, D))
                   for g in (ln_pre_mix, ln_post_mix, ln_pre_ffn, ln_post_ffn)]).astype(np.float32)
    gb = np.ascontiguousarray(gb)
    ps_ = np.asarray(pool_scale[0], np.float32)
    pscale = np.ascontiguousarray(np.stack([ps_[0:128], ps_[128:256]], 1))
    in_maps = []
    for c in range(NCORES):
        b, hf = c // 2, c % 2
        xe = np.zeros((HALO + TOK, D), np.float32)
        xe[HALO:] = x[b, hf * TOK:(hf + 1) * TOK]
        if hf == 1:
            xe[:HALO] = x[b, TOK - HALO:TOK]
        cs, sn = _tables(hf)
        perm, masks, invwin, invcnt = _consts(hf)
        in_maps.append({"x": xe, "cs": cs, "sn": sn, "w_in": w_in0, "w_out": w_out0, "w_gate": wg0, "w_up": wu0,
                        "w_down": wd0, "wpool_bd": wpbd, "gb": gb, "pscale": pscale, "perm": perm, "masks": masks,
                        "invwin": invwin, "invcnt": invcnt})
    if "nc" not in _NC_CACHE:
        _NC_CACHE["nc"] = build_program()
    nc = _NC_CACHE["nc"]
    res = run_bass_kernel_spmd(nc, in_maps, core_ids=list(range(NCORES)))
    out = np.zeros((4, SEQ, D), np.float32)
    for c in range(NCORES):
        b, hf = c // 2, c % 2
        out[b, hf * TOK:(hf + 1) * TOK] = res.results[c]["out"]
    return out
```

```python
import os
from contextlib import ExitStack
import numpy as np
import concourse.bass as bass
import concourse.mybir as mybir
from concourse.bass_utils import run_bass_kernel_spmd

F32 = mybir.dt.float32
BF16 = mybir.dt.bfloat16
AF = mybir.ActivationFunctionType
ALU = mybir.AluOpType

D = 1024
SEQ = 8192
TOK = 4096
HALO = 2048
BLK = 2048
DFF = 2816
NFT = DFF // 128
EPS = 1e-6
NCORES = 8


class Sem:
    def __init__(self, h, name):
        self.h = h
        self.name = name
        self.count = 0


class Obj:
    __slots__ = ("name", "w", "r")

    def __init__(self, name):
        self.name = name
        self.w = {}
        self.r = {}


class Ctx:
    def __init__(self, nc, es):
        self.nc = nc
        self.es = es
        self.eng = {}
        for n, h in (("pe", nc.tensor), ("act", nc.scalar), ("dve", nc.vector), ("pool", nc.gpsimd), ("sp", nc.sync)):
            s = Sem(es.enter_context(nc.semaphore("sem_" + n)), "sem_" + n)
            self.eng[n] = (h, s, {})
        self.dsems = []
        self._ds = {}
        self.nwaits = 0

    def dsem(self, name):
        s = Sem(self.es.enter_context(self.nc.semaphore("dsem_" + name)), "dsem_" + name)
        self.dsems.append(s)
        return s

    def _wait(self, en, sem, val):
        h, _, waited = self.eng[en]
        if waited.get(sem.name, 0) >= val:
            return
        assert val <= sem.count, f"wait on unsignaled {sem.name} {val}>{sem.count} from {en}"
        h.wait_ge(sem.h, val)
        waited[sem.name] = val
        self.nwaits += 1

    def _deps(self, en, reads, writes, is_dma, ds=None):
        _, own, _ = self.eng[en]
        deps = {}

        def need(s, v, raw, waw=False):
            if s is own and not is_dma:
                if en == "pe":
                    return
            if waw and s is ds:
                return
            if deps.get(s.name, (None, 0))[1] < v:
                deps[s.name] = (s, v)

        for o in reads:
            for s, v in o.w.values():
                need(s, v, True)
        for o in writes:
            for s, v in o.w.values():
                need(s, v, False, True)
            for s, v in o.r.values():
                need(s, v, False)
        for s, v in deps.values():
            self._wait(en, s, v)

    def op(self, en, fn, reads=(), writes=(), signal=True):
        h, own, _ = self.eng[en]
        writes = list(writes) + [o for o in reads if o.name.startswith("bank") and o not in writes]
        self._deps(en, reads, writes, False)
        inst = fn(h)
        if signal:
            inst.then_inc(own.h, 1)
            own.count += 1
            val = own.count
        else:
            val = own.count + 1
        for o in reads:
            o.r[own.name] = (own, val)
        for o in writes:
            o.w[own.name] = (own, val)
        return inst

    def ds_of(self, obj):
        if obj.name not in self._ds:
            self._ds[obj.name] = self.dsem(obj.name)
        return self._ds[obj.name]

    def dma(self, en, out, in_, reads, writes, key):
        h, own, _ = self.eng[en]
        ds = key if isinstance(key, Sem) else self.ds_of(key)
        self._deps(en, reads, writes, True, ds)
        inst = h.dma_start(out=out, in_=in_)
        inst.then_inc(ds.h, 16)
        ds.count += 16
        for o in reads:
            o.r[ds.name] = (ds, ds.count)
        for o in writes:
            o.w[ds.name] = (ds, ds.count)
        return inst

    def barrier(self):
        sems = [e[1] for e in self.eng.values()] + self.dsems
        for en in self.eng:
            for s in sems:
                if s is self.eng[en][1]:
                    continue
                if s.count > 0:
                    self._wait(en, s, s.count)


def build_program(dbg=None, dbg_blk=0):
    nc = bass.Bass("TRN2", target_bir_lowering=False)
    dt_in = lambda n, s: nc.dram_tensor(n, list(s), F32, kind="ExternalInput").ap()
    x_d = dt_in("x", (HALO + TOK, D))
    cs_d = dt_in("cs", (128, HALO + TOK))
    sn_d = dt_in("sn", (128, HALO + TOK))
    win_d = dt_in("w_in", (D, 2560))
    wout_d = dt_in("w_out", (D, D))
    wg_d = dt_in("w_gate", (D, DFF))
    wu_d = dt_in("w_up", (D, DFF))
    wd_d = dt_in("w_down", (DFF, D))
    wpbd_d = dt_in("wpool_bd", (2, 128, 128))
    gb_d = dt_in("gb", (4, 128, D))
    psc_d = dt_in("pscale", (128, 2))
    perm_d = dt_in("perm", (128, 128))
    mask_d = dt_in("masks", (3, 128, 512))
    invw_d = dt_in("invwin", (128, 2))
    invc_d = dt_in("invcnt", (128, 2, 16))
    out_d = nc.dram_tensor("out", [TOK, D], F32, kind="ExternalOutput").ap()
    dbg_d = {}
    if dbg:
        for n, (s, dty) in dbg.items():
            dbg_d[n] = nc.dram_tensor("dbg_" + n, list(s), dty, kind="ExternalOutput").ap()

    with ExitStack() as es:
        C = Ctx(nc, es)
        op, dma = C.op, C.dma
        sb = lambda n, s, d: es.enter_context(nc.sbuf_tensor(n, list(s), d))
        ps = es.enter_context(nc.psum_tensor("ps", [128, 4096], F32))
        bank = [ps[:, 512 * i:512 * (i + 1)] for i in range(8)]
        bankbf = [ps[:, 512 * i:512 * (i + 1)].bitcast(BF16) for i in range(8)]
        bobj = [Obj(f"bank{i}") for i in range(8)]
        bank2 = [ps[:, 1024 * i:1024 * (i + 1)] for i in range(4)]

        ds_const = C.dsem("const")
        o_const = Obj("const")
        ident = sb("ident", (128, 128), BF16)
        mhalf = sb("mhalf", (128, 1), F32)
        ssb = sb("ssb", (128, 160), F32)
        tsb = sb("tsb", (128, 160), F32)
        rsb = sb("rsb", (128, 160), F32)
        o_ss = Obj("ss")
        p1 = ExitStack()
        sbc = lambda n, s, d: p1.enter_context(nc.sbuf_tensor(n, list(s), d))
        ident_f = sbc("ident_f", (128, 128), F32)
        perm = sbc("perm_sb", (128, 128), BF16)
        masks = sbc("masks_sb", (128, 3, 512), BF16)
        wpbd = sbc("wpbd_sb", (128, 2, 128), BF16)
        psc = sbc("psc_sb", (128, 2), F32)
        invw = sbc("invw_sb", (128, 2), F32)
        invc = sbc("invc_sb", (128, 2, 16), F32)
        dma("pool", perm[:], perm_d, [], [o_const], ds_const)
        dma("pool", masks[:], mask_d.rearrange("v p c -> p v c"), [], [o_const], ds_const)
        dma("pool", wpbd[:], wpbd_d.rearrange("t p c -> p t c"), [], [o_const], ds_const)
        ds_const2 = C.dsem("const2")
        dma("sp", psc[:], psc_d, [], [o_const], ds_const2)
        dma("sp", invw[:], invw_d, [], [o_const], ds_const2)
        dma("sp", invc[:], invc_d, [], [o_const], ds_const2)
        o_id = Obj("ident")
        op("pool", lambda e: e.memset(ident_f[:], 1.0), [], [o_id])
        op("pool", lambda e: e.affine_select(out=ident_f[:], in_=ident_f[:], pattern=[[-1, 128]],
                                             compare_op=ALU.is_equal, fill=0.0, base=0, channel_multiplier=1),
           [o_id], [o_id])
        op("dve", lambda e: e.tensor_copy(out=ident[:], in_=ident_f[:]), [o_id], [o_id])
        op("dve", lambda e: e.memset(mhalf[:], -0.5), [], [o_const])
        op("dve", lambda e: e.memset(ssb[:], 0.0), [], [o_ss])

        sscol = [0]

        def rstd_from(psrc_ap_fn, junk_ap, src_objs, junk_obj, n_feat=D):
            c = sscol[0]
            sscol[0] += 1
            assert c < 160
            oc = Obj(f"rs{sscol[0]}")
            op("act", lambda e: e.activation(out=junk_ap, in_=psrc_ap_fn(), func=AF.Square,
                                             accum_out=ssb[:, c:c + 1]),
               list(src_objs) + [o_ss], [junk_obj, oc])
            op("dve", lambda e: e.tensor_scalar(out=tsb[:, c:c + 1], in0=ssb[:, c:c + 1], scalar1=1.0 / n_feat,
                                                scalar2=EPS, op0=ALU.mult, op1=ALU.add), [oc], [oc])
            op("pool", lambda e: e.tensor_tensor(out=rsb[:, c:c + 1], in0=tsb[:, c:c + 1], in1=mhalf[:],
                                                 op=ALU.pow), [oc, o_const], [oc])
            return rsb[:, c:c + 1], oc

        def rstd_multi(items, n_feat=D):
            n = len(items)
            c0 = sscol[0]
            sscol[0] += n
            assert sscol[0] <= 160
            oc = Obj(f"rsm{c0}")
            for j, (src, junk_ap, src_objs, junk_obj) in enumerate(items):
                op("act", lambda e, src=src, junk_ap=junk_ap, j=j: e.activation(
                    out=junk_ap, in_=src, func=AF.Square, accum_out=ssb[:, c0 + j:c0 + j + 1]),
                   list(src_objs) + [o_ss], [junk_obj, oc])
            op("dve", lambda e: e.tensor_scalar(out=tsb[:, c0:c0 + n], in0=ssb[:, c0:c0 + n], scalar1=1.0 / n_feat,
                                                scalar2=EPS, op0=ALU.mult, op1=ALU.add), [oc], [oc])
            op("pool", lambda e: e.tensor_tensor(out=rsb[:, c0:c0 + n], in0=tsb[:, c0:c0 + n],
                                                 in1=mhalf[:].to_broadcast([128, n]), op=ALU.pow),
               [oc, o_const], [oc])
            return [rsb[:, c0 + j:c0 + j + 1] for j in range(n)], oc

        with p1:
            sb1 = lambda n, s, d: p1.enter_context(nc.sbuf_tensor(n, list(s), d))
            xT = sb1("xT", (128, 2, 8, BLK), BF16)
            OT = sb1("OT", (128, 8, BLK), BF16)
            cst = sb1("cst", (128, 2 * BLK), F32)
            snt = sb1("snt", (128, 2 * BLK), F32)
            kT = sb1("kT", (128, 2, BLK), BF16)
            vT = sb1("vT", (128, 2, BLK), BF16)
            qT = sb1("qT", (128, BLK), BF16)
            wg_in = sb1("wg_in", (128, 8, 384), BF16)
            scr = sb1("scr", (128, 10240), F32)
            rbf = sb1("rbf", (128, 2, 512), BF16)
            PT = sb1("PT", (128, 2, 2, 512), BF16)
            acc = scr[:, 0:4096].rearrange("p (h t) -> p h t", h=2)
            Vaug = scr[:, 4096:8192].bitcast(BF16).rearrange("p (s c) -> p s c", s=32)
            Rn = scr[:, 9216:10240]
            t1b = [scr[:, 8192:8704], scr[:, 9216:9728]]
            t2b = [scr[:, 8704:9216], scr[:, 9728:10240]]
            NXS = 5
            xs = [scr[:, 1024 * i:1024 * (i + 1)] for i in range(NXS)]
            NXN = 2
            xn = [scr[:, 5120 + 512 * i:5120 + 512 * (i + 1)].bitcast(BF16) for i in range(NXN)]
            gt = qT[:].bitcast(F32)
            gt2 = scr[:, 6144:7168]
            xs2 = [scr[:, 7168 + 1024 * i:7168 + 1024 * (i + 1)] for i in range(2)]
            ys = [scr[:, 9216:10240]] * 2
            ub = scr[:, 6144:7200].rearrange("p (t c) -> p t c", t=2)
            lvb = [scr[:, 7200 + 1056 * i:7200 + 1056 * (i + 1)].rearrange("p (t c) -> p t c", t=2) for i in range(2)]
            dTb = scr[:, 9312:9824].bitcast(BF16).rearrange("p (t c) -> p t c", t=2)
            ptmp = scr[:, 9824:9888].rearrange("p (t c) -> p t c", t=2)
            wpl_in = PT[:].rearrange("p a b c -> p (a b c)").rearrange("p (k c) -> p k c", k=8)

            o_scr = Obj("scr")
            o_xT = [[Obj(f"xT{s}_{c}") for c in range(4)] for s in range(2)]
            o_OT = [Obj(f"OT{k}") for k in range(8)]
            o_tab = Obj("tab")
            o_k, o_v, o_q = Obj("kT"), Obj("vT"), Obj("qT")
            o_wg, o_wpl = Obj("wg_in"), Obj("wpl_in")
            o_acc, o_V, o_R = Obj("acc"), Obj("Vaug"), Obj("Rn")
            o_t1 = [Obj("t1k"), Obj("t1q")]
            o_t2 = [Obj("t2k"), Obj("t2q")]
            o_rbf = [Obj("rbf0"), Obj("rbf1")]
            o_PT = [[Obj(f"PT{s}{h}") for h in range(2)] for s in range(2)]
            ds_tab = C.dsem("tab")
            o_out = [Obj(f"out{t}") for t in range(TOK // 128)]

            def stage_fence(reads_scr=False):
                pass

            o_g = Obj("gt")
            o_xs_l = [Obj(f"xs{i}") for i in range(NXS)]
            o_xn_l = [Obj(f"xn{i}") for i in range(NXN)]

            xnext = [0, 0]

            def stage_x(b, prefetched=False, only_prefetch=False, after_group=None):
                jobs = [(0, 0), (BLK, 1)] if b == 0 else [(2 * BLK, 0)]
                tiles = [(e0 + tt * 128, slot, tt) for e0, slot in jobs for tt in range(BLK // 128)]
                GS = 2

                def ensure_loaded(upto):
                    while xnext[b] <= min(upto, len(tiles) - 1):
                        r0 = tiles[xnext[b]][0]
                        i = xnext[b] % NXS
                        dma("sp", xs[i], x_d[r0:r0 + 128, :], [], [o_xs_l[i]], o_xs_l[i])
                        xnext[b] += 1

                if not prefetched:
                    dma("sp", gt, gb_d[0], [], [o_g, o_q], o_g)
                ensure_loaded(3)
                if only_prefetch:
                    return
                def rstd_group(g0):
                    n = len(tiles[g0:g0 + GS])
                    items = [(xs[(g0 + j) % NXS], bank2[1], [o_xs_l[(g0 + j) % NXS], bobj[2]], bobj[3])
                             for j in range(n)]
                    return rstd_multi(items)

                rs_cache = {0: rstd_group(0)}
                for g0 in range(0, len(tiles), GS):
                    ensure_loaded(g0 + 2 * GS - 1 + 1)
                    grp = tiles[g0:g0 + GS]
                    rss, o_rs = rs_cache.pop(g0)
                    for j, (r0, slot, tt) in enumerate(grp):
                        it = g0 + j
                        xb, o_xs = xs[it % NXS], o_xs_l[it % NXS]
                        nb, o_xn = xn[it % NXN], o_xn_l[it % NXN]
                        op("dve", lambda e, xb=xb, nb=nb, rs=rss[j]: e.scalar_tensor_tensor(
                            out=nb, in0=xb, scalar=rs, in1=gt, op0=ALU.mult, op1=ALU.mult),
                           [o_xs, o_rs, o_g], [o_xn])
                    if g0 + GS < len(tiles):
                        rs_cache[g0 + GS] = rstd_group(g0 + GS)
                    for j, (r0, slot, tt) in enumerate(grp):
                        it = g0 + j
                        nb, o_xn = xn[it % NXN], o_xn_l[it % NXN]
                        bk = 4 + (it % 4)
                        for kt in range(8):
                            op("pe", lambda e, nb=nb, kt=kt, bk=bk: e.transpose(
                                bankbf[bk][:, kt * 128:(kt + 1) * 128], nb[:, kt * 128:(kt + 1) * 128], ident[:]),
                               [o_xn, o_id], [bobj[bk]], signal=(kt == 7))
                        ch = tt // 4
                        if it % 2 == 0:
                            op("act", lambda e, bk=bk, slot=slot, tt=tt: e.activation(
                                out=xT[:, slot, :, tt * 128:(tt + 1) * 128],
                                in_=bankbf[bk].rearrange("p (k t) -> p k t", k=8), func=AF.Copy),
                               [bobj[bk]], [o_xT[slot][ch]])
                        else:
                            op("dve", lambda e, bk=bk, slot=slot, tt=tt: e.tensor_copy(
                                out=xT[:, slot, :, tt * 128:(tt + 1) * 128],
                                in_=bankbf[bk].rearrange("p (k t) -> p k t", k=8)),
                               [bobj[bk]], [o_xT[slot][ch]])
                    if after_group is not None:
                        after_group(g0 + GS, tiles)

            def load_wg(g):
                for j, c0 in enumerate((256 + 128 * g, 1024 + 128 * g, 1792 + 128 * g)):
                    dma("pool", wg_in[:, :, 128 * j:128 * (j + 1)],
                        win_d.rearrange("(k p) c -> p k c", p=128)[:, :, c0:c0 + 128], [], [o_wg], o_wg)

            def stage_p(b, g):
                pred_slot = b % 2
                cur_slot = 1 - pred_slot
                chunks = [(half, ch) for half in range(2) for ch in range(4)]

                def pmm(wcol0, slot, ch, bk):
                    for kt in range(8):
                        op("pe", lambda e, kt=kt: e.matmul(
                            bank[bk], lhsT=wg_in[:, kt, wcol0:wcol0 + 128],
                            rhs=xT[:, slot, kt, ch * 512:(ch + 1) * 512], start=(kt == 0), stop=(kt == 7)),
                           [o_wg, o_xT[slot][ch]], [bobj[bk]], signal=(kt == 7))

                def proj_mm(ci):
                    half, ch = chunks[ci]
                    slot = pred_slot if half == 0 else cur_slot
                    base = 3 * (ci % 2)
                    pmm(128, slot, ch, base)
                    pmm(256, slot, ch, base + 1)
                    if half == 1:
                        pmm(0, slot, ch, base + 2)

                def copies(ci):
                    half, ch = chunks[ci]
                    base = 3 * (ci % 2)
                    op("act", lambda e: e.activation(out=rbf[:, 0, :], in_=bank[base], func=AF.Copy),
                       [bobj[base]], [o_rbf[0]])
                    if half == 1:
                        op("act", lambda e: e.activation(out=rbf[:, 1, :], in_=bank[base + 2], func=AF.Copy),
                           [bobj[base + 2]], [o_rbf[1]])
                    op("act", lambda e: e.activation(out=vT[:, half, ch * 512:(ch + 1) * 512], in_=bank[base + 1],
                                                     func=AF.Copy), [bobj[base + 1]], [o_v])

                def rope_rest(ci):
                    half, ch = chunks[ci]
                    base = 3 * (ci % 2)
                    tab0 = half * BLK + ch * 512
                    jobs = [(0, base, 6, kT[:, half, ch * 512:(ch + 1) * 512], o_k)]
                    if half == 1:
                        jobs.append((1, base + 2, 7, qT[:, ch * 512:(ch + 1) * 512], o_q))
                    for (i, bk, wb, dst, dobj) in jobs:
                        op("pe", lambda e, i=i, wb=wb: e.matmul(bank[wb], lhsT=perm[:], rhs=rbf[:, i, :],
                                                                 start=True, stop=True),
                           [o_rbf[i], o_const], [bobj[wb]])
                    for (i, bk, wb, dst, dobj) in jobs:
                        op("dve", lambda e, i=i, bk=bk: e.tensor_tensor(
                            out=t1b[i], in0=bank[bk], in1=cst[:, tab0:tab0 + 512], op=ALU.mult),
                           [bobj[bk], o_tab], [o_t1[i]])
                        op("dve", lambda e, i=i, wb=wb: e.tensor_tensor(
                            out=t2b[i], in0=bank[wb], in1=snt[:, tab0:tab0 + 512], op=ALU.mult),
                           [bobj[wb], o_tab], [o_t2[i]])
                        op("pool", lambda e, i=i, dst=dst: e.tensor_tensor(out=dst, in0=t1b[i], in1=t2b[i],
                                                                            op=ALU.add),
                           [o_t1[i], o_t2[i]], [dobj])

                proj_mm(0)
                copies(0)
                for ci in range(len(chunks)):
                    if ci + 1 < len(chunks):
                        proj_mm(ci + 1)
                    rope_rest(ci)
                    if ci + 1 < len(chunks):
                        copies(ci + 1)

            actr = [0]

            def stage_a(b, g):
                first_cfg = True
                for d in (1, 4, 16):
                    nbh = 16 // d
                    span = 128 * d

                    def toks(r, n):
                        if n < 0:
                            return 0, (nbh - 1) * span + r
                        return 1, n * span + r

                    def sl(t0):
                        return slice(t0, t0 + 127 * d + 1, d)

                    klist = [(r, n) for r in range(d) for n in range(-1, nbh)]
                    vslot = {kn: i for i, kn in enumerate(klist)}
                    for j0 in range(0, len(klist), 8):
                        batch = klist[j0:j0 + 8]
                        bk = 6 + (actr[0] % 2)
                        actr[0] += 1
                        for j, (r, n) in enumerate(batch):
                            half, t0 = toks(r, n)
                            op("pe", lambda e, j=j, half=half, t0=t0, bk=bk: e.transpose(
                                bankbf[bk][:, j * 128:(j + 1) * 128], vT[:, half, sl(t0)], ident[:]),
                               [o_v, o_id], [bobj[bk]], signal=(j == len(batch) - 1))
                        nbt = len(batch)
                        dst = Vaug[:, j0:j0 + nbt, :].rearrange("p s (a c) -> p s a c", a=4)[:, :, 0:4:3, :]
                        src = bankbf[bk][:, 0:nbt * 128].rearrange("p (s h c) -> p s h c", s=nbt, h=2)
                        op("act", lambda e, dst=dst, src=src: e.activation(out=dst, in_=src, func=AF.Copy),
                           [bobj[bk]], [o_V])

                    qbs = [(r, n) for r in range(d) for n in range(nbh)]
                    pairs = [qbs[p0:p0 + 2] for p0 in range(0, len(qbs), 2)]
                    base_ctr = actr[0]
                    actr[0] += len(pairs)

                    def emit_qk(pi):
                        pair = pairs[pi]
                        s_ = (base_ctr + pi) % 2
                        halo = [b == 0 and n == 0 for (_, n) in pair]
                        mv = 0 if not any(halo) else (2 if all(halo) else 1)
                        assert not (halo[1] and not halo[0])
                        for h in range(2):
                            bk = 2 * s_ + h
                            for i, (r, n) in enumerate(pair):
                                _, tq = toks(r, n)
                                for kk, nk in enumerate((n - 1, n)):
                                    khalf, tk = toks(r, nk)
                                    col = (2 * i + kk) * 128
                                    last = (i == 1 and kk == 1)
                                    op("pe", lambda e, h=h, bk=bk, col=col, khalf=khalf, tk=tk, tq=tq: e.matmul(
                                        bank[bk][:, col:col + 128],
                                        lhsT=kT[64 * h:64 * h + 64, khalf, sl(tk)],
                                        rhs=qT[64 * h:64 * h + 64, sl(tq)], start=True, stop=True),
                                       [o_k, o_q], [bobj[bk]], signal=last)
                        for h in range(2):
                            bk = 2 * s_ + h
                            op("act", lambda e, h=h, bk=bk: e.activation(
                                out=PT[:, s_, h, :], in_=bank[bk], func=AF.Exp, scale=0.125),
                               [bobj[bk]], [o_PT[s_][h]])
                            op("dve", lambda e, h=h: e.tensor_tensor(
                                out=PT[:, s_, h, :], in0=PT[:, s_, h, :], in1=masks[:, mv, :], op=ALU.mult),
                               [o_PT[s_][h], o_const], [o_PT[s_][h]])

                    def emit_pv(pi, first):
                        pair = pairs[pi]
                        s_ = (base_ctr + pi) % 2
                        ob = 4 + s_
                        for i, (r, n) in enumerate(pair):
                            for h in range(2):
                                col = (2 * i + h) * 128
                                for kk, nk in enumerate((n - 1, n)):
                                    vs = vslot[(r, nk)]
                                    last = (i == 1 and h == 1 and kk == 1)
                                    op("pe", lambda e, vs=vs, h=h, col=col, kk=kk, i=i: e.matmul(
                                        bank[ob][:, col:col + 128],
                                        lhsT=Vaug[:, vs, 128 * h:128 * (h + 1)],
                                        rhs=PT[:, s_, h, (2 * i + kk) * 128:(2 * i + kk + 1) * 128],
                                        start=(kk == 0), stop=(kk == 1)),
                                       [o_V, o_PT[s_][h]], [bobj[ob]], signal=last)
                        for i, (r, n) in enumerate(pair):
                            _, tq = toks(r, n)
                            dst = acc[:, :, sl(tq)]
                            src = bank[ob][:, 256 * i:256 * (i + 1)].rearrange("p (h l) -> p h l", h=2)
                            if first:
                                op("act", lambda e, dst=dst, src=src: e.activation(out=dst, in_=src, func=AF.Copy),
                                   [bobj[ob]], [o_acc])
                            else:
                                op("dve", lambda e, dst=dst, src=src: e.tensor_tensor(
                                    out=dst, in0=src, in1=dst, op=ALU.add),
                                   [bobj[ob], o_acc], [o_acc])

                    emit_qk(0)
                    for pi in range(len(pairs)):
                        if pi + 1 < len(pairs):
                            emit_qk(pi + 1)
                        emit_pv(pi, first_cfg)
                    first_cfg = False
                rn_objs = [o_R, o_t1[1], o_t2[1]]
                kt = 2 + g
                for hv in range(2):
                    tsl = slice(hv * 1024, (hv + 1) * 1024)
                    op("act", lambda e: e.activation(out=Rn[0:64, :], in_=acc[64:128, 0, tsl], func=AF.Ln),
                       [o_acc], rn_objs)
                    op("act", lambda e: e.activation(out=Rn[64:128, :], in_=acc[0:64, 1, tsl], func=AF.Ln),
                       [o_acc], rn_objs)
                    op("act", lambda e: e.activation(out=Rn, in_=Rn, func=AF.Exp, scale=-1.0), rn_objs, rn_objs)
                    op("pool", lambda e: e.tensor_tensor(out=OT[0:64, kt, tsl], in0=acc[0:64, 0, tsl], in1=Rn[0:64, :],
                                                         op=ALU.mult), [o_acc] + rn_objs, [o_OT[kt]])
                    op("dve", lambda e: e.tensor_tensor(out=OT[64:128, kt, tsl], in0=acc[64:128, 1, tsl],
                                                        in1=Rn[64:128, :], op=ALU.mult), [o_acc] + rn_objs, [o_OT[kt]])

            o_ub, o_dT = Obj("ub"), Obj("dT")
            o_lvb = [Obj("lv0"), Obj("lv1")]

            def pool_begin(b):
                pred_slot = b % 2
                dma("pool", wpl_in, win_d.rearrange("(k p) c -> p k c", p=128)[:, :, 0:256],
                    [], [o_wpl, o_PT[0][0], o_PT[0][1], o_PT[1][0], o_PT[1][1]], o_wpl)
                for t in range(2):
                    bk = t
                    for kt in range(8):
                        op("pe", lambda e, kt=kt, bk=bk, t=t: e.matmul(
                            bank[bk][:, 0:16], lhsT=wpl_in[:, kt, 128 * t:128 * (t + 1)],
                            rhs=xT[:, pred_slot, kt, BLK - 16:BLK], start=(kt == 0), stop=(kt == 7)),
                           [o_wpl, o_xT[pred_slot][3]], [bobj[bk]], signal=(kt == 7))
                    op("act", lambda e, bk=bk, t=t: e.activation(out=ub[:, t, 0:16], in_=bank[bk][:, 0:16],
                                                                 func=AF.Copy), [bobj[bk]], [o_ub])

            def pool_chunk(b, ch):
                cur_slot = 1 - (b % 2)
                for t in range(2):
                    bk = t
                    for kt in range(8):
                        op("pe", lambda e, kt=kt, bk=bk, t=t: e.matmul(
                            bank[bk], lhsT=wpl_in[:, kt, 128 * t:128 * (t + 1)],
                            rhs=xT[:, cur_slot, kt, ch * 512:(ch + 1) * 512], start=(kt == 0), stop=(kt == 7)),
                           [o_wpl, o_xT[cur_slot][ch]], [bobj[bk]], signal=(kt == 7))
                    op("act", lambda e, bk=bk, t=t: e.activation(out=ub[:, t, 16:528], in_=bank[bk],
                                                                 func=AF.Copy), [bobj[bk]], [o_ub])
                prev, o_prev = ub, o_ub
                for k in range(4):
                    sh = 1 << k
                    lo = (1 << (k + 1)) - 1
                    cur, o_cur = lvb[k % 2], o_lvb[k % 2]
                    op("dve", lambda e, prev=prev, cur=cur, sh=sh, lo=lo: e.tensor_tensor(
                        out=cur[:, :, lo:528], in0=prev[:, :, lo:528], in1=prev[:, :, lo - sh:528 - sh],
                        op=ALU.add), [o_prev], [o_cur])
                    t, hh = k // 2, k % 2
                    p0 = 64 * hh
                    op("dve", lambda e, t=t, cur=cur, p0=p0: e.scalar_tensor_tensor(
                        out=dTb[p0:p0 + 64, t, :], in0=cur[p0:p0 + 64, t, 16:528],
                        scalar=invw[p0:p0 + 64, t:t + 1], in1=ub[p0:p0 + 64, t, 16:528],
                        op0=ALU.mult, op1=ALU.subtract), [o_cur, o_ub, o_const], [o_dT])
                    if b == 0 and ch == 0:
                        op("dve", lambda e, t=t, cur=cur, p0=p0: e.tensor_tensor(
                            out=ptmp[p0:p0 + 64, t, 0:16], in0=cur[p0:p0 + 64, t, 16:32],
                            in1=invc[p0:p0 + 64, t, :], op=ALU.mult), [o_cur, o_const], [o_dT])
                        op("dve", lambda e, t=t, p0=p0: e.tensor_tensor(
                            out=dTb[p0:p0 + 64, t, 0:16], in0=ptmp[p0:p0 + 64, t, 0:16],
                            in1=ub[p0:p0 + 64, t, 16:32], op=ALU.subtract), [o_dT, o_ub], [o_dT])
                    prev, o_prev = cur, o_cur
                for t in range(2):
                    bk = t
                    op("pe", lambda e, t=t, bk=bk: e.matmul(bank[bk], lhsT=wpbd[:, t, :], rhs=dTb[:, t, :],
                                                           start=True, stop=True),
                       [o_dT, o_const], [bobj[bk]])
                    op("act", lambda e, t=t, bk=bk: e.activation(
                        out=OT[:, t, ch * 512:(ch + 1) * 512], in_=bank[bk], func=AF.Copy,
                        scale=psc[:, t:t + 1]), [bobj[bk], o_const], [o_OT[t]])
                op("dve", lambda e: e.tensor_copy(out=ptmp[:, :, 16:32], in_=ub[:, :, 512:528]), [o_ub], [o_dT])
                op("act", lambda e: e.activation(out=ub[:, :, 0:16], in_=ptmp[:, :, 16:32], func=AF.Copy),
                   [o_dT], [o_ub])

            o_wo, o_g2, o_yb = Obj("wout"), Obj("gt2"), Obj("ys")

            def wout_view(b):
                return xT[:, b % 2, 0:4, :].rearrange("p k (a c) -> p (k a) c", a=2)

            def gt2_view(b):
                return xT[:, b % 2, 4, :].bitcast(F32)

            def prefetch_o(b):
                dma("pool", wout_view(b), wout_d.rearrange("(k p) c -> p k c", p=128),
                    [], [o_wo] + o_xT[b % 2], o_wo)
                dma("sp", gt2_view(b), gb_d[1], [], [o_g2] + o_xT[b % 2], o_g2)

            def stage_o(b):
                wout = wout_view(b)
                gt2 = gt2_view(b)
                o_xs2_l = [Obj("xs2_0"), Obj("xs2_1")]
                junk = scr[:, 5632:6144].bitcast(BF16)
                o_j = Obj("junk_o")
                for tt in range(BLK // 128):
                    st = tt % 4
                    xb = xs2[tt % 2]
                    yb = ys[0]
                    o_xb = o_xs2_l[tt % 2]
                    tok0 = b * BLK + tt * 128
                    dma("sp", xb, x_d[HALO + tok0:HALO + tok0 + 128, :], [], [o_xb], o_xb)
                    for hh in range(2):
                        bk = 2 * st + hh
                        for kt in range(8):
                            op("pe", lambda e, kt=kt, bk=bk, hh=hh, tt=tt: e.matmul(
                                bank[bk], lhsT=OT[:, kt, tt * 128:(tt + 1) * 128],
                                rhs=wout[:, kt, hh * 512:(hh + 1) * 512], start=(kt == 0), stop=(kt == 7)),
                               [o_OT[kt], o_wo], [bobj[bk]], signal=(kt == 7))
                    src = bank2[st]
                    rs, o_rs = rstd_from(lambda src=src: src, junk, [bobj[2 * st], bobj[2 * st + 1]], o_j)
                    op("dve", lambda e, yb=yb, src=src, rs=rs: e.scalar_tensor_tensor(
                        out=yb, in0=src, scalar=rs, in1=gt2, op0=ALU.mult, op1=ALU.mult),
                       [bobj[2 * st], bobj[2 * st + 1], o_rs, o_g2], [o_yb])
                    op("dve", lambda e, yb=yb, xb=xb: e.tensor_tensor(out=xb, in0=yb, in1=xb, op=ALU.add),
                       [o_yb, o_xb], [o_xb])
                    ti = tok0 // 128
                    dma("sp", out_d[tok0:tok0 + 128, :], xb, [o_xb], [o_out[ti]], o_xb)

            def fence():
                pass

            for b in range(2):
                dma("sp", cst[:], cs_d[:, b * BLK:b * BLK + 2 * BLK], [], [o_tab], ds_tab)
                dma("sp", snt[:], sn_d[:, b * BLK:b * BLK + 2 * BLK], [], [o_tab], ds_tab)
                C.barrier()
                load_wg(0)

                def hook(ndone, tiles, b=b):
                    npred = len(tiles) - BLK // 128
                    if ndone == npred or (npred == 0 and ndone == 2):
                        pass
                    if ndone == max(npred, 2) and not hook.started:
                        pool_begin(b)
                        hook.started = True
                    k = ndone - npred
                    if hook.started and k > 0 and k % 4 == 0:
                        while hook.next_ch < k // 4:
                            pool_chunk(b, hook.next_ch)
                            hook.next_ch += 1
                hook.started = False
                hook.next_ch = 0
                stage_x(b, prefetched=(b == 1), after_group=hook)
                C.barrier()
                op("pool", lambda e: e.memset(Vaug[:, :, 64:192], 1.0), [], [o_V])
                for g in range(6):
                    stage_p(b, g)
                    if g + 1 < 6:
                        load_wg(g + 1)
                    else:
                        prefetch_o(b)
                    stage_a(b, g)
                C.barrier()
                if dbg and b == dbg_blk:
                    ds_dbg = C.dsem("dbg")
                    od = Obj("dbg")
                    srcs = {"OT": OT[:].rearrange("p k t -> p (k t)"), "rsb": rsb[:], "kT": kT[:].rearrange("p h t -> p (h t)"),
                            "vT": vT[:].rearrange("p h t -> p (h t)"), "qT": qT[:], "acc": scr[:, 0:4096],
                            "xT": xT[:].rearrange("p s k t -> p (s k t)")}
                    for n in dbg_d:
                        dma("sp", dbg_d[n], srcs[n], [], [od], ds_dbg)
                    C.barrier()
                if b == 0:
                    stage_x(1, only_prefetch=True)
                stage_o(b)
                C.barrier()
            if dbg and "x1" in dbg:
                pass
        C.barrier()
        with ExitStack() as p2:
            sb2 = lambda n, s, d: p2.enter_context(nc.sbuf_tensor(n, list(s), d))
            wg = sb2("wg", (128, 8, DFF), BF16)
            wu = sb2("wu", (128, 8, DFF), BF16)
            wd = sb2("wd", (128, NFT, D), BF16)
            x1b = sb2("x1b", (128, 4, D), F32)
            yb2 = sb2("yb2", (128, D), F32)
            g3 = sb2("g3", (128, D), F32)
            g4 = sb2("g4", (128, D), F32)
            sg = sb2("sg", (128, 2, 512), F32)
            h2 = sb2("h2", (128, 2, D), BF16)
            h2T = sb2("h2T", (128, 8, 512), BF16)
            actT = sb2("actT", (128, NFT, 512), BF16)
            o_wgu, o_wd, o_g34 = Obj("wgu"), Obj("wd"), Obj("g34")
            FTG = [(0, 3), (3, 8), (8, 15), (15, NFT)]
            o_wgu_l = [Obj(f"wgu{i}") for i in range(len(FTG))]
            ftg_of = {}
            for gi, (f0, f1) in enumerate(FTG):
                for ft in range(f0, f1):
                    ftg_of[ft] = gi
                dma("pool", wg[:, :, f0 * 128:f1 * 128],
                    wg_d.rearrange("(k p) c -> p k c", p=128)[:, :, f0 * 128:f1 * 128], [], [o_wgu_l[gi]], o_wgu_l[gi])
                dma("pool", wu[:, :, f0 * 128:f1 * 128],
                    wu_d.rearrange("(k p) c -> p k c", p=128)[:, :, f0 * 128:f1 * 128], [], [o_wgu_l[gi]], o_wgu_l[gi])
            for ft in range(NFT):
                dma("pool", wd[:, ft, :], wd_d[ft * 128:(ft + 1) * 128, :], [], [o_wd], o_wd)
            dma("sp", g3[:], gb_d[2], [], [o_g34], o_g34)
            dma("sp", g4[:], gb_d[3], [], [o_g34], o_g34)
            o_h2T, o_act = Obj("h2T"), Obj("actT")
            o_sg = [Obj("sg0"), Obj("sg1")]
            o_h2 = [Obj("h2_0"), Obj("h2_1")]
            o_y2 = Obj("yb2")
            o_x1_l = [Obj(f"x1b{i}") for i in range(4)]
            NCH = TOK // 512

            def prep_nonpe(c, s_):
                ti = c * 4 + s_
                bi = s_ % 2
                xb, o_x1 = x1b[:, bi, :], o_x1_l[bi]
                dma("sp", xb, out_d[ti * 128:(ti + 1) * 128, :], [o_out[ti]], [o_x1], o_x1)
                hb, o_hb = h2[:, s_ % 2, :], o_h2[s_ % 2]
                rs, o_rs = rstd_from(lambda: xb, hb, [o_x1], o_hb)
                op("dve", lambda e: e.scalar_tensor_tensor(
                    out=hb, in0=xb, scalar=rs, in1=g3[:], op0=ALU.mult, op1=ALU.mult),
                   [o_x1, o_rs, o_g34], [o_hb])

            def prep_pe(c, s_):
                hb, o_hb = h2[:, s_ % 2, :], o_h2[s_ % 2]
                bk = s_ % 2
                for kt in range(8):
                    op("pe", lambda e, kt=kt: e.transpose(
                        bankbf[bk][:, kt * 128:(kt + 1) * 128], hb[:, kt * 128:(kt + 1) * 128], ident[:]),
                       [o_hb, o_id], [bobj[bk]], signal=(kt == 7))
                op("act", lambda e: e.activation(
                    out=h2T[:, :, s_ * 128:(s_ + 1) * 128],
                    in_=bankbf[bk].rearrange("p (k t) -> p k t", k=8), func=AF.Copy),
                   [bobj[bk]], [o_h2T])

            def gateup(c):
                for ft in range(NFT):
                    i = ft % 2
                    bg, bu = i, 2 + i
                    for (wt, bk) in ((wg, bg), (wu, bu)):
                        for kt in range(8):
                            op("pe", lambda e, wt=wt, bk=bk, kt=kt, ft=ft: e.matmul(
                                bank[bk], lhsT=wt[:, kt, ft * 128:(ft + 1) * 128], rhs=h2T[:, kt, :],
                                start=(kt == 0), stop=(kt == 7)),
                               [o_wgu_l[ftg_of[ft]], o_h2T], [bobj[bk]], signal=(kt == 7))
                    op("act", lambda e, i=i, bg=bg: e.activation(out=sg[:, i, :], in_=bank[bg], func=AF.Silu),
                       [bobj[bg]], [o_sg[i]])
                    op("dve", lambda e, i=i, bu=bu, ft=ft: e.tensor_tensor(
                        out=actT[:, ft, :], in0=bank[bu], in1=sg[:, i, :], op=ALU.mult),
                       [bobj[bu], o_sg[i]], [o_act])

            def epi_load(c, s_):
                ti = c * 4 + s_
                bi = 2 + s_ % 2
                dma("sp", x1b[:, bi, :], out_d[ti * 128:(ti + 1) * 128, :], [o_out[ti]], [o_x1_l[bi]], o_x1_l[bi])

            def down_mm(c, s_):
                st = 2 + (s_ % 2)
                for hh in range(2):
                    bk = 2 * st + hh
                    for ft in range(NFT):
                        op("pe", lambda e, ft=ft, bk=bk, hh=hh: e.matmul(
                            bank[bk], lhsT=actT[:, ft, s_ * 128:(s_ + 1) * 128],
                            rhs=wd[:, ft, hh * 512:(hh + 1) * 512], start=(ft == 0), stop=(ft == NFT - 1)),
                           [o_act, o_wd], [bobj[bk]], signal=(ft == NFT - 1))

            def epilogue(c, s_):
                ti = c * 4 + s_
                bi = 2 + s_ % 2
                xb, o_x1 = x1b[:, bi, :], o_x1_l[bi]
                st = 2 + (s_ % 2)
                src = bank2[st]
                rs, o_rs = rstd_from(lambda: src, yb2[:], [bobj[2 * st], bobj[2 * st + 1]], o_y2)
                op("dve", lambda e: e.scalar_tensor_tensor(
                    out=yb2[:], in0=src, scalar=rs, in1=g4[:], op0=ALU.mult, op1=ALU.mult),
                   [bobj[2 * st], bobj[2 * st + 1], o_rs, o_g34], [o_y2])
                op("pool", lambda e: e.tensor_tensor(out=xb, in0=yb2[:], in1=xb, op=ALU.add),
                   [o_y2, o_x1], [o_x1])
                dma("sp", out_d[ti * 128:(ti + 1) * 128, :], xb, [o_x1], [o_out[ti]], o_x1)

            for s_ in range(4):
                prep_nonpe(0, s_)
                prep_pe(0, s_)
            for c in range(NCH):
                if c + 1 < NCH:
                    prep_nonpe(c + 1, 0)
                gateup(c)
                for s_ in range(4):
                    epi_load(c, s_)
                    if c + 1 < NCH and s_ + 1 < 4:
                        prep_nonpe(c + 1, s_ + 1)
                    down_mm(c, s_)
                    if c + 1 < NCH:
                        prep_pe(c + 1, s_)
                    epilogue(c, s_)
            C.barrier()
    return nc


def _tables(hf):
    half = 32
    freqs = (np.float32(10000.0) ** (-np.arange(half, dtype=np.float32) * np.float32(2.0 / 64))).astype(np.float32)
    pos = (hf * TOK - HALO + np.arange(HALO + TOK)).astype(np.float32)
    ang = (pos[:, None] * freqs[None, :]).astype(np.float32).astype(np.float64)
    p = np.arange(128)
    fi = (p % 64) % 32
    sign = np.where((p % 64) < 32, -1.0, 1.0)
    cs = np.cos(ang[:, fi]).T
    sn = np.sin(ang[:, fi]).T * sign[:, None]
    return np.ascontiguousarray(cs.astype(np.float32)), np.ascontiguousarray(sn.astype(np.float32))


def _consts(hf):
    m = np.arange(128)
    partner = np.where((m % 64) < 32, m + 32, m - 32)
    perm = np.zeros((128, 128), np.float32)
    perm[partner, m] = 1.0
    k = np.arange(128)[:, None]
    q = np.arange(128)[None, :]
    mp = (k >= q).astype(np.float32)
    mc = (k <= q).astype(np.float32)
    flag = 0.0 if hf == 0 else 1.0
    mh = mp * flag
    masks = np.stack([np.concatenate([mp, mc, mp, mc], 1),
                      np.concatenate([mh, mc, mp, mc], 1),
                      np.concatenate([mh, mc, mh, mc], 1)]).astype(np.float32)
    wins = np.array([2, 4, 8, 16], np.float32)
    invwin = np.zeros((128, 2), np.float32)
    invcnt = np.zeros((128, 2, 16), np.float32)
    for t in range(2):
        for hh in range(2):
            w = wins[2 * t + hh]
            invwin[64 * hh:64 * hh + 64, t] = 1.0 / w
            posj = hf * TOK + np.arange(16)
            invcnt[64 * hh:64 * hh + 64, t, :] = (1.0 / np.minimum(posj + 1, w)).astype(np.float32)[None, :]
    return perm, masks, invwin, invcnt


_NC_CACHE = {}


def kernel(x, ln_pre_mix, w_in, w_pool, pool_scale, w_out, ln_post_mix, ln_pre_ffn, w_gate, w_up, w_down,
           ln_post_ffn):
    x = np.asarray(x, np.float32)
    f = lambda a: np.ascontiguousarray(np.asarray(a, np.float32))
    w_in0, w_out0, wg0, wu0, wd0 = f(w_in[0]), f(w_out[0]), f(w_gate[0]), f(w_up[0]), f(w_down[0])
    wp = np.asarray(w_pool[0], np.float32)
    wpbd = np.zeros((2, 128, 128), np.float32)
    for t in range(2):
        for hh in range(2):
            wpbd[t, 64 * hh:64 * hh + 64, 64 * hh:64 * hh + 64] = wp[2 * t + hh]
    gb = np.stack([np.broadcast_to(np.asarray(g[0], np.float32)[None, :], (128, D))
                   for g in (ln_pre_mix, ln_post_mix, ln_pre_ffn, ln_post_ffn)]).astype(np.float32)
    gb = np.ascontiguousarray(gb)
    ps_ = np.asarray(pool_scale[0], np.float32)
    pscale = np.ascontiguousarray(np.stack([ps_[0:128], ps_[128:256]], 1))
    in_maps = []
    for c in range(NCORES):
        b, hf = c // 2, c % 2
        xe = np.zeros((HALO + TOK, D), np.float32)
        xe[HALO:] = x[b, hf * TOK:(hf + 1) * TOK]
        if hf == 1:
            xe[:HALO] = x[b, TOK - HALO:TOK]
        cs, sn = _tables(hf)
        perm, masks, invwin, invcnt = _consts(hf)
        in_maps.append({"x": xe, "cs": cs, "sn": sn, "w_in": w_in0, "w_out": w_out0, "w_gate": wg0, "w_up": wu0,
                        "w_down": wd0, "wpool_bd": wpbd, "gb": gb, "pscale": pscale, "perm": perm, "masks": masks,
                        "invwin": invwin, "invcnt": invcnt})
    if "nc" not in _NC_CACHE:
        _NC_CACHE["nc"] = build_program()
    nc = _NC_CACHE["nc"]
    res = run_bass_kernel_spmd(nc, in_maps, core_ids=list(range(NCORES)))
    out = np.zeros((4, SEQ, D), np.float32)
    for c in range(NCORES):
        b, hf = c // 2, c % 2
        out[b, hf * TOK:(hf + 1) * TOK] = res.results[c]["out"]
    return out
```

```python
import os
from contextlib import ExitStack
import numpy as np
import concourse.bass as bass
import concourse.mybir as mybir
from concourse.bass_utils import run_bass_kernel_spmd

F32 = mybir.dt.float32
BF16 = mybir.dt.bfloat16
AF = mybir.ActivationFunctionType
ALU = mybir.AluOpType

D = 1024
SEQ = 8192
TOK = 4096
HALO = 2048
BLK = 2048
DFF = 2816
NFT = DFF // 128
EPS = 1e-6
NCORES = 8


class Sem:
    def __init__(self, h, name):
        self.h = h
        self.name = name
        self.count = 0


class Obj:
    __slots__ = ("name", "w", "r")

    def __init__(self, name):
        self.name = name
        self.w = {}
        self.r = {}


class Ctx:
    def __init__(self, nc, es):
        self.nc = nc
        self.es = es
        self.eng = {}
        for n, h in (("pe", nc.tensor), ("act", nc.scalar), ("dve", nc.vector), ("pool", nc.gpsimd), ("sp", nc.sync)):
            s = Sem(es.enter_context(nc.semaphore("sem_" + n)), "sem_" + n)
            self.eng[n] = (h, s, {})
        self.dsems = []
        self._ds = {}
        self.nwaits = 0

    def dsem(self, name):
        s = Sem(self.es.enter_context(self.nc.semaphore("dsem_" + name)), "dsem_" + name)
        self.dsems.append(s)
        return s

    def _wait(self, en, sem, val):
        h, _, waited = self.eng[en]
        if waited.get(sem.name, 0) >= val:
            return
        assert val <= sem.count, f"wait on unsignaled {sem.name} {val}>{sem.count} from {en}"
        h.wait_ge(sem.h, val)
        waited[sem.name] = val
        self.nwaits += 1

    def _deps(self, en, reads, writes, is_dma, ds=None):
        _, own, _ = self.eng[en]
        deps = {}

        def need(s, v, raw, waw=False):
            if s is own and not is_dma:
                if en == "pe":
                    return
            if waw and s is ds:
                return
            if deps.get(s.name, (None, 0))[1] < v:
                deps[s.name] = (s, v)

        for o in reads:
            for s, v in o.w.values():
                need(s, v, True)
        for o in writes:
            for s, v in o.w.values():
                need(s, v, False, True)
            for s, v in o.r.values():
                need(s, v, False)
        for s, v in deps.values():
            self._wait(en, s, v)

    def op(self, en, fn, reads=(), writes=(), signal=True):
        h, own, _ = self.eng[en]
        writes = list(writes) + [o for o in reads if o.name.startswith("bank") and o not in writes]
        self._deps(en, reads, writes, False)
        inst = fn(h)
        if signal:
            inst.then_inc(own.h, 1)
            own.count += 1
            val = own.count
        else:
            val = own.count + 1
        for o in reads:
            o.r[own.name] = (own, val)
        for o in writes:
            o.w[own.name] = (own, val)
        return inst

    def ds_of(self, obj):
        if obj.name not in self._ds:
            self._ds[obj.name] = self.dsem(obj.name)
        return self._ds[obj.name]

    def dma(self, en, out, in_, reads, writes, key):
        h, own, _ = self.eng[en]
        ds = key if isinstance(key, Sem) else self.ds_of(key)
        self._deps(en, reads, writes, True, ds)
        inst = h.dma_start(out=out, in_=in_)
        inst.then_inc(ds.h, 16)
        ds.count += 16
        for o in reads:
            o.r[ds.name] = (ds, ds.count)
        for o in writes:
            o.w[ds.name] = (ds, ds.count)
        return inst

    def barrier(self):
        sems = [e[1] for e in self.eng.values()] + self.dsems
        for en in self.eng:
            for s in sems:
                if s is self.eng[en][1]:
                    continue
                if s.count > 0:
                    self._wait(en, s, s.count)


def build_program(dbg=None, dbg_blk=0):
    nc = bass.Bass("TRN2", target_bir_lowering=False)
    dt_in = lambda n, s: nc.dram_tensor(n, list(s), F32, kind="ExternalInput").ap()
    x_d = dt_in("x", (HALO + TOK, D))
    cs_d = dt_in("cs", (128, HALO + TOK))
    sn_d = dt_in("sn", (128, HALO + TOK))
    win_d = dt_in("w_in", (D, 2560))
    wout_d = dt_in("w_out", (D, D))
    wg_d = dt_in("w_gate", (D, DFF))
    wu_d = dt_in("w_up", (D, DFF))
    wd_d = dt_in("w_down", (DFF, D))
    wpbd_d = dt_in("wpool_bd", (2, 128, 128))
    gb_d = dt_in("gb", (4, 128, D))
    psc_d = dt_in("pscale", (128, 2))
    perm_d = dt_in("perm", (128, 128))
    mask_d = dt_in("masks", (3, 128, 512))
    invw_d = dt_in("invwin", (128, 2))
    invc_d = dt_in("invcnt", (128, 2, 16))
    out_d = nc.dram_tensor("out", [TOK, D], F32, kind="ExternalOutput").ap()
    dbg_d = {}
    if dbg:
        for n, (s, dty) in dbg.items():
            dbg_d[n] = nc.dram_tensor("dbg_" + n, list(s), dty, kind="ExternalOutput").ap()

    with ExitStack() as es:
        C = Ctx(nc, es)
        op, dma = C.op, C.dma
        sb = lambda n, s, d: es.enter_context(nc.sbuf_tensor(n, list(s), d))
        ps = es.enter_context(nc.psum_tensor("ps", [128, 4096], F32))
        bank = [ps[:, 512 * i:512 * (i + 1)] for i in range(8)]
        bankbf = [ps[:, 512 * i:512 * (i + 1)].bitcast(BF16) for i in range(8)]
        bobj = [Obj(f"bank{i}") for i in range(8)]
        bank2 = [ps[:, 1024 * i:1024 * (i + 1)] for i in range(4)]

        ds_const = C.dsem("const")
        o_const = Obj("const")
        ident = sb("ident", (128, 128), BF16)
        mhalf = sb("mhalf", (128, 1), F32)
        ssb = sb("ssb", (128, 160), F32)
        tsb = sb("tsb", (128, 160), F32)
        rsb = sb("rsb", (128, 160), F32)
        o_ss = Obj("ss")
        p1 = ExitStack()
        sbc = lambda n, s, d: p1.enter_context(nc.sbuf_tensor(n, list(s), d))
        ident_f = sbc("ident_f", (128, 128), F32)
        perm = sbc("perm_sb", (128, 128), BF16)
        masks = sbc("masks_sb", (128, 3, 512), BF16)
        wpbd = sbc("wpbd_sb", (128, 2, 128), BF16)
        psc = sbc("psc_sb", (128, 2), F32)
        invw = sbc("invw_sb", (128, 2), F32)
        invc = sbc("invc_sb", (128, 2, 16), F32)
        dma("pool", perm[:], perm_d, [], [o_const], ds_const)
        dma("pool", masks[:], mask_d.rearrange("v p c -> p v c"), [], [o_const], ds_const)
        dma("pool", wpbd[:], wpbd_d.rearrange("t p c -> p t c"), [], [o_const], ds_const)
        ds_const2 = C.dsem("const2")
        dma("sp", psc[:], psc_d, [], [o_const], ds_const2)
        dma("sp", invw[:], invw_d, [], [o_const], ds_const2)
        dma("sp", invc[:], invc_d, [], [o_const], ds_const2)
        o_id = Obj("ident")
        op("pool", lambda e: e.memset(ident_f[:], 1.0), [], [o_id])
        op("pool", lambda e: e.affine_select(out=ident_f[:], in_=ident_f[:], pattern=[[-1, 128]],
                                             compare_op=ALU.is_equal, fill=0.0, base=0, channel_multiplier=1),
           [o_id], [o_id])
        op("dve", lambda e: e.tensor_copy(out=ident[:], in_=ident_f[:]), [o_id], [o_id])
        op("dve", lambda e: e.memset(mhalf[:], -0.5), [], [o_const])
        op("dve", lambda e: e.memset(ssb[:], 0.0), [], [o_ss])

        sscol = [0]

        def rstd_from(psrc_ap_fn, junk_ap, src_objs, junk_obj, n_feat=D):
            c = sscol[0]
            sscol[0] += 1
            assert c < 160
            oc = Obj(f"rs{sscol[0]}")
            op("act", lambda e: e.activation(out=junk_ap, in_=psrc_ap_fn(), func=AF.Square,
                                             accum_out=ssb[:, c:c + 1]),
               list(src_objs) + [o_ss], [junk_obj, oc])
            op("dve", lambda e: e.tensor_scalar(out=tsb[:, c:c + 1], in0=ssb[:, c:c + 1], scalar1=1.0 / n_feat,
                                                scalar2=EPS, op0=ALU.mult, op1=ALU.add), [oc], [oc])
            op("pool", lambda e: e.tensor_tensor(out=rsb[:, c:c + 1], in0=tsb[:, c:c + 1], in1=mhalf[:],
                                                 op=ALU.pow), [oc, o_const], [oc])
            return rsb[:, c:c + 1], oc

        def rstd_multi(items, n_feat=D):
            n = len(items)
            c0 = sscol[0]
            sscol[0] += n
            assert sscol[0] <= 160
            oc = Obj(f"rsm{c0}")
            for j, (src, junk_ap, src_objs, junk_obj) in enumerate(items):
                op("act", lambda e, src=src, junk_ap=junk_ap, j=j: e.activation(
                    out=junk_ap, in_=src, func=AF.Square, accum_out=ssb[:, c0 + j:c0 + j + 1]),
                   list(src_objs) + [o_ss], [junk_obj, oc])
            op("dve", lambda e: e.tensor_scalar(out=tsb[:, c0:c0 + n], in0=ssb[:, c0:c0 + n], scalar1=1.0 / n_feat,
                                                scalar2=EPS, op0=ALU.mult, op1=ALU.add), [oc], [oc])
            op("pool", lambda e: e.tensor_tensor(out=rsb[:, c0:c0 + n], in0=tsb[:, c0:c0 + n],
                                                 in1=mhalf[:].to_broadcast([128, n]), op=ALU.pow),
               [oc, o_const], [oc])
            return [rsb[:, c0 + j:c0 + j + 1] for j in range(n)], oc

        with p1:
            sb1 = lambda n, s, d: p1.enter_context(nc.sbuf_tensor(n, list(s), d))
            xT = sb1("xT", (128, 2, 8, BLK), BF16)
            OT = sb1("OT", (128, 8, BLK), BF16)
            cst = sb1("cst", (128, 2 * BLK), F32)
            snt = sb1("snt", (128, 2 * BLK), F32)
            kT = sb1("kT", (128, 2, BLK), BF16)
            vT = sb1("vT", (128, 2, BLK), BF16)
            qT = sb1("qT", (128, BLK), BF16)
            wg_in = sb1("wg_in", (128, 8, 384), BF16)
            scr = sb1("scr", (128, 10240), F32)
            rbf = sb1("rbf", (128, 2, 512), BF16)
            PT = sb1("PT", (128, 2, 2, 512), BF16)
            acc = scr[:, 0:4096].rearrange("p (h t) -> p h t", h=2)
            Vaug = scr[:, 4096:8192].bitcast(BF16).rearrange("p (s c) -> p s c", s=32)
            Rn = scr[:, 9216:10240]
            t1b = [scr[:, 8192:8704], scr[:, 9216:9728]]
            t2b = [scr[:, 8704:9216], scr[:, 9728:10240]]
            NXS = 5
            xs = [scr[:, 1024 * i:1024 * (i + 1)] for i in range(NXS)]
            NXN = 2
            xn = [scr[:, 5120 + 512 * i:5120 + 512 * (i + 1)].bitcast(BF16) for i in range(NXN)]
            gt = qT[:].bitcast(F32)
            gt2 = scr[:, 6144:7168]
            xs2 = [scr[:, 7168 + 1024 * i:7168 + 1024 * (i + 1)] for i in range(2)]
            ys = [scr[:, 9216:10240]] * 2
            ub = scr[:, 6144:7200].rearrange("p (t c) -> p t c", t=2)
            lvb = [scr[:, 7200 + 1056 * i:7200 + 1056 * (i + 1)].rearrange("p (t c) -> p t c", t=2) for i in range(2)]
            dTb = scr[:, 9312:9824].bitcast(BF16).rearrange("p (t c) -> p t c", t=2)
            ptmp = scr[:, 9824:9888].rearrange("p (t c) -> p t c", t=2)
            wpl_in = PT[:].rearrange("p a b c -> p (a b c)").rearrange("p (k c) -> p k c", k=8)

            o_scr = Obj("scr")
            o_xT = [[Obj(f"xT{s}_{c}") for c in range(4)] for s in range(2)]
            o_OT = [Obj(f"OT{k}") for k in range(8)]
            o_tab = Obj("tab")
            o_k, o_v, o_q = Obj("kT"), Obj("vT"), Obj("qT")
            o_wg, o_wpl = Obj("wg_in"), Obj("wpl_in")
            o_acc, o_V, o_R = Obj("acc"), Obj("Vaug"), Obj("Rn")
            o_t1 = [Obj("t1k"), Obj("t1q")]
            o_t2 = [Obj("t2k"), Obj("t2q")]
            o_rbf = [Obj("rbf0"), Obj("rbf1")]
            o_PT = [[Obj(f"PT{s}{h}") for h in range(2)] for s in range(2)]
            ds_tab = C.dsem("tab")
            o_out = [Obj(f"out{t}") for t in range(TOK // 128)]

            def stage_fence(reads_scr=False):
                pass

            o_g = Obj("gt")
            o_xs_l = [Obj(f"xs{i}") for i in range(NXS)]
            o_xn_l = [Obj(f"xn{i}") for i in range(NXN)]

            xnext = [0, 0]

            def stage_x(b, prefetched=False, only_prefetch=False, after_group=None):
                jobs = [(0, 0), (BLK, 1)] if b == 0 else [(2 * BLK, 0)]
                tiles = [(e0 + tt * 128, slot, tt) for e0, slot in jobs for tt in range(BLK // 128)]
                GS = 2

                def ensure_loaded(upto):
                    while xnext[b] <= min(upto, len(tiles) - 1):
                        r0 = tiles[xnext[b]][0]
                        i = xnext[b] % NXS
                        dma("sp", xs[i], x_d[r0:r0 + 128, :], [], [o_xs_l[i]], o_xs_l[i])
                        xnext[b] += 1

                if not prefetched:
                    dma("sp", gt, gb_d[0], [], [o_g, o_q], o_g)
                ensure_loaded(3)
                if only_prefetch:
                    return
                def rstd_group(g0):
                    n = len(tiles[g0:g0 + GS])
                    items = [(xs[(g0 + j) % NXS], bank2[1], [o_xs_l[(g0 + j) % NXS], bobj[2]], bobj[3])
                             for j in range(n)]
                    return rstd_multi(items)

                rs_cache = {0: rstd_group(0)}
                for g0 in range(0, len(tiles), GS):
                    ensure_loaded(g0 + 2 * GS - 1 + 1)
                    grp = tiles[g0:g0 + GS]
                    rss, o_rs = rs_cache.pop(g0)
                    for j, (r0, slot, tt) in enumerate(grp):
                        it = g0 + j
                        xb, o_xs = xs[it % NXS], o_xs_l[it % NXS]
                        nb, o_xn = xn[it % NXN], o_xn_l[it % NXN]
                        op("dve", lambda e, xb=xb, nb=nb, rs=rss[j]: e.scalar_tensor_tensor(
                            out=nb, in0=xb, scalar=rs, in1=gt, op0=ALU.mult, op1=ALU.mult),
                           [o_xs, o_rs, o_g], [o_xn])
                    if g0 + GS < len(tiles):
                        rs_cache[g0 + GS] = rstd_group(g0 + GS)
                    for j, (r0, slot, tt) in enumerate(grp):
                        it = g0 + j
                        nb, o_xn = xn[it % NXN], o_xn_l[it % NXN]
                        bk = 4 + (it % 4)
                        for kt in range(8):
                            op("pe", lambda e, nb=nb, kt=kt, bk=bk: e.transpose(
                                bankbf[bk][:, kt * 128:(kt + 1) * 128], nb[:, kt * 128:(kt + 1) * 128], ident[:]),
                               [o_xn, o_id], [bobj[bk]], signal=(kt == 7))
                        ch = tt // 4
                        if it % 2 == 0:
                            op("act", lambda e, bk=bk, slot=slot, tt=tt: e.activation(
                                out=xT[:, slot, :, tt * 128:(tt + 1) * 128],
                                in_=bankbf[bk].rearrange("p (k t) -> p k t", k=8), func=AF.Copy),
                               [bobj[bk]], [o_xT[slot][ch]])
                        else:
                            op("dve", lambda e, bk=bk, slot=slot, tt=tt: e.tensor_copy(
                                out=xT[:, slot, :, tt * 128:(tt + 1) * 128],
                                in_=bankbf[bk].rearrange("p (k t) -> p k t", k=8)),
                               [bobj[bk]], [o_xT[slot][ch]])
                    if after_group is not None:
                        after_group(g0 + GS, tiles)

            def load_wg(g):
                for j, c0 in enumerate((256 + 128 * g, 1024 + 128 * g, 1792 + 128 * g)):
                    dma("pool", wg_in[:, :, 128 * j:128 * (j + 1)],
                        win_d.rearrange("(k p) c -> p k c", p=128)[:, :, c0:c0 + 128], [], [o_wg], o_wg)

            def stage_p(b, g):
                pred_slot = b % 2
                cur_slot = 1 - pred_slot
                chunks = [(half, ch) for half in range(2) for ch in range(4)]

                def pmm(wcol0, slot, ch, bk):
                    for kt in range(8):
                        op("pe", lambda e, kt=kt: e.matmul(
                            bank[bk], lhsT=wg_in[:, kt, wcol0:wcol0 + 128],
                            rhs=xT[:, slot, kt, ch * 512:(ch + 1) * 512], start=(kt == 0), stop=(kt == 7)),
                           [o_wg, o_xT[slot][ch]], [bobj[bk]], signal=(kt == 7))

                def proj_mm(ci):
                    half, ch = chunks[ci]
                    slot = pred_slot if half == 0 else cur_slot
                    base = 3 * (ci % 2)
                    pmm(128, slot, ch, base)
                    pmm(256, slot, ch, base + 1)
                    if half == 1:
                        pmm(0, slot, ch, base + 2)

                def copies(ci):
                    half, ch = chunks[ci]
                    base = 3 * (ci % 2)
                    op("act", lambda e: e.activation(out=rbf[:, 0, :], in_=bank[base], func=AF.Copy),
                       [bobj[base]], [o_rbf[0]])
                    if half == 1:
                        op("act", lambda e: e.activation(out=rbf[:, 1, :], in_=bank[base + 2], func=AF.Copy),
                           [bobj[base + 2]], [o_rbf[1]])
                    op("act", lambda e: e.activation(out=vT[:, half, ch * 512:(ch + 1) * 512], in_=bank[base + 1],
                                                     func=AF.Copy), [bobj[base + 1]], [o_v])

                def rope_rest(ci):
                    half, ch = chunks[ci]
                    base = 3 * (ci % 2)
                    tab0 = half * BLK + ch * 512
                    jobs = [(0, base, 6, kT[:, half, ch * 512:(ch + 1) * 512], o_k)]
                    if half == 1:
                        jobs.append((1, base + 2, 7, qT[:, ch * 512:(ch + 1) * 512], o_q))
                    for (i, bk, wb, dst, dobj) in jobs:
                        op("pe", lambda e, i=i, wb=wb: e.matmul(bank[wb], lhsT=perm[:], rhs=rbf[:, i, :],
                                                                 start=True, stop=True),
                           [o_rbf[i], o_const], [bobj[wb]])
                    for (i, bk, wb, dst, dobj) in jobs:
                        op("dve", lambda e, i=i, bk=bk: e.tensor_tensor(
                            out=t1b[i], in0=bank[bk], in1=cst[:, tab0:tab0 + 512], op=ALU.mult),
                           [bobj[bk], o_tab], [o_t1[i]])
                        op("dve", lambda e, i=i, wb=wb: e.tensor_tensor(
                            out=t2b[i], in0=bank[wb], in1=snt[:, tab0:tab0 + 512], op=ALU.mult),
                           [bobj[wb], o_tab], [o_t2[i]])
                        op("pool", lambda e, i=i, dst=dst: e.tensor_tensor(out=dst, in0=t1b[i], in1=t2b[i],
                                                                            op=ALU.add),
                           [o_t1[i], o_t2[i]], [dobj])

                proj_mm(0)
                copies(0)
                for ci in range(len(chunks)):
                    if ci + 1 < len(chunks):
                        proj_mm(ci + 1)
                    rope_rest(ci)
                    if ci + 1 < len(chunks):
                        copies(ci + 1)

            actr = [0]

            def stage_a(b, g):
                first_cfg = True
                for d in (1, 4, 16):
                    nbh = 16 // d
                    span = 128 * d

                    def toks(r, n):
                        if n < 0:
                            return 0, (nbh - 1) * span + r
                        return 1, n * span + r

                    def sl(t0):
                        return slice(t0, t0 + 127 * d + 1, d)

                    klist = [(r, n) for r in range(d) for n in range(-1, nbh)]
                    vslot = {kn: i for i, kn in enumerate(klist)}
                    for j0 in range(0, len(klist), 8):
                        batch = klist[j0:j0 + 8]
                        bk = 6 + (actr[0] % 2)
                        actr[0] += 1
                        for j, (r, n) in enumerate(batch):
                            half, t0 = toks(r, n)
                            op("pe", lambda e, j=j, half=half, t0=t0, bk=bk: e.transpose(
                                bankbf[bk][:, j * 128:(j + 1) * 128], vT[:, half, sl(t0)], ident[:]),
                               [o_v, o_id], [bobj[bk]], signal=(j == len(batch) - 1))
                        nbt = len(batch)
                        dst = Vaug[:, j0:j0 + nbt, :].rearrange("p s (a c) -> p s a c", a=4)[:, :, 0:4:3, :]
                        src = bankbf[bk][:, 0:nbt * 128].rearrange("p (s h c) -> p s h c", s=nbt, h=2)
                        op("act", lambda e, dst=dst, src=src: e.activation(out=dst, in_=src, func=AF.Copy),
                           [bobj[bk]], [o_V])

                    qbs = [(r, n) for r in range(d) for n in range(nbh)]
                    pairs = [qbs[p0:p0 + 2] for p0 in range(0, len(qbs), 2)]
                    base_ctr = actr[0]
                    actr[0] += len(pairs)

                    def emit_qk(pi):
                        pair = pairs[pi]
                        s_ = (base_ctr + pi) % 2
                        halo = [b == 0 and n == 0 for (_, n) in pair]
                        mv = 0 if not any(halo) else (2 if all(halo) else 1)
                        assert not (halo[1] and not halo[0])
                        for h in range(2):
                            bk = 2 * s_ + h
                            for i, (r, n) in enumerate(pair):
                                _, tq = toks(r, n)
                                for kk, nk in enumerate((n - 1, n)):
                                    khalf, tk = toks(r, nk)
                                    col = (2 * i + kk) * 128
                                    last = (i == 1 and kk == 1)
                                    op("pe", lambda e, h=h, bk=bk, col=col, khalf=khalf, tk=tk, tq=tq: e.matmul(
                                        bank[bk][:, col:col + 128],
                                        lhsT=kT[64 * h:64 * h + 64, khalf, sl(tk)],
                                        rhs=qT[64 * h:64 * h + 64, sl(tq)], start=True, stop=True),
                                       [o_k, o_q], [bobj[bk]], signal=last)
                        for h in range(2):
                            bk = 2 * s_ + h
                            op("act", lambda e, h=h, bk=bk: e.activation(
                                out=PT[:, s_, h, :], in_=bank[bk], func=AF.Exp, scale=0.125),
                               [bobj[bk]], [o_PT[s_][h]])
                            op("dve", lambda e, h=h: e.tensor_tensor(
                                out=PT[:, s_, h, :], in0=PT[:, s_, h, :], in1=masks[:, mv, :], op=ALU.mult),
                               [o_PT[s_][h], o_const], [o_PT[s_][h]])

                    def emit_pv(pi, first):
                        pair = pairs[pi]
                        s_ = (base_ctr + pi) % 2
                        ob = 4 + s_
                        for i, (r, n) in enumerate(pair):
                            for h in range(2):
                                col = (2 * i + h) * 128
                                for kk, nk in enumerate((n - 1, n)):
                                    vs = vslot[(r, nk)]
                                    last = (i == 1 and h == 1 and kk == 1)
                                    op("pe", lambda e, vs=vs, h=h, col=col, kk=kk, i=i: e.matmul(
                                        bank[ob][:, col:col + 128],
                                        lhsT=Vaug[:, vs, 128 * h:128 * (h + 1)],
                                        rhs=PT[:, s_, h, (2 * i + kk) * 128:(2 * i + kk + 1) * 128],
                                        start=(kk == 0), stop=(kk == 1)),
                                       [o_V, o_PT[s_][h]], [bobj[ob]], signal=last)
                        for i, (r, n) in enumerate(pair):
                            _, tq = toks(r, n)
                            dst = acc[:, :, sl(tq)]
                            src = bank[ob][:, 256 * i:256 * (i + 1)].rearrange("p (h l) -> p h l", h=2)
                            if first:
                                op("act", lambda e, dst=dst, src=src: e.activation(out=dst, in_=src, func=AF.Copy),
                                   [bobj[ob]], [o_acc])
                            else:
                                op("dve", lambda e, dst=dst, src=src: e.tensor_tensor(
                                    out=dst, in0=src, in1=dst, op=ALU.add),
                                   [bobj[ob], o_acc], [o_acc])

                    emit_qk(0)
                    for pi in range(len(pairs)):
                        if pi + 1 < len(pairs):
                            emit_qk(pi + 1)
                        emit_pv(pi, first_cfg)
                    first_cfg = False
                rn_objs = [o_R, o_t1[1], o_t2[1]]
                kt = 2 + g
                for hv in range(2):
                    tsl = slice(hv * 1024, (hv + 1) * 1024)
                    op("act", lambda e: e.activation(out=Rn[0:64, :], in_=acc[64:128, 0, tsl], func=AF.Ln),
                       [o_acc], rn_objs)
                    op("act", lambda e: e.activation(out=Rn[64:128, :], in_=acc[0:64, 1, tsl], func=AF.Ln),
                       [o_acc], rn_objs)
                    op("act", lambda e: e.activation(out=Rn, in_=Rn, func=AF.Exp, scale=-1.0), rn_objs, rn_objs)
                    op("pool", lambda e: e.tensor_tensor(out=OT[0:64, kt, tsl], in0=acc[0:64, 0, tsl], in1=Rn[0:64, :],
                                                         op=ALU.mult), [o_acc] + rn_objs, [o_OT[kt]])
                    op("dve", lambda e: e.tensor_tensor(out=OT[64:128, kt, tsl], in0=acc[64:128, 1, tsl],
                                                        in1=Rn[64:128, :], op=ALU.mult), [o_acc] + rn_objs, [o_OT[kt]])

            o_ub, o_dT = Obj("ub"), Obj("dT")
            o_lvb = [Obj("lv0"), Obj("lv1")]

            def pool_begin(b):
                pred_slot = b % 2
                dma("pool", wpl_in, win_d.rearrange("(k p) c -> p k c", p=128)[:, :, 0:256],
                    [], [o_wpl, o_PT[0][0], o_PT[0][1], o_PT[1][0], o_PT[1][1]], o_wpl)
                for t in range(2):
                    bk = t
                    for kt in range(8):
                        op("pe", lambda e, kt=kt, bk=bk, t=t: e.matmul(
                            bank[bk][:, 0:16], lhsT=wpl_in[:, kt, 128 * t:128 * (t + 1)],
                            rhs=xT[:, pred_slot, kt, BLK - 16:BLK], start=(kt == 0), stop=(kt == 7)),
                           [o_wpl, o_xT[pred_slot][3]], [bobj[bk]], signal=(kt == 7))
                    op("act", lambda e, bk=bk, t=t: e.activation(out=ub[:, t, 0:16], in_=bank[bk][:, 0:16],
                                                                 func=AF.Copy), [bobj[bk]], [o_ub])

            def pool_chunk(b, ch):
                cur_slot = 1 - (b % 2)
                for t in range(2):
                    bk = t
                    for kt in range(8):
                        op("pe", lambda e, kt=kt, bk=bk, t=t: e.matmul(
                            bank[bk], lhsT=wpl_in[:, kt, 128 * t:128 * (t + 1)],
                            rhs=xT[:, cur_slot, kt, ch * 512:(ch + 1) * 512], start=(kt == 0), stop=(kt == 7)),
                           [o_wpl, o_xT[cur_slot][ch]], [bobj[bk]], signal=(kt == 7))
                    op("act", lambda e, bk=bk, t=t: e.activation(out=ub[:, t, 16:528], in_=bank[bk],
                                                                 func=AF.Copy), [bobj[bk]], [o_ub])
                prev, o_prev = ub, o_ub
                for k in range(4):
                    sh = 1 << k
                    lo = (1 << (k + 1)) - 1
                    cur, o_cur = lvb[k % 2], o_lvb[k % 2]
                    op("dve", lambda e, prev=prev, cur=cur, sh=sh, lo=lo: e.tensor_tensor(
                        out=cur[:, :, lo:528], in0=prev[:, :, lo:528], in1=prev[:, :, lo - sh:528 - sh],
                        op=ALU.add), [o_prev], [o_cur])
                    t, hh = k // 2, k % 2
                    p0 = 64 * hh
                    op("dve", lambda e, t=t, cur=cur, p0=p0: e.scalar_tensor_tensor(
                        out=dTb[p0:p0 + 64, t, :], in0=cur[p0:p0 + 64, t, 16:528],
                        scalar=invw[p0:p0 + 64, t:t + 1], in1=ub[p0:p0 + 64, t, 16:528],
                        op0=ALU.mult, op1=ALU.subtract), [o_cur, o_ub, o_const], [o_dT])
                    if b == 0 and ch == 0:
                        op("dve", lambda e, t=t, cur=cur, p0=p0: e.tensor_tensor(
                            out=ptmp[p0:p0 + 64, t, 0:16], in0=cur[p0:p0 + 64, t, 16:32],
                            in1=invc[p0:p0 + 64, t, :], op=ALU.mult), [o_cur, o_const], [o_dT])
                        op("dve", lambda e, t=t, p0=p0: e.tensor_tensor(
                            out=dTb[p0:p0 + 64, t, 0:16], in0=ptmp[p0:p0 + 64, t, 0:16],
                            in1=ub[p0:p0 + 64, t, 16:32], op=ALU.subtract), [o_dT, o_ub], [o_dT])
                    prev, o_prev = cur, o_cur
                for t in range(2):
                    bk = t
                    op("pe", lambda e, t=t, bk=bk: e.matmul(bank[bk], lhsT=wpbd[:, t, :], rhs=dTb[:, t, :],
                                                           start=True, stop=True),
                       [o_dT, o_const], [bobj[bk]])
                    op("act", lambda e, t=t, bk=bk: e.activation(
                        out=OT[:, t, ch * 512:(ch + 1) * 512], in_=bank[bk], func=AF.Copy,
                        scale=psc[:, t:t + 1]), [bobj[bk], o_const], [o_OT[t]])
                op("dve", lambda e: e.tensor_copy(out=ptmp[:, :, 16:32], in_=ub[:, :, 512:528]), [o_ub], [o_dT])
                op("act", lambda e: e.activation(out=ub[:, :, 0:16], in_=ptmp[:, :, 16:32], func=AF.Copy),
                   [o_dT], [o_ub])

            o_wo, o_g2, o_yb = Obj("wout"), Obj("gt2"), Obj("ys")

            def wout_view(b):
                return xT[:, b % 2, 0:4, :].rearrange("p k (a c) -> p (k a) c", a=2)

            def gt2_view(b):
                return xT[:, b % 2, 4, :].bitcast(F32)

            def prefetch_o(b):
                dma("pool", wout_view(b), wout_d.rearrange("(k p) c -> p k c", p=128),
                    [], [o_wo] + o_xT[b % 2], o_wo)
                dma("sp", gt2_view(b), gb_d[1], [], [o_g2] + o_xT[b % 2], o_g2)

            def stage_o(b):
                wout = wout_view(b)
                gt2 = gt2_view(b)
                o_xs2_l = [Obj("xs2_0"), Obj("xs2_1")]
                junk = scr[:, 5632:6144].bitcast(BF16)
                o_j = Obj("junk_o")
                for tt in range(BLK // 128):
                    st = tt % 4
                    xb = xs2[tt % 2]
                    yb = ys[0]
                    o_xb = o_xs2_l[tt % 2]
                    tok0 = b * BLK + tt * 128
                    dma("sp", xb, x_d[HALO + tok0:HALO + tok0 + 128, :], [], [o_xb], o_xb)
                    for hh in range(2):
                        bk = 2 * st + hh
                        for kt in range(8):
                            op("pe", lambda e, kt=kt, bk=bk, hh=hh, tt=tt: e.matmul(
                                bank[bk], lhsT=OT[:, kt, tt * 128:(tt + 1) * 128],
                                rhs=wout[:, kt, hh * 512:(hh + 1) * 512], start=(kt == 0), stop=(kt == 7)),
                               [o_OT[kt], o_wo], [bobj[bk]], signal=(kt == 7))
                    src = bank2[st]
                    rs, o_rs = rstd_from(lambda src=src: src, junk, [bobj[2 * st], bobj[2 * st + 1]], o_j)
                    op("dve", lambda e, yb=yb, src=src, rs=rs: e.scalar_tensor_tensor(
                        out=yb, in0=src, scalar=rs, in1=gt2, op0=ALU.mult, op1=ALU.mult),
                       [bobj[2 * st], bobj[2 * st + 1], o_rs, o_g2], [o_yb])
                    op("dve", lambda e, yb=yb, xb=xb: e.tensor_tensor(out=xb, in0=yb, in1=xb, op=ALU.add),
                       [o_yb, o_xb], [o_xb])
                    ti = tok0 // 128
                    dma("sp", out_d[tok0:tok0 + 128, :], xb, [o_xb], [o_out[ti]], o_xb)

            def fence():
                pass

            for b in range(2):
                dma("sp", cst[:], cs_d[:, b * BLK:b * BLK + 2 * BLK], [], [o_tab], ds_tab)
                dma("sp", snt[:], sn_d[:, b * BLK:b * BLK + 2 * BLK], [], [o_tab], ds_tab)
                load_wg(0)

                def hook(ndone, tiles, b=b):
                    npred = len(tiles) - BLK // 128
                    if ndone == npred or (npred == 0 and ndone == 2):
                        pass
                    if ndone == max(npred, 2) and not hook.started:
                        pool_begin(b)
                        hook.started = True
                    k = ndone - npred
                    if hook.started and k > 0 and k % 4 == 0:
                        while hook.next_ch < k // 4:
                            pool_chunk(b, hook.next_ch)
                            hook.next_ch += 1
                hook.started = False
                hook.next_ch = 0
                stage_x(b, prefetched=(b == 1), after_group=hook)
                C.barrier()
                op("pool", lambda e: e.memset(Vaug[:, :, 64:192], 1.0), [], [o_V])
                for g in range(6):
                    stage_p(b, g)
                    if g + 1 < 6:
                        load_wg(g + 1)
                    else:
                        prefetch_o(b)
                    stage_a(b, g)
                C.barrier()
                if dbg and b == dbg_blk:
                    ds_dbg = C.dsem("dbg")
                    od = Obj("dbg")
                    srcs = {"OT": OT[:].rearrange("p k t -> p (k t)"), "rsb": rsb[:], "kT": kT[:].rearrange("p h t -> p (h t)"),
                            "vT": vT[:].rearrange("p h t -> p (h t)"), "qT": qT[:], "acc": scr[:, 0:4096],
                            "xT": xT[:].rearrange("p s k t -> p (s k t)")}
                    for n in dbg_d:
                        dma("sp", dbg_d[n], srcs[n], [], [od], ds_dbg)
                    C.barrier()
                if b == 0:
                    stage_x(1, only_prefetch=True)
                stage_o(b)
                C.barrier()
            if dbg and "x1" in dbg:
                pass
        C.barrier()
        with ExitStack() as p2:
            sb2 = lambda n, s, d: p2.enter_context(nc.sbuf_tensor(n, list(s), d))
            wg = sb2("wg", (128, 8, DFF), BF16)
            wu = sb2("wu", (128, 8, DFF), BF16)
            wd = sb2("wd", (128, NFT, D), BF16)
            x1b = sb2("x1b", (128, 4, D), F32)
            yb2 = sb2("yb2", (128, D), F32)
            g3 = sb2("g3", (128, D), F32)
            g4 = sb2("g4", (128, D), F32)
            sg = sb2("sg", (128, 2, 512), F32)
            h2 = sb2("h2", (128, 2, D), BF16)
            h2T = sb2("h2T", (128, 8, 512), BF16)
            actT = sb2("actT", (128, NFT, 512), BF16)
            o_wgu, o_wd, o_g34 = Obj("wgu"), Obj("wd"), Obj("g34")
            FTG = [(0, 3), (3, 8), (8, 15), (15, NFT)]
            o_wgu_l = [Obj(f"wgu{i}") for i in range(len(FTG))]
            ftg_of = {}
            for gi, (f0, f1) in enumerate(FTG):
                for ft in range(f0, f1):
                    ftg_of[ft] = gi
                dma("pool", wg[:, :, f0 * 128:f1 * 128],
                    wg_d.rearrange("(k p) c -> p k c", p=128)[:, :, f0 * 128:f1 * 128], [], [o_wgu_l[gi]], o_wgu_l[gi])
                dma("pool", wu[:, :, f0 * 128:f1 * 128],
                    wu_d.rearrange("(k p) c -> p k c", p=128)[:, :, f0 * 128:f1 * 128], [], [o_wgu_l[gi]], o_wgu_l[gi])
            for ft in range(NFT):
                dma("pool", wd[:, ft, :], wd_d[ft * 128:(ft + 1) * 128, :], [], [o_wd], o_wd)
            dma("sp", g3[:], gb_d[2], [], [o_g34], o_g34)
            dma("sp", g4[:], gb_d[3], [], [o_g34], o_g34)
            o_h2T, o_act = Obj("h2T"), Obj("actT")
            o_sg = [Obj("sg0"), Obj("sg1")]
            o_h2 = [Obj("h2_0"), Obj("h2_1")]
            o_y2 = Obj("yb2")
            o_x1_l = [Obj(f"x1b{i}") for i in range(4)]
            NCH = TOK // 512

            def prep_nonpe(c, s_):
                ti = c * 4 + s_
                bi = s_ % 2
                xb, o_x1 = x1b[:, bi, :], o_x1_l[bi]
                dma("sp", xb, out_d[ti * 128:(ti + 1) * 128, :], [o_out[ti]], [o_x1], o_x1)
                hb, o_hb = h2[:, s_ % 2, :], o_h2[s_ % 2]
                rs, o_rs = rstd_from(lambda: xb, hb, [o_x1], o_hb)
                op("dve", lambda e: e.scalar_tensor_tensor(
                    out=hb, in0=xb, scalar=rs, in1=g3[:], op0=ALU.mult, op1=ALU.mult),
                   [o_x1, o_rs, o_g34], [o_hb])

            def prep_pe(c, s_):
                hb, o_hb = h2[:, s_ % 2, :], o_h2[s_ % 2]
                bk = s_ % 2
                for kt in range(8):
                    op("pe", lambda e, kt=kt: e.transpose(
                        bankbf[bk][:, kt * 128:(kt + 1) * 128], hb[:, kt * 128:(kt + 1) * 128], ident[:]),
                       [o_hb, o_id], [bobj[bk]], signal=(kt == 7))
                op("act", lambda e: e.activation(
                    out=h2T[:, :, s_ * 128:(s_ + 1) * 128],
                    in_=bankbf[bk].rearrange("p (k t) -> p k t", k=8), func=AF.Copy),
                   [bobj[bk]], [o_h2T])

            def gateup(c):
                for ft in range(NFT):
                    i = ft % 2
                    bg, bu = i, 2 + i
                    for (wt, bk) in ((wg, bg), (wu, bu)):
                        for kt in range(8):
                            op("pe", lambda e, wt=wt, bk=bk, kt=kt, ft=ft: e.matmul(
                                bank[bk], lhsT=wt[:, kt, ft * 128:(ft + 1) * 128], rhs=h2T[:, kt, :],
                                start=(kt == 0), stop=(kt == 7)),
                               [o_wgu_l[ftg_of[ft]], o_h2T], [bobj[bk]], signal=(kt == 7))
                    op("act", lambda e, i=i, bg=bg: e.activation(out=sg[:, i, :], in_=bank[bg], func=AF.Silu),
                       [bobj[bg]], [o_sg[i]])
                    op("dve", lambda e, i=i, bu=bu, ft=ft: e.tensor_tensor(
                        out=actT[:, ft, :], in0=bank[bu], in1=sg[:, i, :], op=ALU.mult),
                       [bobj[bu], o_sg[i]], [o_act])

            def epi_load(c, s_):
                ti = c * 4 + s_
                bi = 2 + s_ % 2
                dma("sp", x1b[:, bi, :], out_d[ti * 128:(ti + 1) * 128, :], [o_out[ti]], [o_x1_l[bi]], o_x1_l[bi])

            def down_mm(c, s_):
                st = 2 + (s_ % 2)
                for hh in range(2):
                    bk = 2 * st + hh
                    for ft in range(NFT):
                        op("pe", lambda e, ft=ft, bk=bk, hh=hh: e.matmul(
                            bank[bk], lhsT=actT[:, ft, s_ * 128:(s_ + 1) * 128],
                            rhs=wd[:, ft, hh * 512:(hh + 1) * 512], start=(ft == 0), stop=(ft == NFT - 1)),
                           [o_act, o_wd], [bobj[bk]], signal=(ft == NFT - 1))

            def epilogue(c, s_):
                ti = c * 4 + s_
                bi = 2 + s_ % 2
                xb, o_x1 = x1b[:, bi, :], o_x1_l[bi]
                st = 2 + (s_ % 2)
                src = bank2[st]
                rs, o_rs = rstd_from(lambda: src, yb2[:], [bobj[2 * st], bobj[2 * st + 1]], o_y2)
                op("dve", lambda e: e.scalar_tensor_tensor(
                    out=yb2[:], in0=src, scalar=rs, in1=g4[:], op0=ALU.mult, op1=ALU.mult),
                   [bobj[2 * st], bobj[2 * st + 1], o_rs, o_g34], [o_y2])
                op("pool", lambda e: e.tensor_tensor(out=xb, in0=yb2[:], in1=xb, op=ALU.add),
                   [o_y2, o_x1], [o_x1])
                dma("sp", out_d[ti * 128:(ti + 1) * 128, :], xb, [o_x1], [o_out[ti]], o_x1)

            for s_ in range(4):
                prep_nonpe(0, s_)
                prep_pe(0, s_)
            for c in range(NCH):
                if c + 1 < NCH:
                    prep_nonpe(c + 1, 0)
                gateup(c)
                for s_ in range(4):
                    epi_load(c, s_)
                    if c + 1 < NCH and s_ + 1 < 4:
                        prep_nonpe(c + 1, s_ + 1)
                    down_mm(c, s_)
                    if c + 1 < NCH:
                        prep_pe(c + 1, s_)
                    epilogue(c, s_)
            C.barrier()
    return nc


def _tables(hf):
    half = 32
    freqs = (np.float32(10000.0) ** (-np.arange(half, dtype=np.float32) * np.float32(2.0 / 64))).astype(np.float32)
    pos = (hf * TOK - HALO + np.arange(HALO + TOK)).astype(np.float32)
    ang = (pos[:, None] * freqs[None, :]).astype(np.float32).astype(np.float64)
    p = np.arange(128)
    fi = (p % 64) % 32
    sign = np.where((p % 64) < 32, -1.0, 1.0)
    cs = np.cos(ang[:, fi]).T
    sn = np.sin(ang[:, fi]).T * sign[:, None]
    return np.ascontiguousarray(cs.astype(np.float32)), np.ascontiguousarray(sn.astype(np.float32))


def _consts(hf):
    m = np.arange(128)
    partner = np.where((m % 64) < 32, m + 32, m - 32)
    perm = np.zeros((128, 128), np.float32)
    perm[partner, m] = 1.0
    k = np.arange(128)[:, None]
    q = np.arange(128)[None, :]
    mp = (k >= q).astype(np.float32)
    mc = (k <= q).astype(np.float32)
    flag = 0.0 if hf == 0 else 1.0
    mh = mp * flag
    masks = np.stack([np.concatenate([mp, mc, mp, mc], 1),
                      np.concatenate([mh, mc, mp, mc], 1),
                      np.concatenate([mh, mc, mh, mc], 1)]).astype(np.float32)
    wins = np.array([2, 4, 8, 16], np.float32)
    invwin = np.zeros((128, 2), np.float32)
    invcnt = np.zeros((128, 2, 16), np.float32)
    for t in range(2):
        for hh in range(2):
            w = wins[2 * t + hh]
            invwin[64 * hh:64 * hh + 64, t] = 1.0 / w
            posj = hf * TOK + np.arange(16)
            invcnt[64 * hh:64 * hh + 64, t, :] = (1.0 / np.minimum(posj + 1, w)).astype(np.float32)[None, :]
    return perm, masks, invwin, invcnt


_NC_CACHE = {}


def kernel(x, ln_pre_mix, w_in, w_pool, pool_scale, w_out, ln_post_mix, ln_pre_ffn, w_gate, w_up, w_down,
           ln_post_ffn):
    x = np.asarray(x, np.float32)
    f = lambda a: np.ascontiguousarray(np.asarray(a, np.float32))
    w_in0, w_out0, wg0, wu0, wd0 = f(w_in[0]), f(w_out[0]), f(w_gate[0]), f(w_up[0]), f(w_down[0])
    wp = np.asarray(w_pool[0], np.float32)
    wpbd = np.zeros((2, 128, 128), np.float32)
    for t in range(2):
        for hh in range(2):
            wpbd[t, 64 * hh:64 * hh + 64, 64 * hh:64 * hh + 64] = wp[2 * t + hh]
    gb = np.stack([np.broadcast_to(np.asarray(g[0], np.float32)[None, :], (128, D))
                   for g in (ln_pre_mix, ln_post_mix, ln_pre_ffn, ln_post_ffn)]).astype(np.float32)
    gb = np.ascontiguousarray(gb)
    ps_ = np.asarray(pool_scale[0], np.float32)
    pscale = np.ascontiguousarray(np.stack([ps_[0:128], ps_[128:256]], 1))
    in_maps = []
    for c in range(NCORES):
        b, hf = c // 2, c % 2
        xe = np.zeros((HALO + TOK, D), np.float32)
        xe[HALO:] = x[b, hf * TOK:(hf + 1) * TOK]
        if hf == 1:
            xe[:HALO] = x[b, TOK - HALO:TOK]
        cs, sn = _tables(hf)
        perm, masks, invwin, invcnt = _consts(hf)
        in_maps.append({"x": xe, "cs": cs, "sn": sn, "w_in": w_in0, "w_out": w_out0, "w_gate": wg0, "w_up": wu0,
                        "w_down": wd0, "wpool_bd": wpbd, "gb": gb, "pscale": pscale, "perm": perm, "masks": masks,
                        "invwin": invwin, "invcnt": invcnt})
    if "nc" not in _NC_CACHE:
        _NC_CACHE["nc"] = build_program()
    nc = _NC_CACHE["nc"]
    res = run_bass_kernel_spmd(nc, in_maps, core_ids=list(range(NCORES)))
    out = np.zeros((4, SEQ, D), np.float32)
    for c in range(NCORES):
        b, hf = c // 2, c % 2
        out[b, hf * TOK:(hf + 1) * TOK] = res.results[c]["out"]
    return out
```

```python
import os
from contextlib import ExitStack
import numpy as np
import concourse.bass as bass
import concourse.mybir as mybir
from concourse.bass_utils import run_bass_kernel_spmd

F32 = mybir.dt.float32
BF16 = mybir.dt.bfloat16
AF = mybir.ActivationFunctionType
ALU = mybir.AluOpType

D = 1024
SEQ = 8192
TOK = 4096
HALO = 2048
BLK = 2048
DFF = 2816
NFT = DFF // 128
EPS = 1e-6
NCORES = 8


class Sem:
    def __init__(self, h, name):
        self.h = h
        self.name = name
        self.count = 0


class Obj:
    __slots__ = ("name", "w", "r")

    def __init__(self, name):
        self.name = name
        self.w = {}
        self.r = {}


class Ctx:
    def __init__(self, nc, es):
        self.nc = nc
        self.es = es
        self.eng = {}
        for n, h in (("pe", nc.tensor), ("act", nc.scalar), ("dve", nc.vector), ("pool", nc.gpsimd), ("sp", nc.sync)):
            s = Sem(es.enter_context(nc.semaphore("sem_" + n)), "sem_" + n)
            self.eng[n] = (h, s, {})
        self.dsems = []
        self._ds = {}
        self.nwaits = 0

    def dsem(self, name):
        s = Sem(self.es.enter_context(self.nc.semaphore("dsem_" + name)), "dsem_" + name)
        self.dsems.append(s)
        return s

    def _wait(self, en, sem, val):
        h, _, waited = self.eng[en]
        if waited.get(sem.name, 0) >= val:
            return
        assert val <= sem.count, f"wait on unsignaled {sem.name} {val}>{sem.count} from {en}"
        h.wait_ge(sem.h, val)
        waited[sem.name] = val
        self.nwaits += 1

    def _deps(self, en, reads, writes, is_dma, ds=None):
        _, own, _ = self.eng[en]
        deps = {}

        def need(s, v, raw, waw=False):
            if s is own and not is_dma:
                if en == "pe":
                    return
            if waw and s is ds:
                return
            if deps.get(s.name, (None, 0))[1] < v:
                deps[s.name] = (s, v)

        for o in reads:
            for s, v in o.w.values():
                need(s, v, True)
        for o in writes:
            for s, v in o.w.values():
                need(s, v, False, True)
            for s, v in o.r.values():
                need(s, v, False)
        for s, v in deps.values():
            self._wait(en, s, v)

    def op(self, en, fn, reads=(), writes=(), signal=True):
        h, own, _ = self.eng[en]
        writes = list(writes) + [o for o in reads if o.name.startswith("bank") and o not in writes]
        self._deps(en, reads, writes, False)
        inst = fn(h)
        if signal:
            inst.then_inc(own.h, 1)
            own.count += 1
            val = own.count
        else:
            val = own.count + 1
        for o in reads:
            o.r[own.name] = (own, val)
        for o in writes:
            o.w[own.name] = (own, val)
        return inst

    def ds_of(self, obj):
        if obj.name not in self._ds:
            self._ds[obj.name] = self.dsem(obj.name)
        return self._ds[obj.name]

    def dma(self, en, out, in_, reads, writes, key):
        h, own, _ = self.eng[en]
        ds = key if isinstance(key, Sem) else self.ds_of(key)
        self._deps(en, reads, writes, True, ds)
        inst = h.dma_start(out=out, in_=in_)
        inst.then_inc(ds.h, 16)
        ds.count += 16
        for o in reads:
            o.r[ds.name] = (ds, ds.count)
        for o in writes:
            o.w[ds.name] = (ds, ds.count)
        return inst

    def barrier(self):
        sems = [e[1] for e in self.eng.values()] + self.dsems
        for en in self.eng:
            for s in sems:
                if s is self.eng[en][1]:
                    continue
                if s.count > 0:
                    self._wait(en, s, s.count)


def build_program(dbg=None, dbg_blk=0):
    nc = bass.Bass("TRN2", target_bir_lowering=False)
    dt_in = lambda n, s: nc.dram_tensor(n, list(s), F32, kind="ExternalInput").ap()
    x_d = dt_in("x", (HALO + TOK, D))
    cs_d = dt_in("cs", (128, HALO + TOK))
    sn_d = dt_in("sn", (128, HALO + TOK))
    win_d = dt_in("w_in", (D, 2560))
    wout_d = dt_in("w_out", (D, D))
    wg_d = dt_in("w_gate", (D, DFF))
    wu_d = dt_in("w_up", (D, DFF))
    wd_d = dt_in("w_down", (DFF, D))
    wpbd_d = dt_in("wpool_bd", (2, 128, 128))
    gb_d = dt_in("gb", (4, 128, D))
    psc_d = dt_in("pscale", (128, 2))
    perm_d = dt_in("perm", (128, 128))
    mask_d = dt_in("masks", (3, 128, 512))
    invw_d = dt_in("invwin", (128, 2))
    invc_d = dt_in("invcnt", (128, 2, 16))
    out_d = nc.dram_tensor("out", [TOK, D], F32, kind="ExternalOutput").ap()
    dbg_d = {}
    if dbg:
        for n, (s, dty) in dbg.items():
            dbg_d[n] = nc.dram_tensor("dbg_" + n, list(s), dty, kind="ExternalOutput").ap()

    with ExitStack() as es:
        C = Ctx(nc, es)
        op, dma = C.op, C.dma
        sb = lambda n, s, d: es.enter_context(nc.sbuf_tensor(n, list(s), d))
        ps = es.enter_context(nc.psum_tensor("ps", [128, 4096], F32))
        bank = [ps[:, 512 * i:512 * (i + 1)] for i in range(8)]
        bankbf = [ps[:, 512 * i:512 * (i + 1)].bitcast(BF16) for i in range(8)]
        bobj = [Obj(f"bank{i}") for i in range(8)]
        bank2 = [ps[:, 1024 * i:1024 * (i + 1)] for i in range(4)]

        ds_const = C.dsem("const")
        o_const = Obj("const")
        ident = sb("ident", (128, 128), BF16)
        mhalf = sb("mhalf", (128, 1), F32)
        ssb = sb("ssb", (128, 160), F32)
        tsb = sb("tsb", (128, 160), F32)
        rsb = sb("rsb", (128, 160), F32)
        o_ss = Obj("ss")
        p1 = ExitStack()
        sbc = lambda n, s, d: p1.enter_context(nc.sbuf_tensor(n, list(s), d))
        ident_f = sbc("ident_f", (128, 128), F32)
        perm = sbc("perm_sb", (128, 128), BF16)
        masks = sbc("masks_sb", (128, 3, 512), BF16)
        wpbd = sbc("wpbd_sb", (128, 2, 128), BF16)
        psc = sbc("psc_sb", (128, 2), F32)
        invw = sbc("invw_sb", (128, 2), F32)
        invc = sbc("invc_sb", (128, 2, 16), F32)
        dma("pool", perm[:], perm_d, [], [o_const], ds_const)
        dma("pool", masks[:], mask_d.rearrange("v p c -> p v c"), [], [o_const], ds_const)
        dma("pool", wpbd[:], wpbd_d.rearrange("t p c -> p t c"), [], [o_const], ds_const)
        ds_const2 = C.dsem("const2")
        dma("sp", psc[:], psc_d, [], [o_const], ds_const2)
        dma("sp", invw[:], invw_d, [], [o_const], ds_const2)
        dma("sp", invc[:], invc_d, [], [o_const], ds_const2)
        o_id = Obj("ident")
        op("pool", lambda e: e.memset(ident_f[:], 1.0), [], [o_id])
        op("pool", lambda e: e.affine_select(out=ident_f[:], in_=ident_f[:], pattern=[[-1, 128]],
                                             compare_op=ALU.is_equal, fill=0.0, base=0, channel_multiplier=1),
           [o_id], [o_id])
        op("dve", lambda e: e.tensor_copy(out=ident[:], in_=ident_f[:]), [o_id], [o_id])
        op("dve", lambda e: e.memset(mhalf[:], -0.5), [], [o_const])
        op("dve", lambda e: e.memset(ssb[:], 0.0), [], [o_ss])

        sscol = [0]

        def rstd_from(psrc_ap_fn, junk_ap, src_objs, junk_obj, n_feat=D):
            c = sscol[0]
            sscol[0] += 1
            assert c < 160
            oc = Obj(f"rs{sscol[0]}")
            op("act", lambda e: e.activation(out=junk_ap, in_=psrc_ap_fn(), func=AF.Square,
                                             accum_out=ssb[:, c:c + 1]),
               list(src_objs) + [o_ss], [junk_obj, oc])
            op("dve", lambda e: e.tensor_scalar(out=tsb[:, c:c + 1], in0=ssb[:, c:c + 1], scalar1=1.0 / n_feat,
                                                scalar2=EPS, op0=ALU.mult, op1=ALU.add), [oc], [oc])
            op("pool", lambda e: e.tensor_tensor(out=rsb[:, c:c + 1], in0=tsb[:, c:c + 1], in1=mhalf[:],
                                                 op=ALU.pow), [oc, o_const], [oc])
            return rsb[:, c:c + 1], oc

        def rstd_multi(items, n_feat=D):
            n = len(items)
            c0 = sscol[0]
            sscol[0] += n
            assert sscol[0] <= 160
            oc = Obj(f"rsm{c0}")
            for j, (src, junk_ap, src_objs, junk_obj) in enumerate(items):
                op("act", lambda e, src=src, junk_ap=junk_ap, j=j: e.activation(
                    out=junk_ap, in_=src, func=AF.Square, accum_out=ssb[:, c0 + j:c0 + j + 1]),
                   list(src_objs) + [o_ss], [junk_obj, oc])
            op("dve", lambda e: e.tensor_scalar(out=tsb[:, c0:c0 + n], in0=ssb[:, c0:c0 + n], scalar1=1.0 / n_feat,
                                                scalar2=EPS, op0=ALU.mult, op1=ALU.add), [oc], [oc])
            op("pool", lambda e: e.tensor_tensor(out=rsb[:, c0:c0 + n], in0=tsb[:, c0:c0 + n],
                                                 in1=mhalf[:].to_broadcast([128, n]), op=ALU.pow),
               [oc, o_const], [oc])
            return [rsb[:, c0 + j:c0 + j + 1] for j in range(n)], oc

        with p1:
            sb1 = lambda n, s, d: p1.enter_context(nc.sbuf_tensor(n, list(s), d))
            xT = sb1("xT", (128, 2, 8, BLK), BF16)
            OT = sb1("OT", (128, 8, BLK), BF16)
            cst = sb1("cst", (128, 2 * BLK), F32)
            snt = sb1("snt", (128, 2 * BLK), F32)
            kT = sb1("kT", (128, 2, BLK), BF16)
            vT = sb1("vT", (128, 2, BLK), BF16)
            qT = sb1("qT", (128, BLK), BF16)
            wg_in = sb1("wg_in", (128, 8, 384), BF16)
            scr = sb1("scr", (128, 10240), F32)
            rbf = sb1("rbf", (128, 2, 512), BF16)
            PT = sb1("PT", (128, 2, 2, 512), BF16)
            acc = scr[:, 0:4096].rearrange("p (h t) -> p h t", h=2)
            Vaug = scr[:, 4096:8192].bitcast(BF16).rearrange("p (s c) -> p s c", s=32)
            Rn = scr[:, 9216:10240]
            t1b = [scr[:, 8192:8704], scr[:, 9216:9728]]
            t2b = [scr[:, 8704:9216], scr[:, 9728:10240]]
            NXS = 5
            xs = [scr[:, 1024 * i:1024 * (i + 1)] for i in range(NXS)]
            NXN = 2
            xn = [scr[:, 5120 + 512 * i:5120 + 512 * (i + 1)].bitcast(BF16) for i in range(NXN)]
            gt = qT[:].bitcast(F32)
            gt2 = scr[:, 6144:7168]
            xs2 = [scr[:, 7168 + 1024 * i:7168 + 1024 * (i + 1)] for i in range(2)]
            ys = [scr[:, 9216:10240]] * 2
            ub = scr[:, 6144:7200].rearrange("p (t c) -> p t c", t=2)
            lvb = [scr[:, 7200 + 1056 * i:7200 + 1056 * (i + 1)].rearrange("p (t c) -> p t c", t=2) for i in range(2)]
            dTb = scr[:, 9312:9824].bitcast(BF16).rearrange("p (t c) -> p t c", t=2)
            ptmp = scr[:, 9824:9888].rearrange("p (t c) -> p t c", t=2)
            wpl_in = PT[:].rearrange("p a b c -> p (a b c)").rearrange("p (k c) -> p k c", k=8)

            o_scr = Obj("scr")
            o_xT = [[Obj(f"xT{s}_{c}") for c in range(4)] for s in range(2)]
            o_OT = [Obj(f"OT{k}") for k in range(8)]
            o_tab = Obj("tab")
            o_k, o_v, o_q = Obj("kT"), Obj("vT"), Obj("qT")
            o_wg, o_wpl = Obj("wg_in"), Obj("wpl_in")
            o_acc, o_V, o_R = Obj("acc"), Obj("Vaug"), Obj("Rn")
            o_t1 = [Obj("t1k"), Obj("t1q")]
            o_t2 = [Obj("t2k"), Obj("t2q")]
            o_rbf = [Obj("rbf0"), Obj("rbf1")]
            o_PT = [[Obj(f"PT{s}{h}") for h in range(2)] for s in range(2)]
            ds_tab = C.dsem("tab")
            o_out = [Obj(f"out{t}") for t in range(TOK // 128)]

            def stage_fence(reads_scr=False):
                pass

            o_g = Obj("gt")
            o_xs_l = [Obj(f"xs{i}") for i in range(NXS)]
            o_xn_l = [Obj(f"xn{i}") for i in range(NXN)]

            xnext = [0, 0]

            def stage_x(b, prefetched=False, only_prefetch=False, after_group=None):
                jobs = [(0, 0), (BLK, 1)] if b == 0 else [(2 * BLK, 0)]
                tiles = [(e0 + tt * 128, slot, tt) for e0, slot in jobs for tt in range(BLK // 128)]
                GS = 2

                def ensure_loaded(upto):
                    while xnext[b] <= min(upto, len(tiles) - 1):
                        r0 = tiles[xnext[b]][0]
                        i = xnext[b] % NXS
                        dma("sp", xs[i], x_d[r0:r0 + 128, :], [], [o_xs_l[i]], o_xs_l[i])
                        xnext[b] += 1

                if not prefetched:
                    dma("sp", gt, gb_d[0], [], [o_g, o_q], o_g)
                ensure_loaded(3)
                if only_prefetch:
                    return
                def rstd_group(g0):
                    n = len(tiles[g0:g0 + GS])
                    items = [(xs[(g0 + j) % NXS], bank2[1], [o_xs_l[(g0 + j) % NXS], bobj[2]], bobj[3])
                             for j in range(n)]
                    return rstd_multi(items)

                rs_cache = {0: rstd_group(0)}
                for g0 in range(0, len(tiles), GS):
                    ensure_loaded(g0 + 2 * GS - 1 + 1)
                    grp = tiles[g0:g0 + GS]
                    rss, o_rs = rs_cache.pop(g0)
                    for j, (r0, slot, tt) in enumerate(grp):
                        it = g0 + j
                        xb, o_xs = xs[it % NXS], o_xs_l[it % NXS]
                        nb, o_xn = xn[it % NXN], o_xn_l[it % NXN]
                        op("dve", lambda e, xb=xb, nb=nb, rs=rss[j]: e.scalar_tensor_tensor(
                            out=nb, in0=xb, scalar=rs, in1=gt, op0=ALU.mult, op1=ALU.mult),
                           [o_xs, o_rs, o_g], [o_xn])
                    if g0 + GS < len(tiles):
                        rs_cache[g0 + GS] = rstd_group(g0 + GS)
                    for j, (r0, slot, tt) in enumerate(grp):
                        it = g0 + j
                        nb, o_xn = xn[it % NXN], o_xn_l[it % NXN]
                        bk = 4 + (it % 4)
                        for kt in range(8):
                            op("pe", lambda e, nb=nb, kt=kt, bk=bk: e.transpose(
                                bankbf[bk][:, kt * 128:(kt + 1) * 128], nb[:, kt * 128:(kt + 1) * 128], ident[:]),
                               [o_xn, o_id], [bobj[bk]], signal=(kt == 7))
                        ch = tt // 4
                        if it % 2 == 0:
                            op("act", lambda e, bk=bk, slot=slot, tt=tt: e.activation(
                                out=xT[:, slot, :, tt * 128:(tt + 1) * 128],
                                in_=bankbf[bk].rearrange("p (k t) -> p k t", k=8), func=AF.Copy),
                               [bobj[bk]], [o_xT[slot][ch]])
                        else:
                            op("dve", lambda e, bk=bk, slot=slot, tt=tt: e.tensor_copy(
                                out=xT[:, slot, :, tt * 128:(tt + 1) * 128],
                                in_=bankbf[bk].rearrange("p (k t) -> p k t", k=8)),
                               [bobj[bk]], [o_xT[slot][ch]])
                    if after_group is not None:
                        after_group(g0 + GS, tiles)

            def load_wg(g):
                for j, c0 in enumerate((256 + 128 * g, 1024 + 128 * g, 1792 + 128 * g)):
                    dma("pool", wg_in[:, :, 128 * j:128 * (j + 1)],
                        win_d.rearrange("(k p) c -> p k c", p=128)[:, :, c0:c0 + 128], [], [o_wg], o_wg)

            def stage_p(b, g):
                pred_slot = b % 2
                cur_slot = 1 - pred_slot
                chunks = [(half, ch) for half in range(2) for ch in range(4)]

                def pmm(wcol0, slot, ch, bk):
                    for kt in range(8):
                        op("pe", lambda e, kt=kt: e.matmul(
                            bank[bk], lhsT=wg_in[:, kt, wcol0:wcol0 + 128],
                            rhs=xT[:, slot, kt, ch * 512:(ch + 1) * 512], start=(kt == 0), stop=(kt == 7)),
                           [o_wg, o_xT[slot][ch]], [bobj[bk]], signal=(kt == 7))

                def proj_mm(ci):
                    half, ch = chunks[ci]
                    slot = pred_slot if half == 0 else cur_slot
                    base = 3 * (ci % 2)
                    pmm(128, slot, ch, base)
                    pmm(256, slot, ch, base + 1)
                    if half == 1:
                        pmm(0, slot, ch, base + 2)

                def copies(ci):
                    half, ch = chunks[ci]
                    base = 3 * (ci % 2)
                    op("act", lambda e: e.activation(out=rbf[:, 0, :], in_=bank[base], func=AF.Copy),
                       [bobj[base]], [o_rbf[0]])
                    if half == 1:
                        op("act", lambda e: e.activation(out=rbf[:, 1, :], in_=bank[base + 2], func=AF.Copy),
                           [bobj[base + 2]], [o_rbf[1]])
                    op("act", lambda e: e.activation(out=vT[:, half, ch * 512:(ch + 1) * 512], in_=bank[base + 1],
                                                     func=AF.Copy), [bobj[base + 1]], [o_v])

                def rope_rest(ci):
                    half, ch = chunks[ci]
                    base = 3 * (ci % 2)
                    tab0 = half * BLK + ch * 512
                    jobs = [(0, base, 6, kT[:, half, ch * 512:(ch + 1) * 512], o_k)]
                    if half == 1:
                        jobs.append((1, base + 2, 7, qT[:, ch * 512:(ch + 1) * 512], o_q))
                    for (i, bk, wb, dst, dobj) in jobs:
                        op("pe", lambda e, i=i, wb=wb: e.matmul(bank[wb], lhsT=perm[:], rhs=rbf[:, i, :],
                                                                 start=True, stop=True),
                           [o_rbf[i], o_const], [bobj[wb]])
                    for (i, bk, wb, dst, dobj) in jobs:
                        op("dve", lambda e, i=i, bk=bk: e.tensor_tensor(
                            out=t1b[i], in0=bank[bk], in1=cst[:, tab0:tab0 + 512], op=ALU.mult),
                           [bobj[bk], o_tab], [o_t1[i]])
                        op("dve", lambda e, i=i, wb=wb: e.tensor_tensor(
                            out=t2b[i], in0=bank[wb], in1=snt[:, tab0:tab0 + 512], op=ALU.mult),
                           [bobj[wb], o_tab], [o_t2[i]])
                        op("pool", lambda e, i=i, dst=dst: e.tensor_tensor(out=dst, in0=t1b[i], in1=t2b[i],
                                                                            op=ALU.add),
                           [o_t1[i], o_t2[i]], [dobj])

                proj_mm(0)
                copies(0)
                for ci in range(len(chunks)):
                    if ci + 1 < len(chunks):
                        proj_mm(ci + 1)
                    rope_rest(ci)
                    if ci + 1 < len(chunks):
                        copies(ci + 1)

            actr = [0]

            def stage_a(b, g):
                first_cfg = True
                for d in (1, 4, 16):
                    nbh = 16 // d
                    span = 128 * d

                    def toks(r, n):
                        if n < 0:
                            return 0, (nbh - 1) * span + r
                        return 1, n * span + r

                    def sl(t0):
                        return slice(t0, t0 + 127 * d + 1, d)

                    klist = [(r, n) for r in range(d) for n in range(-1, nbh)]
                    vslot = {kn: i for i, kn in enumerate(klist)}
                    for j0 in range(0, len(klist), 8):
                        batch = klist[j0:j0 + 8]
                        bk = 6 + (actr[0] % 2)
                        actr[0] += 1
                        for j, (r, n) in enumerate(batch):
                            half, t0 = toks(r, n)
                            op("pe", lambda e, j=j, half=half, t0=t0, bk=bk: e.transpose(
                                bankbf[bk][:, j * 128:(j + 1) * 128], vT[:, half, sl(t0)], ident[:]),
                               [o_v, o_id], [bobj[bk]], signal=(j == len(batch) - 1))
                        nbt = len(batch)
                        dst = Vaug[:, j0:j0 + nbt, :].rearrange("p s (a c) -> p s a c", a=4)[:, :, 0:4:3, :]
                        src = bankbf[bk][:, 0:nbt * 128].rearrange("p (s h c) -> p s h c", s=nbt, h=2)
                        op("act", lambda e, dst=dst, src=src: e.activation(out=dst, in_=src, func=AF.Copy),
                           [bobj[bk]], [o_V])

                    qbs = [(r, n) for r in range(d) for n in range(nbh)]
                    pairs = [qbs[p0:p0 + 2] for p0 in range(0, len(qbs), 2)]
                    base_ctr = actr[0]
                    actr[0] += len(pairs)

                    def emit_qk(pi):
                        pair = pairs[pi]
                        s_ = (base_ctr + pi) % 2
                        halo = [b == 0 and n == 0 for (_, n) in pair]
                        mv = 0 if not any(halo) else (2 if all(halo) else 1)
                        assert not (halo[1] and not halo[0])
                        for h in range(2):
                            bk = 2 * s_ + h
                            for i, (r, n) in enumerate(pair):
                                _, tq = toks(r, n)
                                for kk, nk in enumerate((n - 1, n)):
                                    khalf, tk = toks(r, nk)
                                    col = (2 * i + kk) * 128
                                    last = (i == 1 and kk == 1)
                                    op("pe", lambda e, h=h, bk=bk, col=col, khalf=khalf, tk=tk, tq=tq: e.matmul(
                                        bank[bk][:, col:col + 128],
                                        lhsT=kT[64 * h:64 * h + 64, khalf, sl(tk)],
                                        rhs=qT[64 * h:64 * h + 64, sl(tq)], start=True, stop=True),
                                       [o_k, o_q], [bobj[bk]], signal=last)
                        for h in range(2):
                            bk = 2 * s_ + h
                            op("act", lambda e, h=h, bk=bk: e.activation(
                                out=PT[:, s_, h, :], in_=bank[bk], func=AF.Exp, scale=0.125),
                               [bobj[bk]], [o_PT[s_][h]])
                            op("dve", lambda e, h=h: e.tensor_tensor(
                                out=PT[:, s_, h, :], in0=PT[:, s_, h, :], in1=masks[:, mv, :], op=ALU.mult),
                               [o_PT[s_][h], o_const], [o_PT[s_][h]])

                    def emit_pv(pi, first):
                        pair = pairs[pi]
                        s_ = (base_ctr + pi) % 2
                        ob = 4 + s_
                        for i, (r, n) in enumerate(pair):
                            for h in range(2):
                                col = (2 * i + h) * 128
                                for kk, nk in enumerate((n - 1, n)):
                                    vs = vslot[(r, nk)]
                                    last = (i == 1 and h == 1 and kk == 1)
                                    op("pe", lambda e, vs=vs, h=h, col=col, kk=kk, i=i: e.matmul(
                                        bank[ob][:, col:col + 128],
                                        lhsT=Vaug[:, vs, 128 * h:128 * (h + 1)],
                                        rhs=PT[:, s_, h, (2 * i + kk) * 128:(2 * i + kk + 1) * 128],
                                        start=(kk == 0), stop=(kk == 1)),
                                       [o_V, o_PT[s_][h]], [bobj[ob]], signal=last)
                        for i, (r, n) in enumerate(pair):
                            _, tq = toks(r, n)
                            dst = acc[:, :, sl(tq)]
                            src = bank[ob][:, 256 * i:256 * (i + 1)].rearrange("p (h l) -> p h l", h=2)
                            if first:
                                op("act", lambda e, dst=dst, src=src: e.activation(out=dst, in_=src, func=AF.Copy),
                                   [bobj[ob]], [o_acc])
                            else:
                                op("dve", lambda e, dst=dst, src=src: e.tensor_tensor(
                                    out=dst, in0=src, in1=dst, op=ALU.add),
                                   [bobj[ob], o_acc], [o_acc])

                    emit_qk(0)
                    for pi in range(len(pairs)):
                        if pi + 1 < len(pairs):
                            emit_qk(pi + 1)
                        emit_pv(pi, first_cfg)
                    first_cfg = False
                rn_objs = [o_R, o_t1[1], o_t2[1]]
                kt = 2 + g
                for hv in range(2):
                    tsl = slice(hv * 1024, (hv + 1) * 1024)
                    op("act", lambda e: e.activation(out=Rn[0:64, :], in_=acc[64:128, 0, tsl], func=AF.Ln),
                       [o_acc], rn_objs)
                    op("act", lambda e: e.activation(out=Rn[64:128, :], in_=acc[0:64, 1, tsl], func=AF.Ln),
                       [o_acc], rn_objs)
                    op("act", lambda e: e.activation(out=Rn, in_=Rn, func=AF.Exp, scale=-1.0), rn_objs, rn_objs)
                    op("pool", lambda e: e.tensor_tensor(out=OT[0:64, kt, tsl], in0=acc[0:64, 0, tsl], in1=Rn[0:64, :],
                                                         op=ALU.mult), [o_acc] + rn_objs, [o_OT[kt]])
                    op("dve", lambda e: e.tensor_tensor(out=OT[64:128, kt, tsl], in0=acc[64:128, 1, tsl],
                                                        in1=Rn[64:128, :], op=ALU.mult), [o_acc] + rn_objs, [o_OT[kt]])

            o_ub, o_dT = Obj("ub"), Obj("dT")
            o_lvb = [Obj("lv0"), Obj("lv1")]

            def pool_begin(b):
                pred_slot = b % 2
                dma("pool", wpl_in, win_d.rearrange("(k p) c -> p k c", p=128)[:, :, 0:256],
                    [], [o_wpl, o_PT[0][0], o_PT[0][1], o_PT[1][0], o_PT[1][1]], o_wpl)
                for t in range(2):
                    bk = t
                    for kt in range(8):
                        op("pe", lambda e, kt=kt, bk=bk, t=t: e.matmul(
                            bank[bk][:, 0:16], lhsT=wpl_in[:, kt, 128 * t:128 * (t + 1)],
                            rhs=xT[:, pred_slot, kt, BLK - 16:BLK], start=(kt == 0), stop=(kt == 7)),
                           [o_wpl, o_xT[pred_slot][3]], [bobj[bk]], signal=(kt == 7))
                    op("act", lambda e, bk=bk, t=t: e.activation(out=ub[:, t, 0:16], in_=bank[bk][:, 0:16],
                                                                 func=AF.Copy), [bobj[bk]], [o_ub])

            def pool_chunk(b, ch):
                cur_slot = 1 - (b % 2)
                for t in range(2):
                    bk = t
                    for kt in range(8):
                        op("pe", lambda e, kt=kt, bk=bk, t=t: e.matmul(
                            bank[bk], lhsT=wpl_in[:, kt, 128 * t:128 * (t + 1)],
                            rhs=xT[:, cur_slot, kt, ch * 512:(ch + 1) * 512], start=(kt == 0), stop=(kt == 7)),
                           [o_wpl, o_xT[cur_slot][ch]], [bobj[bk]], signal=(kt == 7))
                    op("act", lambda e, bk=bk, t=t: e.activation(out=ub[:, t, 16:528], in_=bank[bk],
                                                                 func=AF.Copy), [bobj[bk]], [o_ub])
                prev, o_prev = ub, o_ub
                for k in range(4):
                    sh = 1 << k
                    lo = (1 << (k + 1)) - 1
                    cur, o_cur = lvb[k % 2], o_lvb[k % 2]
                    op("dve", lambda e, prev=prev, cur=cur, sh=sh, lo=lo: e.tensor_tensor(
                        out=cur[:, :, lo:528], in0=prev[:, :, lo:528], in1=prev[:, :, lo - sh:528 - sh],
                        op=ALU.add), [o_prev], [o_cur])
                    t, hh = k // 2, k % 2
                    p0 = 64 * hh
                    op("dve", lambda e, t=t, cur=cur, p0=p0: e.scalar_tensor_tensor(
                        out=dTb[p0:p0 + 64, t, :], in0=cur[p0:p0 + 64, t, 16:528],
                        scalar=invw[p0:p0 + 64, t:t + 1], in1=ub[p0:p0 + 64, t, 16:528],
                        op0=ALU.mult, op1=ALU.subtract), [o_cur, o_ub, o_const], [o_dT])
                    if b == 0 and ch == 0:
                        op("dve", lambda e, t=t, cur=cur, p0=p0: e.tensor_tensor(
                            out=ptmp[p0:p0 + 64, t, 0:16], in0=cur[p0:p0 + 64, t, 16:32],
                            in1=invc[p0:p0 + 64, t, :], op=ALU.mult), [o_cur, o_const], [o_dT])
                        op("dve", lambda e, t=t, p0=p0: e.tensor_tensor(
                            out=dTb[p0:p0 + 64, t, 0:16], in0=ptmp[p0:p0 + 64, t, 0:16],
                            in1=ub[p0:p0 + 64, t, 16:32], op=ALU.subtract), [o_dT, o_ub], [o_dT])
                    prev, o_prev = cur, o_cur
                for t in range(2):
                    bk = t
                    op("pe", lambda e, t=t, bk=bk: e.matmul(bank[bk], lhsT=wpbd[:, t, :], rhs=dTb[:, t, :],
                                                           start=True, stop=True),
                       [o_dT, o_const], [bobj[bk]])
                    op("act", lambda e, t=t, bk=bk: e.activation(
                        out=OT[:, t, ch * 512:(ch + 1) * 512], in_=bank[bk], func=AF.Copy,
                        scale=psc[:, t:t + 1]), [bobj[bk], o_const], [o_OT[t]])
                op("dve", lambda e: e.tensor_copy(out=ptmp[:, :, 16:32], in_=ub[:, :, 512:528]), [o_ub], [o_dT])
                op("act", lambda e: e.activation(out=ub[:, :, 0:16], in_=ptmp[:, :, 16:32], func=AF.Copy),
                   [o_dT], [o_ub])

            o_wo, o_g2, o_yb = Obj("wout"), Obj("gt2"), Obj("ys")

            def wout_view(b):
                return xT[:, b % 2, 0:4, :].rearrange("p k (a c) -> p (k a) c", a=2)

            def gt2_view(b):
                return xT[:, b % 2, 4, :].bitcast(F32)

            def prefetch_o(b):
                dma("pool", wout_view(b), wout_d.rearrange("(k p) c -> p k c", p=128),
                    [], [o_wo] + o_xT[b % 2], o_wo)
                dma("sp", gt2_view(b), gb_d[1], [], [o_g2] + o_xT[b % 2], o_g2)

            def stage_o(b):
                wout = wout_view(b)
                gt2 = gt2_view(b)
                o_xs2_l = [Obj("xs2_0"), Obj("xs2_1")]
                junk = scr[:, 5632:6144].bitcast(BF16)
                o_j = Obj("junk_o")
                for tt in range(BLK // 128):
                    st = tt % 4
                    xb = xs2[tt % 2]
                    yb = ys[0]
                    o_xb = o_xs2_l[tt % 2]
                    tok0 = b * BLK + tt * 128
                    dma("sp", xb, x_d[HALO + tok0:HALO + tok0 + 128, :], [], [o_xb], o_xb)
                    for hh in range(2):
                        bk = 2 * st + hh
                        for kt in range(8):
                            op("pe", lambda e, kt=kt, bk=bk, hh=hh, tt=tt: e.matmul(
                                bank[bk], lhsT=OT[:, kt, tt * 128:(tt + 1) * 128],
                                rhs=wout[:, kt, hh * 512:(hh + 1) * 512], start=(kt == 0), stop=(kt == 7)),
                               [o_OT[kt], o_wo], [bobj[bk]], signal=(kt == 7))
                    src = bank2[st]
                    rs, o_rs = rstd_from(lambda src=src: src, junk, [bobj[2 * st], bobj[2 * st + 1]], o_j)
                    op("dve", lambda e, yb=yb, src=src, rs=rs: e.scalar_tensor_tensor(
                        out=yb, in0=src, scalar=rs, in1=gt2, op0=ALU.mult, op1=ALU.mult),
                       [bobj[2 * st], bobj[2 * st + 1], o_rs, o_g2], [o_yb])
                    op("dve", lambda e, yb=yb, xb=xb: e.tensor_tensor(out=xb, in0=yb, in1=xb, op=ALU.add),
                       [o_yb, o_xb], [o_xb])
                    ti = tok0 // 128
                    dma("sp", out_d[tok0:tok0 + 128, :], xb, [o_xb], [o_out[ti]], o_xb)

            def fence():
                pass

            for b in range(2):
                dma("sp", cst[:], cs_d[:, b * BLK:b * BLK + 2 * BLK], [], [o_tab], ds_tab)
                dma("sp", snt[:], sn_d[:, b * BLK:b * BLK + 2 * BLK], [], [o_tab], ds_tab)
                load_wg(0)

                def hook(ndone, tiles, b=b):
                    npred = len(tiles) - BLK // 128
                    if ndone == npred or (npred == 0 and ndone == 2):
                        pass
                    if ndone == max(npred, 2) and not hook.started:
                        pool_begin(b)
                        hook.started = True
                    k = ndone - npred
                    if hook.started and k > 0 and k % 4 == 0:
                        while hook.next_ch < k // 4:
                            pool_chunk(b, hook.next_ch)
                            hook.next_ch += 1
                hook.started = False
                hook.next_ch = 0
                stage_x(b, prefetched=(b == 1), after_group=hook)
                C.barrier()
                op("pool", lambda e: e.memset(Vaug[:, :, 64:192], 1.0), [], [o_V])
                for g in range(6):
                    stage_p(b, g)
                    if g + 1 < 6:
                        load_wg(g + 1)
                    else:
                        prefetch_o(b)
                    stage_a(b, g)
                C.barrier()
                if dbg and b == dbg_blk:
                    ds_dbg = C.dsem("dbg")
                    od = Obj("dbg")
                    srcs = {"OT": OT[:].rearrange("p k t -> p (k t)"), "rsb": rsb[:], "kT": kT[:].rearrange("p h t -> p (h t)"),
                            "vT": vT[:].rearrange("p h t -> p (h t)"), "qT": qT[:], "acc": scr[:, 0:4096],
                            "xT": xT[:].rearrange("p s k t -> p (s k t)")}
                    for n in dbg_d:
                        dma("sp", dbg_d[n], srcs[n], [], [od], ds_dbg)
                    C.barrier()
                if b == 0:
                    stage_x(1, only_prefetch=True)
                stage_o(b)
                C.barrier()
            if dbg and "x1" in dbg:
                pass
        C.barrier()
        with ExitStack() as p2:
            sb2 = lambda n, s, d: p2.enter_context(nc.sbuf_tensor(n, list(s), d))
            wg = sb2("wg", (128, 8, DFF), BF16)
            wu = sb2("wu", (128, 8, DFF), BF16)
            wd = sb2("wd", (128, NFT, D), BF16)
            x1b = sb2("x1b", (128, 4, D), F32)
            yb2 = sb2("yb2", (128, D), F32)
            g3 = sb2("g3", (128, D), F32)
            g4 = sb2("g4", (128, D), F32)
            sg = sb2("sg", (128, 2, 512), F32)
            h2 = sb2("h2", (128, 2, D), BF16)
            h2T = sb2("h2T", (128, 8, 512), BF16)
            actT = sb2("actT", (128, NFT, 512), BF16)
            o_wgu, o_wd, o_g34 = Obj("wgu"), Obj("wd"), Obj("g34")
            FTG = [(0, 3), (3, 8), (8, 15), (15, NFT)]
            o_wgu_l = [Obj(f"wgu{i}") for i in range(len(FTG))]
            ftg_of = {}
            for gi, (f0, f1) in enumerate(FTG):
                for ft in range(f0, f1):
                    ftg_of[ft] = gi
                dma("pool", wg[:, :, f0 * 128:f1 * 128],
                    wg_d.rearrange("(k p) c -> p k c", p=128)[:, :, f0 * 128:f1 * 128], [], [o_wgu_l[gi]], o_wgu_l[gi])
                dma("pool", wu[:, :, f0 * 128:f1 * 128],
                    wu_d.rearrange("(k p) c -> p k c", p=128)[:, :, f0 * 128:f1 * 128], [], [o_wgu_l[gi]], o_wgu_l[gi])
            dma("sp", g3[:], gb_d[2], [], [o_g34], o_g34)
            dma("sp", g4[:], gb_d[3], [], [o_g34], o_g34)
            o_h2T, o_act = Obj("h2T"), Obj("actT")
            o_sg = [Obj("sg0"), Obj("sg1")]
            o_h2 = [Obj("h2_0"), Obj("h2_1")]
            o_y2 = Obj("yb2")
            o_x1_l = [Obj(f"x1b{i}") for i in range(4)]
            NCH = TOK // 512

            def prep_nonpe(c, s_):
                ti = c * 4 + s_
                bi = s_ % 2
                xb, o_x1 = x1b[:, bi, :], o_x1_l[bi]
                dma("sp", xb, out_d[ti * 128:(ti + 1) * 128, :], [o_out[ti]], [o_x1], o_x1)
                hb, o_hb = h2[:, s_ % 2, :], o_h2[s_ % 2]
                rs, o_rs = rstd_from(lambda: xb, hb, [o_x1], o_hb)
                op("dve", lambda e: e.scalar_tensor_tensor(
                    out=hb, in0=xb, scalar=rs, in1=g3[:], op0=ALU.mult, op1=ALU.mult),
                   [o_x1, o_rs, o_g34], [o_hb])

            def prep_pe(c, s_):
                hb, o_hb = h2[:, s_ % 2, :], o_h2[s_ % 2]
                bk = s_ % 2
                for kt in range(8):
                    op("pe", lambda e, kt=kt: e.transpose(
                        bankbf[bk][:, kt * 128:(kt + 1) * 128], hb[:, kt * 128:(kt + 1) * 128], ident[:]),
                       [o_hb, o_id], [bobj[bk]], signal=(kt == 7))
                op("act", lambda e: e.activation(
                    out=h2T[:, :, s_ * 128:(s_ + 1) * 128],
                    in_=bankbf[bk].rearrange("p (k t) -> p k t", k=8), func=AF.Copy),
                   [bobj[bk]], [o_h2T])

            def gateup(c):
                for ft in range(NFT):
                    i = ft % 2
                    bg, bu = i, 2 + i
                    for (wt, bk) in ((wg, bg), (wu, bu)):
                        for kt in range(8):
                            op("pe", lambda e, wt=wt, bk=bk, kt=kt, ft=ft: e.matmul(
                                bank[bk], lhsT=wt[:, kt, ft * 128:(ft + 1) * 128], rhs=h2T[:, kt, :],
                                start=(kt == 0), stop=(kt == 7)),
                               [o_wgu_l[ftg_of[ft]], o_h2T], [bobj[bk]], signal=(kt == 7))
                    op("act", lambda e, i=i, bg=bg: e.activation(out=sg[:, i, :], in_=bank[bg], func=AF.Silu),
                       [bobj[bg]], [o_sg[i]])
                    op("dve", lambda e, i=i, bu=bu, ft=ft: e.tensor_tensor(
                        out=actT[:, ft, :], in0=bank[bu], in1=sg[:, i, :], op=ALU.mult),
                       [bobj[bu], o_sg[i]], [o_act])

            def epi_load(c, s_):
                ti = c * 4 + s_
                bi = 2 + s_ % 2
                dma("sp", x1b[:, bi, :], out_d[ti * 128:(ti + 1) * 128, :], [o_out[ti]], [o_x1_l[bi]], o_x1_l[bi])

            def down_mm(c, s_):
                st = 2 + (s_ % 2)
                for hh in range(2):
                    bk = 2 * st + hh
                    for ft in range(NFT):
                        op("pe", lambda e, ft=ft, bk=bk, hh=hh: e.matmul(
                            bank[bk], lhsT=actT[:, ft, s_ * 128:(s_ + 1) * 128],
                            rhs=wd[:, ft, hh * 512:(hh + 1) * 512], start=(ft == 0), stop=(ft == NFT - 1)),
                           [o_act, o_wd], [bobj[bk]], signal=(ft == NFT - 1))

            def epilogue(c, s_):
                ti = c * 4 + s_
                bi = 2 + s_ % 2
                xb, o_x1 = x1b[:, bi, :], o_x1_l[bi]
                st = 2 + (s_ % 2)
                src = bank2[st]
                rs, o_rs = rstd_from(lambda: src, yb2[:], [bobj[2 * st], bobj[2 * st + 1]], o_y2)
                op("dve", lambda e: e.scalar_tensor_tensor(
                    out=yb2[:], in0=src, scalar=rs, in1=g4[:], op0=ALU.mult, op1=ALU.mult),
                   [bobj[2 * st], bobj[2 * st + 1], o_rs, o_g34], [o_y2])
                op("pool", lambda e: e.tensor_tensor(out=xb, in0=yb2[:], in1=xb, op=ALU.add),
                   [o_y2, o_x1], [o_x1])
                dma("sp", out_d[ti * 128:(ti + 1) * 128, :], xb, [o_x1], [o_out[ti]], o_x1)

            for s_ in range(4):
                prep_nonpe(0, s_)
                prep_pe(0, s_)
            for ft in range(NFT):
                dma("pool", wd[:, ft, :], wd_d[ft * 128:(ft + 1) * 128, :], [], [o_wd], o_wd)
            for c in range(NCH):
                if c + 1 < NCH:
                    prep_nonpe(c + 1, 0)
                gateup(c)
                for s_ in range(4):
                    epi_load(c, s_)
                    if c + 1 < NCH and s_ + 1 < 4:
                        prep_nonpe(c + 1, s_ + 1)
                    down_mm(c, s_)
                    if c + 1 < NCH:
                        prep_pe(c + 1, s_)
                    epilogue(c, s_)
            C.barrier()
    return nc


def _tables(hf):
    half = 32
    freqs = (np.float32(10000.0) ** (-np.arange(half, dtype=np.float32) * np.float32(2.0 / 64))).astype(np.float32)
    pos = (hf * TOK - HALO + np.arange(HALO + TOK)).astype(np.float32)
    ang = (pos[:, None] * freqs[None, :]).astype(np.float32).astype(np.float64)
    p = np.arange(128)
    fi = (p % 64) % 32
    sign = np.where((p % 64) < 32, -1.0, 1.0)
    cs = np.cos(ang[:, fi]).T
    sn = np.sin(ang[:, fi]).T * sign[:, None]
    return np.ascontiguousarray(cs.astype(np.float32)), np.ascontiguousarray(sn.astype(np.float32))


def _consts(hf):
    m = np.arange(128)
    partner = np.where((m % 64) < 32, m + 32, m - 32)
    perm = np.zeros((128, 128), np.float32)
    perm[partner, m] = 1.0
    k = np.arange(128)[:, None]
    q = np.arange(128)[None, :]
    mp = (k >= q).astype(np.float32)
    mc = (k <= q).astype(np.float32)
    flag = 0.0 if hf == 0 else 1.0
    mh = mp * flag
    masks = np.stack([np.concatenate([mp, mc, mp, mc], 1),
                      np.concatenate([mh, mc, mp, mc], 1),
                      np.concatenate([mh, mc, mh, mc], 1)]).astype(np.float32)
    wins = np.array([2, 4, 8, 16], np.float32)
    invwin = np.zeros((128, 2), np.float32)
    invcnt = np.zeros((128, 2, 16), np.float32)
    for t in range(2):
        for hh in range(2):
            w = wins[2 * t + hh]
            invwin[64 * hh:64 * hh + 64, t] = 1.0 / w
            posj = hf * TOK + np.arange(16)
            invcnt[64 * hh:64 * hh + 64, t, :] = (1.0 / np.minimum(posj + 1, w)).astype(np.float32)[None, :]
    return perm, masks, invwin, invcnt


_NC_CACHE = {}


def kernel(x, ln_pre_mix, w_in, w_pool, pool_scale, w_out, ln_post_mix, ln_pre_ffn, w_gate, w_up, w_down,
           ln_post_ffn):
    x = np.asarray(x, np.float32)
    f = lambda a: np.ascontiguousarray(np.asarray(a, np.float32))
    w_in0, w_out0, wg0, wu0, wd0 = f(w_in[0]), f(w_out[0]), f(w_gate[0]), f(w_up[0]), f(w_down[0])
    wp = np.asarray(w_pool[0], np.float32)
    wpbd = np.zeros((2, 128, 128), np.float32)
    for t in range(2):
        for hh in range(2):
            wpbd[t, 64 * hh:64 * hh + 64, 64 * hh:64 * hh + 64] = wp[2 * t + hh]
    gb = np.stack([np.broadcast_to(np.asarray(g[0], np.float32)[None, :], (128, D))
                   for g in (ln_pre_mix, ln_post_mix, ln_pre_ffn, ln_post_ffn)]).astype(np.float32)
    gb = np.ascontiguousarray(gb)
    ps_ = np.asarray(pool_scale[0], np.float32)
    pscale = np.ascontiguousarray(np.stack([ps_[0:128], ps_[128:256]], 1))
    in_maps = []
    for c in range(NCORES):
        b, hf = c // 2, c % 2
        xe = np.zeros((HALO + TOK, D), np.float32)
        xe[HALO:] = x[b, hf * TOK:(hf + 1) * TOK]
        if hf == 1:
            xe[:HALO] = x[b, TOK - HALO:TOK]
        cs, sn = _tables(hf)
        perm, masks, invwin, invcnt = _consts(hf)
        in_maps.append({"x": xe, "cs": cs, "sn": sn, "w_in": w_in0, "w_out": w_out0, "w_gate": wg0, "w_up": wu0,
                        "w_down": wd0, "wpool_bd": wpbd, "gb": gb, "pscale": pscale, "perm": perm, "masks": masks,
                        "invwin": invwin, "invcnt": invcnt})
    if "nc" not in _NC_CACHE:
        _NC_CACHE["nc"] = build_program()
    nc = _NC_CACHE["nc"]
    res = run_bass_kernel_spmd(nc, in_maps, core_ids=list(range(NCORES)))
    out = np.zeros((4, SEQ, D), np.float32)
    for c in range(NCORES):
        b, hf = c // 2, c % 2
        out[b, hf * TOK:(hf + 1) * TOK] = res.results[c]["out"]
    return out
```
